# Optimizing a Trainium2 kernel written in Bass

```python
import math
import jax, jax.numpy as jnp
from jax import lax
import numpy as np

D_MODEL = 2048
BATCH = 4
SEQ = 8192
DEPTH = 4
DEC_BATCH = 1
DEC_SEQ = 8192
PAST_LEN = 128

GRID_W = 64
N_MEM = 256
XA_HEADS = 4
XA_HEAD_DIM = D_MODEL // XA_HEADS
D_SSD = D_MODEL
SSD_HEAD_DIM = 64
SSD_HEADS = D_SSD // SSD_HEAD_DIM
SSD_GROUPS = 4
SSD_STATE = 128
SSD_CONV_W = 5
SSD_CHUNK = 128
SSD_CONV_CH = D_SSD + 2 * SSD_GROUPS * SSD_STATE
ATTN_HEAD_DIM = 128
ATTN_HEADS = D_MODEL // ATTN_HEAD_DIM
ATTN_KV_HEADS = 4
ATTN_Q_GROUP = ATTN_HEADS // ATTN_KV_HEADS
D_ATTN = ATTN_HEADS * ATTN_HEAD_DIM
D_KV = ATTN_KV_HEADS * ATTN_HEAD_DIM
ROPE_THETA = 10000.0
Q_BLOCK = 128
EVEN_IN_W = D_SSD + SSD_CONV_CH + 2 * SSD_HEADS + D_ATTN + 2 * D_KV
EVEN_OUT_W = D_SSD + D_ATTN
HY_SHORT_W = 3
HY_EMB = 33
HY_BANDS = (HY_EMB - 1) // 2
HY_FILTER_W = 64
HY_TARGET = 1e-2
HY_FAST_PCT = 0.3
HY_SLOW_PCT = 1.5
D_FF = 5632
FFN_CONV_W = 3
N_EVEN = (DEPTH + 1) // 2
N_ODD = DEPTH // 2
EPS = 1e-6

kernel_name = "hybrid_ssd_gqa_hyena_encoder"

F32 = jnp.float32


def rms_norm(x, g):
    xf = x.astype(F32)
    y = xf * lax.rsqrt(jnp.mean(xf * xf, axis=-1, keepdims=True) + EPS)
    return (y * g.astype(F32)).astype(x.dtype)


def dwconv_centred(x, w, b):
    width = w.shape[0]
    half = width // 2
    seq = x.shape[1]
    xp = jnp.pad(x, ((0, 0), (half, half), (0, 0)))
    out = xp[:, 0:seq] * w[0]
    for k in range(1, width):
        out = out + xp[:, k:k + seq] * w[k]
    return out + b


def axial_rope_tables(seq):
    rows = seq // GRID_W
    row = jnp.repeat(jnp.arange(rows), GRID_W).astype(F32)
    col = jnp.tile(jnp.arange(GRID_W), rows).astype(F32)
    axis_dim = ATTN_HEAD_DIM // 2
    inv_freq = ROPE_THETA ** (-jnp.arange(0, axis_dim, 2, dtype=F32) / axis_dim)
    ang = jnp.concatenate([row[:, None] * inv_freq, col[:, None] * inv_freq], axis=-1)
    return jnp.cos(ang), jnp.sin(ang)


def apply_rope(x, cos, sin):
    xf = x.astype(F32).reshape(x.shape[:-1] + (-1, 2))
    xe, xo = xf[..., 0], xf[..., 1]
    c = cos[None, :, None, :]
    s = sin[None, :, None, :]
    out = jnp.stack([xe * c - xo * s, xe * s + xo * c], axis=-1)
    return out.reshape(x.shape).astype(x.dtype)


def hyena_pos_features(seq):
    t = jnp.linspace(0.0, 1.0, seq, dtype=F32)
    w = 2.0 * math.pi * jnp.arange(seq, dtype=F32) / seq
    f = jnp.linspace(1e-4, HY_BANDS - 1, HY_BANDS, dtype=F32)
    fw = w[:, None] * f[None, :]
    z = jnp.concatenate([t[:, None], jnp.cos(fw), -jnp.sin(fw)], axis=-1)
    deltas = jnp.abs(jnp.linspace(math.log(HY_TARGET) / HY_SLOW_PCT,
                                  math.log(HY_TARGET) / HY_FAST_PCT, D_MODEL, dtype=F32))
    window = jnp.exp(-t[:, None] * deltas[None, :])
    return z, window


def segsum_exp(cs):
    T = cs.shape[-1]
    diff = cs[..., :, None] - cs[..., None, :]
    mask = jnp.tril(jnp.ones((T, T), dtype=bool))
    return jnp.exp(jnp.where(mask, diff, -jnp.inf))


def ssd_scan(x, dt, a, bm, cm):
    bsz, seq = x.shape[:2]
    nc = seq // SSD_CHUNK
    R = SSD_HEADS // SSD_GROUPS
    dt = dt.astype(F32)
    xc = (x.astype(F32) * dt[..., None]).reshape(bsz, nc, SSD_CHUNK, SSD_GROUPS, R, SSD_HEAD_DIM)
    la = (dt * a.astype(F32)).reshape(bsz, nc, SSD_CHUNK, SSD_GROUPS, R).transpose(0, 3, 4, 1, 2)
    bc = bm.astype(F32).reshape(bsz, nc, SSD_CHUNK, SSD_GROUPS, SSD_STATE)
    cc = cm.astype(F32).reshape(bsz, nc, SSD_CHUNK, SSD_GROUPS, SSD_STATE)
    cs = jnp.cumsum(la, axis=-1)
    cb = jnp.einsum('bclgn,bcsgn->bgcls', cc, bc)
    wgt = cb[:, :, None] * segsum_exp(cs)
    y_diag = jnp.einsum('bgrcls,bcsgrp->bclgrp', wgt, xc)
    ds = jnp.exp(cs[..., -1:] - cs).transpose(0, 3, 4, 1, 2)
    states = jnp.einsum('bcsgn,bcsgrp->bcgrpn', bc, xc * ds[..., None])
    chunk_tot = jnp.pad(cs[..., -1], ((0, 0), (0, 0), (0, 0), (1, 0)))
    decay_chunk = segsum_exp(jnp.cumsum(chunk_tot, axis=-1))
    states = jnp.concatenate([jnp.zeros_like(states[:, :1]), states], axis=1)
    states_in = jnp.einsum('bgrzc,bcgrpn->bzgrpn', decay_chunk, states)[:, :-1]
    y_off = jnp.einsum('bclgn,bcgrpn->bclgrp', cc, states_in) * jnp.exp(cs).transpose(0, 3, 4, 1, 2)[..., None]
    return (y_diag + y_off).reshape(bsz, seq, SSD_HEADS, SSD_HEAD_DIM)


def block_attention(q, k, v):
    bsz, seq = q.shape[:2]
    nb = seq // Q_BLOCK
    scale = ATTN_HEAD_DIM ** -0.5
    qb = jnp.moveaxis(q.reshape(bsz, nb, Q_BLOCK, ATTN_KV_HEADS, ATTN_Q_GROUP, ATTN_HEAD_DIM), 1, 0)

    def one_block(qblk):
        s = jnp.einsum('bqkgd,bskd->bkgqs', qblk, k).astype(F32) * scale
        p = jax.nn.softmax(s, axis=-1).astype(v.dtype)
        return jnp.einsum('bkgqs,bskd->bqkgd', p, v)

    ob = lax.map(one_block, qb)
    return jnp.moveaxis(ob, 0, 1).reshape(bsz, seq, D_ATTN)


def ssd_attn_mixer(h, rope, w_in, w_out, conv_w, conv_b, a_log, dt_bias, d_skip, ssd_norm, q_norm, k_norm):
    bsz, seq = h.shape[:2]
    cos, sin = rope
    proj = h @ w_in
    o1 = D_SSD
    o2 = o1 + SSD_CONV_CH
    o3 = o2 + 2 * SSD_HEADS
    o4 = o3 + D_ATTN
    o5 = o4 + D_KV
    z, xbc, dt, q, k, v = jnp.split(proj, [o1, o2, o3, o4, o5], axis=-1)
    xbc = jax.nn.silu(dwconv_centred(xbc, conv_w, conv_b))
    xs, bm, cm = jnp.split(xbc, [D_SSD, D_SSD + SSD_GROUPS * SSD_STATE], axis=-1)
    xs = xs.reshape(bsz, seq, SSD_HEADS, SSD_HEAD_DIM)
    bm = bm.reshape(bsz, seq, SSD_GROUPS, SSD_STATE)
    cm = cm.reshape(bsz, seq, SSD_GROUPS, SSD_STATE)
    dt_f = jax.nn.softplus(dt[..., :SSD_HEADS] + dt_bias[0])
    dt_b = jax.nn.softplus(dt[..., SSD_HEADS:] + dt_bias[1])
    a_f = -jnp.exp(a_log[0].astype(F32))
    a_b = -jnp.exp(a_log[1].astype(F32))
    flip = lambda t: jnp.flip(t, axis=1)
    y = (ssd_scan(xs, dt_f, a_f, bm, cm)
         + flip(ssd_scan(flip(xs), flip(dt_b), a_b, flip(bm), flip(cm)))
         + xs.astype(F32) * d_skip.astype(F32)[:, None])
    y = y.reshape(bsz, seq, D_SSD).astype(h.dtype)
    y_ssd = rms_norm(y * jax.nn.silu(z), ssd_norm)
    q = apply_rope(rms_norm(q.reshape(bsz, seq, ATTN_HEADS, ATTN_HEAD_DIM), q_norm), cos, sin)
    k = apply_rope(rms_norm(k.reshape(bsz, seq, ATTN_KV_HEADS, ATTN_HEAD_DIM), k_norm), cos, sin)
    v = v.reshape(bsz, seq, ATTN_KV_HEADS, ATTN_HEAD_DIM)
    q = q.reshape(bsz, seq, ATTN_KV_HEADS, ATTN_Q_GROUP, ATTN_HEAD_DIM)
    y_attn = block_attention(q, k, v)
    return jnp.concatenate([y_ssd, y_attn], axis=-1) @ w_out


def hyena_kernel(z, window, w1, b1, w2, b2, w3, b3, freq, w_out):
    h = jnp.sin(freq * (z @ w1 + b1))
    h = jnp.sin(freq * (h @ w2 + b2))
    h = jnp.sin(freq * (h @ w3 + b3))
    h = (h @ w_out).astype(F32)
    h_fwd = h[:, :D_MODEL] * window
    h_bwd = h[:, D_MODEL:] * window
    kern = jnp.concatenate([h_fwd, jnp.zeros((1, D_MODEL), F32), h_bwd[:0:-1]], axis=0)
    return kern / jnp.sum(jnp.abs(kern), axis=0, keepdims=True)


def bidir_long_conv(u, kern, skip):
    seq = u.shape[1]
    n = 2 * seq
    uf = jnp.fft.rfft(u.astype(F32), n=n, axis=1)
    kf = jnp.fft.rfft(kern, n=n, axis=0)
    y = jnp.fft.irfft(uf * kf[None], n=n, axis=1)[:, :seq]
    return (y + u.astype(F32) * skip.astype(F32)).astype(u.dtype)


def hyena_mixer(h, hz, window, w_in, conv_w, conv_b, f_w1, f_b1, f_w2, f_b2, f_w3, f_b3, f_freq, f_w_out, skip, w_out):
    u = dwconv_centred(h @ w_in, conv_w, conv_b)
    x0, x1, v = jnp.split(u, 3, axis=-1)
    kern = hyena_kernel(hz, window, f_w1, f_b1, f_w2, f_b2, f_w3, f_b3, f_freq, f_w_out)
    y = x0 * bidir_long_conv(v * x1, kern, skip)
    return y @ w_out


def memory_cross_attention(h, mem_n, wq, wk, wv, wo):
    bsz, seq = h.shape[:2]
    n_mem = mem_n.shape[1]
    q = (h @ wq).reshape(bsz, seq, XA_HEADS, XA_HEAD_DIM)
    k = (mem_n @ wk).reshape(bsz, n_mem, XA_HEADS, XA_HEAD_DIM)
    v = (mem_n @ wv).reshape(bsz, n_mem, XA_HEADS, XA_HEAD_DIM)
    s = jnp.einsum('blhd,bmhd->bhlm', q, k).astype(F32) * (XA_HEAD_DIM ** -0.5)
    p = jax.nn.softmax(s, axis=-1).astype(v.dtype)
    o = jnp.einsum('bhlm,bmhd->blhd', p, v).reshape(bsz, seq, D_MODEL)
    return o @ wo


def conv_ffn(h, w_in, conv_w, conv_b, w_out):
    u = dwconv_centred(h @ w_in, conv_w, conv_b)
    g, up = jnp.split(u, 2, axis=-1)
    return (jax.nn.silu(g) * up) @ w_out


def trunk(x, mem, prm):
    seq = x.shape[1]
    rope = axial_rope_tables(seq)
    hz, window = hyena_pos_features(seq)
    for i in range(DEPTH):
        h = rms_norm(x, prm['norm_mix'][i])
        if i % 2 == 0:
            e = i // 2
            x = x + ssd_attn_mixer(h, rope, prm['mix_w_in'][e], prm['mix_w_out'][e],
                                   prm['ssd_conv_w'][e], prm['ssd_conv_b'][e], prm['ssd_a_log'][e],
                                   prm['ssd_dt_bias'][e], prm['ssd_d'][e], prm['ssd_norm'][e],
                                   prm['attn_q_norm'][e], prm['attn_k_norm'][e])
        else:
            o = i // 2
            x = x + hyena_mixer(h, hz, window, prm['hy_w_in'][o], prm['hy_conv_w'][o], prm['hy_conv_b'][o],
                                prm['hy_f_w1'][o], prm['hy_f_b1'][o], prm['hy_f_w2'][o], prm['hy_f_b2'][o],
                                prm['hy_f_w3'][o], prm['hy_f_b3'][o], prm['hy_f_freq'][o], prm['hy_f_w_out'][o],
                                prm['hy_skip'][o], prm['hy_w_out'][o])
        x = x + memory_cross_attention(rms_norm(x, prm['norm_xa'][i]), rms_norm(mem, prm['norm_mem'][i]),
                                       prm['xa_wq'][i], prm['xa_wk'][i], prm['xa_wv'][i], prm['xa_wo'][i])
        x = x + conv_ffn(rms_norm(x, prm['norm_ffn'][i]), prm['ffn_w_in'][i], prm['ffn_conv_w'][i],
                         prm['ffn_conv_b'][i], prm['ffn_w_out'][i])
    return rms_norm(x, prm['final_norm'])


def setup_inputs(seed: int = 0) -> dict:
    key = jax.random.key(seed)
    ks = iter(jax.random.split(key, 64))

    def w(shape, fan_in):
        return jax.random.normal(next(ks), shape, F32) * (fan_in ** -0.5)

    def gain(shape):
        return 1.0 + 0.02 * jax.random.normal(next(ks), shape, F32)

    def small(shape):
        return 0.01 * jax.random.normal(next(ks), shape, F32)

    F2 = 2 * D_FF
    dt0 = jnp.exp(jax.random.uniform(next(ks), (N_EVEN, 2, SSD_HEADS), F32, math.log(1e-3), math.log(1e-1)))
    dt_bias = dt0 + jnp.log(-jnp.expm1(-dt0))
    a_log = jnp.log(jax.random.uniform(next(ks), (N_EVEN, 2, SSD_HEADS), F32, 1.0, 16.0))
    return {
        'x_prompt': jax.random.normal(next(ks), (BATCH, SEQ, D_MODEL), F32),
        'x_sample': jax.random.normal(next(ks), (DEC_BATCH, DEC_SEQ, D_MODEL), F32),
        'mem_prompt': jax.random.normal(next(ks), (BATCH, N_MEM, D_MODEL), F32),
        'mem_sample': jax.random.normal(next(ks), (DEC_BATCH, N_MEM, D_MODEL), F32),
        'norm_mix': gain((DEPTH, D_MODEL)),
        'norm_xa': gain((DEPTH, D_MODEL)),
        'norm_mem': gain((DEPTH, D_MODEL)),
        'norm_ffn': gain((DEPTH, D_MODEL)),
        'xa_wq': w((DEPTH, D_MODEL, D_MODEL), D_MODEL),
        'xa_wk': w((DEPTH, D_MODEL, D_MODEL), D_MODEL),
        'xa_wv': w((DEPTH, D_MODEL, D_MODEL), D_MODEL),
        'xa_wo': w((DEPTH, D_MODEL, D_MODEL), D_MODEL),
        'ffn_w_in': w((DEPTH, D_MODEL, F2), D_MODEL),
        'ffn_conv_w': w((DEPTH, FFN_CONV_W, F2), FFN_CONV_W),
        'ffn_conv_b': small((DEPTH, F2)),
        'ffn_w_out': w((DEPTH, D_FF, D_MODEL), D_FF),
        'mix_w_in': w((N_EVEN, D_MODEL, EVEN_IN_W), D_MODEL),
        'mix_w_out': w((N_EVEN, EVEN_OUT_W, D_MODEL), EVEN_OUT_W),
        'ssd_conv_w': w((N_EVEN, SSD_CONV_W, SSD_CONV_CH), SSD_CONV_W),
        'ssd_conv_b': small((N_EVEN, SSD_CONV_CH)),
        'ssd_a_log': a_log,
        'ssd_dt_bias': dt_bias,
        'ssd_d': gain((N_EVEN, SSD_HEADS)),
        'ssd_norm': gain((N_EVEN, D_SSD)),
        'attn_q_norm': gain((N_EVEN, ATTN_HEAD_DIM)),
        'attn_k_norm': gain((N_EVEN, ATTN_HEAD_DIM)),
        'hy_w_in': w((N_ODD, D_MODEL, 3 * D_MODEL), D_MODEL),
        'hy_conv_w': w((N_ODD, HY_SHORT_W, 3 * D_MODEL), HY_SHORT_W),
        'hy_conv_b': small((N_ODD, 3 * D_MODEL)),
        'hy_f_w1': w((N_ODD, HY_EMB, HY_FILTER_W), HY_EMB),
        'hy_f_b1': small((N_ODD, HY_FILTER_W)),
        'hy_f_w2': w((N_ODD, HY_FILTER_W, HY_FILTER_W), HY_FILTER_W),
        'hy_f_b2': small((N_ODD, HY_FILTER_W)),
        'hy_f_w3': w((N_ODD, HY_FILTER_W, HY_FILTER_W), HY_FILTER_W),
        'hy_f_b3': small((N_ODD, HY_FILTER_W)),
        'hy_f_freq': gain((N_ODD, HY_FILTER_W)),
        'hy_f_w_out': w((N_ODD, HY_FILTER_W, 2 * D_MODEL), HY_FILTER_W),
        'hy_skip': jax.random.normal(next(ks), (N_ODD, D_MODEL), F32),
        'hy_w_out': w((N_ODD, D_MODEL, D_MODEL), D_MODEL),
        'final_norm': gain((D_MODEL,)),
    }


def reference(x_prompt, x_sample, mem_prompt, mem_sample, norm_mix, norm_xa, norm_mem, norm_ffn,
              xa_wq, xa_wk, xa_wv, xa_wo, ffn_w_in, ffn_conv_w, ffn_conv_b, ffn_w_out,
              mix_w_in, mix_w_out, ssd_conv_w, ssd_conv_b, ssd_a_log, ssd_dt_bias, ssd_d, ssd_norm,
              attn_q_norm, attn_k_norm, hy_w_in, hy_conv_w, hy_conv_b, hy_f_w1, hy_f_b1, hy_f_w2, hy_f_b2,
              hy_f_w3, hy_f_b3, hy_f_freq, hy_f_w_out, hy_skip, hy_w_out, final_norm):
    prm = dict(norm_mix=norm_mix, norm_xa=norm_xa, norm_mem=norm_mem, norm_ffn=norm_ffn,
               xa_wq=xa_wq, xa_wk=xa_wk, xa_wv=xa_wv, xa_wo=xa_wo,
               ffn_w_in=ffn_w_in, ffn_conv_w=ffn_conv_w, ffn_conv_b=ffn_conv_b, ffn_w_out=ffn_w_out,
               mix_w_in=mix_w_in, mix_w_out=mix_w_out, ssd_conv_w=ssd_conv_w, ssd_conv_b=ssd_conv_b,
               ssd_a_log=ssd_a_log, ssd_dt_bias=ssd_dt_bias, ssd_d=ssd_d, ssd_norm=ssd_norm,
               attn_q_norm=attn_q_norm, attn_k_norm=attn_k_norm,
               hy_w_in=hy_w_in, hy_conv_w=hy_conv_w, hy_conv_b=hy_conv_b,
               hy_f_w1=hy_f_w1, hy_f_b1=hy_f_b1, hy_f_w2=hy_f_w2, hy_f_b2=hy_f_b2,
               hy_f_w3=hy_f_w3, hy_f_b3=hy_f_b3, hy_f_freq=hy_f_freq, hy_f_w_out=hy_f_w_out,
               hy_skip=hy_skip, hy_w_out=hy_w_out, final_norm=final_norm)
    y_prompt = trunk(x_prompt, mem_prompt, prm)
    y_sample = trunk(x_sample, mem_sample, prm)
    return (y_prompt, y_sample)
```

```python
import math
import types
import contextlib
import numpy as np
import concourse.bass as bass
import concourse.mybir as mybir
from concourse.bass_utils import run_bass_kernel_spmd

F32 = mybir.dt.float32
BF16 = mybir.dt.bfloat16
ALU = mybir.AluOpType
AF = mybir.ActivationFunctionType

L = 8192
D = 2048
NL = 4
DFF = 5632
F2 = 2 * DFF
NMEM = 256
EPS = 1e-6
NFFT = 2 * L
SB_BASE = 16512


class Buf:
    __slots__ = ("last_w", "readers")

    def __init__(self):
        self.last_w = None
        self.readers = []


class Op:
    __slots__ = ("eng", "fn", "deps", "signal", "sem", "count", "dma", "idx")

    def __init__(self, eng, fn, dma):
        self.eng = eng
        self.fn = fn
        self.deps = ()
        self.signal = False
        self.sem = None
        self.count = 0
        self.dma = dma


def freeze(fn):
    if fn is None or fn.__closure__ is None:
        return fn
    cells = []
    for c in fn.__closure__:
        try:
            cells.append(types.CellType(c.cell_contents))
        except ValueError:
            cells.append(types.CellType())
    return types.FunctionType(fn.__code__, fn.__globals__, fn.__name__, fn.__defaults__, tuple(cells))


ENGS = ("pe", "act", "dve", "pool", "sp")
NDMASEM = {"sp": 32, "act": 4, "pool": 12}


class Rec:
    def __init__(self):
        self.ops = []
        self.bufs = []
        self.last_eng = {e: None for e in ENGS}
        self.dma_rr = {q: 0 for q in NDMASEM}
        self.dma_last = {q: [None] * n for q, n in NDMASEM.items()}

    def buf(self):
        b = Buf()
        self.bufs.append(b)
        return b

    def _add(self, o, reads, writes, extra=()):
        deps = set(extra)
        for b in reads:
            if b.last_w is not None:
                deps.add(b.last_w)
        for b in writes:
            if b.last_w is not None:
                deps.add(b.last_w)
            deps.update(b.readers)
        for b in reads:
            b.readers.append(o)
        for b in writes:
            b.last_w = o
            b.readers = []
        deps.discard(o)
        if o.eng == "pe" and not o.dma:
            deps = {d for d in deps if d.dma or d.eng != "pe"}
        o.deps = deps
        o.idx = len(self.ops)
        self.ops.append(o)
        return o

    def op(self, eng, fn, reads=(), writes=()):
        o = Op(eng, freeze(fn), False)
        self._add(o, reads, writes)
        self.last_eng[eng] = o
        return o

    def dma(self, q, fn, reads=(), writes=()):
        o = Op(q, fn, True)
        slot = self.dma_rr[q]
        self.dma_rr[q] = (slot + 1) % NDMASEM[q]
        prev = self.dma_last[q][slot]
        o.sem = (q, slot)
        self._add(o, reads, writes, extra=(prev,) if prev is not None else ())
        self.dma_last[q][slot] = o
        return o

    def barrier(self):
        pend = [o for o in self.last_eng.values() if o is not None]
        for q in NDMASEM:
            pend += [o for o in self.dma_last[q] if o is not None]
        for e in ENGS:
            o = Op(e, None, False)
            o.deps = set(pend)
            o.idx = len(self.ops)
            self.ops.append(o)
        for b in self.bufs:
            b.last_w = None
            b.readers = []
        self.bufs = []

    def emit(self, nc, final_wait_ops=()):
        for o in self.ops:
            for d in o.deps:
                if not d.dma:
                    d.signal = True
        fin = Op("sp", None, False)
        fin.deps = set(final_wait_ops)
        fin.idx = len(self.ops)
        for d in fin.deps:
            if not d.dma:
                d.signal = True
        self.ops.append(fin)
        cnt = {e: 0 for e in ENGS}
        dcnt = {q: [0] * n for q, n in NDMASEM.items()}
        for o in self.ops:
            if o.dma:
                q, s = o.sem
                dcnt[q][s] += 16
                o.count = dcnt[q][s]
            elif o.signal:
                cnt[o.eng] += 1
                o.count = cnt[o.eng]
        with contextlib.ExitStack() as st:
            csem = {e: st.enter_context(nc.semaphore(f"c_{e}")) for e in ("pe", "act", "dve", "pool")}
            dsem = {q: [st.enter_context(nc.semaphore(f"d_{q}{i}")) for i in range(n)] for q, n in NDMASEM.items()}
            segs, cur = [], []
            for o in self.ops:
                if o.fn is None and o.eng == "pe" and cur:
                    segs.append(cur)
                    cur = []
                cur.append(o)
            segs.append(cur)
            seen = {e: {} for e in ENGS}

            def run(e, eng, ops):
                sn = seen[e]
                for o in ops:
                    if o.eng != e:
                        continue
                    for d in sorted(o.deps, key=lambda d: d.idx):
                        if d.dma:
                            sem = dsem[d.sem[0]][d.sem[1]]
                            key = ("d",) + d.sem
                        else:
                            sem = csem[d.eng]
                            key = d.eng
                        if sn.get(key, 0) >= d.count:
                            continue
                        sn[key] = d.count
                        eng.wait_ge(sem, d.count)
                    if o.fn is None:
                        continue
                    ins = o.fn(eng)
                    if o.dma:
                        ins.then_inc(dsem[o.sem[0]][o.sem[1]], 16)
                    elif o.signal:
                        ins.then_inc(csem[e], 1)

            for ops in segs:
                with nc.Block() as block:
                    @block.tensor
                    def _(eng, ops=ops):
                        run("pe", eng, ops)

                    @block.scalar
                    def _(eng, ops=ops):
                        run("act", eng, ops)

                    @block.vector
                    def _(eng, ops=ops):
                        run("dve", eng, ops)

                    @block.gpsimd
                    def _(eng, ops=ops):
                        run("pool", eng, ops)

                    @block.sync
                    def _(eng, ops=ops):
                        run("sp", eng, ops)


def _chunked(v):
    v = np.asarray(v, np.float32)
    return np.ascontiguousarray(v.reshape(-1, 128).T)


class Packer:
    def __init__(self):
        self.cols = {}
        self.parts = []
        self.n = 0

    def add(self, name, arr):
        arr = np.asarray(arr, np.float32)
        if arr.shape[0] < 128:
            arr = np.concatenate([arr, np.zeros((128 - arr.shape[0],) + arr.shape[1:], np.float32)], 0)
        arr = arr.reshape(128, -1)
        self.cols[name] = self.n
        self.parts.append(arr)
        self.n += arr.shape[1]

    def done(self):
        return np.ascontiguousarray(np.concatenate(self.parts, 1))


PERM = np.concatenate([np.arange(0, 128, 2), np.arange(1, 128, 2)])
SWAP = np.concatenate([PERM[64:], PERM[:64]])


def layout_offsets():
    sm, rw, cs = {}, {}, {}
    n = 0
    for i in range(NL):
        for nm in ("g_mix", "g_xa", "g_ffn"):
            sm[f"{nm}{i}"] = n
            n += 16
    sm["g_fin"] = n
    n += 16
    for i in range(NL):
        sm[f"ffnc{i}"] = n
        n += 88 * 4
    for e in range(2):
        sm[f"ssdc{e}"] = n
        n += 24 * 6
    for o in range(2):
        sm[f"hyc{o}"] = n
        n += 48 * 4
    for e in range(2):
        sm[f"qk{e}"] = n
        n += 4
    for o in range(2):
        sm[f"hysk{o}"] = n
        n += 16
    for o in range(2):
        sm[f"hyf{o}"] = n
        n += 4
    sm["_n"] = n
    n = 0
    for e in range(2):
        for nm, w in (("ssdn", 2048), ("dtb", 64), ("alog", 64), ("dsk", 32)):
            rw[f"{nm}{e}"] = n
            n += w
    for i in range(NL):
        rw[f"gmem{i}"] = n
        n += 2048
    rw["delta"] = n
    n += 2048
    rw["_n"] = n
    n = 0
    for nm in ("ident", "ones", "triF", "triB", "COS", "SIN", "NSIN", "C16", "S16", "NS16"):
        cs[nm] = n
        n += 128
    cs["tcol"] = n
    n += 64
    cs["tcolr"] = n
    n += 64
    cs["_n"] = n
    return sm, rw, cs


SM, RW, CS = layout_offsets()


def pack_small(inp):
    P = Packer()
    for i in range(NL):
        P.add(f"g_mix{i}", _chunked(inp["norm_mix"][i]))
        P.add(f"g_xa{i}", _chunked(inp["norm_xa"][i]))
        P.add(f"g_ffn{i}", _chunked(inp["norm_ffn"][i]))
    P.add("g_fin", _chunked(inp["final_norm"]))
    for i in range(NL):
        st = np.concatenate([inp["ffn_conv_w"][i], inp["ffn_conv_b"][i][None]], 0)
        P.add(f"ffnc{i}", st.reshape(4, 88, 128).transpose(2, 1, 0))
    for e in range(2):
        st = np.concatenate([inp["ssd_conv_w"][e], inp["ssd_conv_b"][e][None]], 0)
        P.add(f"ssdc{e}", st.reshape(6, 24, 128).transpose(2, 1, 0))
    for o in range(2):
        st = np.concatenate([inp["hy_conv_w"][o], inp["hy_conv_b"][o][None]], 0)
        P.add(f"hyc{o}", st.reshape(4, 48, 128).transpose(2, 1, 0))
    for e in range(2):
        qn, kn = inp["attn_q_norm"][e], inp["attn_k_norm"][e]
        P.add(f"qk{e}", np.stack([qn[PERM], qn[SWAP], kn[PERM], kn[SWAP]], 1))
    for o in range(2):
        P.add(f"hysk{o}", _chunked(inp["hy_skip"][o]))
    for o in range(2):
        P.add(f"hyf{o}", np.stack([inp["hy_f_b1"][o], inp["hy_f_b2"][o], inp["hy_f_b3"][o], inp["hy_f_freq"][o]], 1))
    small = P.done()
    assert P.cols == {k: v for k, v in SM.items() if k != "_n"} and P.n == SM["_n"]
    R = Packer()
    rep = lambda v: np.broadcast_to(np.asarray(v, np.float32).reshape(1, -1), (128, np.asarray(v).size))
    for e in range(2):
        R.add(f"ssdn{e}", rep(inp["ssd_norm"][e]))
        R.add(f"dtb{e}", rep(inp["ssd_dt_bias"][e]))
        R.add(f"alog{e}", rep(inp["ssd_a_log"][e]))
        R.add(f"dsk{e}", rep(inp["ssd_d"][e]))
    for i in range(NL):
        R.add(f"gmem{i}", rep(inp["norm_mem"][i]))
    deltas = np.abs(np.linspace(math.log(1e-2) / 1.5, math.log(1e-2) / 0.3, D, dtype=np.float32))
    R.add("delta", rep(deltas))
    rows = R.done()
    assert R.n == RW["_n"]
    return small, rows


def make_consts():
    C = Packer()
    i = np.arange(128)
    C.add("ident", np.eye(128))
    C.add("ones", np.ones((128, 128)))
    C.add("triF", (i[:, None] <= i[None, :]).astype(np.float32))
    C.add("triB", (i[:, None] >= i[None, :]).astype(np.float32))
    a = 2.0 * np.pi * np.outer(i, i).astype(np.float64) / 128.0
    C.add("COS", np.cos(a))
    C.add("SIN", np.sin(a))
    C.add("NSIN", -np.sin(a))
    a = 2.0 * np.pi * np.outer(i, i).astype(np.float64) / float(NFFT)
    C.add("C16", np.cos(a))
    C.add("S16", np.sin(a))
    C.add("NS16", -np.sin(a))
    t = np.linspace(0.0, 1.0, L, dtype=np.float32)
    C.add("tcol", t.reshape(64, 128).T)
    tr = t[::-1].copy()
    tr[L - 1] = 1.0e4
    C.add("tcolr", tr.reshape(64, 128).T)
    consts = C.done()
    sel2 = np.zeros((64, 32, 128), np.float32)
    for h in range(32):
        sel2[h, h, :] = 1.0
        sel2[32 + h, h, :] = 1.0
    rows = L // 64
    row = np.repeat(np.arange(rows), 64).astype(np.float32)
    col = np.tile(np.arange(64), rows).astype(np.float32)
    inv = (10000.0 ** (-np.arange(0, 64, 2, dtype=np.float32) / 64.0)).astype(np.float32)
    ang = np.concatenate([row[:, None] * inv, col[:, None] * inv], -1)
    c, s = np.cos(ang).T, np.sin(ang).T
    ropeC = np.concatenate([c, c], 0).astype(np.float32)
    ropeS = np.concatenate([-s, s], 0).astype(np.float32)
    w = (2.0 * np.pi * np.arange(L, dtype=np.float32) / L).astype(np.float32)
    f = np.linspace(1e-4, 15.0, 16, dtype=np.float32)
    fw = w[:, None] * f[None, :]
    z = np.concatenate([t[:, None], np.cos(fw), -np.sin(fw)], -1).astype(np.float32)
    zfT = np.ascontiguousarray(z.T)
    zfTr = np.ascontiguousarray(z[::-1].T)
    return consts, sel2.reshape(64, 4096), ropeC, ropeS, zfT, zfTr


WNAMES = [
    ("xa_wq", (NL, D, D)), ("xa_wk", (NL, D, D)), ("xa_wv", (NL, D, D)), ("xa_wo", (NL, D, D)),
    ("ffn_w_in", (NL, D, F2)), ("ffn_w_out", (NL, DFF, D)),
    ("mix_w_in", (2, D, 8256)), ("mix_w_out", (2, 4096, D)),
    ("mix_qkp", (2, D, 2560)), ("mix_qks", (2, D, 2560)),
    ("hy_w_in", (2, D, 3 * D)), ("hy_w_out", (2, D, D)),
    ("hy_f_w1", (2, 33, 64)), ("hy_f_w2", (2, 64, 64)), ("hy_f_w3", (2, 64, 64)), ("hy_f_w_out", (2, 64, 2 * D)),
]


class T:
    __slots__ = ("ap", "b")

    def __init__(self, ap, b):
        self.ap = ap
        self.b = b


class Ring:
    def __init__(self, items):
        self.items = items
        self.i = 0

    def next(self):
        t = self.items[self.i]
        self.i = (self.i + 1) % len(self.items)
        return t


def split(total, w):
    out, s = [], 0
    while s < total:
        n = min(w, total - s)
        out.append((s, n))
        s += n
    return out


class Builder:
    def __init__(self, n_layers=NL, debug=(), stop_after=None):
        self.nc = nc = bass.Bass("TRN2", target_bir_lowering=False)
        self.R = Rec()
        self.debug = set(debug)
        self.stop_after = stop_after
        self.n_layers = n_layers
        self.dram = {}
        din = lambda name, shape, dt=F32: nc.dram_tensor(name, list(shape), dt, kind="ExternalInput").ap()
        self.x = din("x", (L, D))
        self.mem = din("mem", (NMEM, D))
        self.W = {n: din(n, s) for n, s in WNAMES}
        self.small_d = din("small", (128, SM["_n"]))
        self.rows_d = din("rows", (128, RW["_n"]))
        self.consts_d = din("consts", (128, CS["_n"]))
        self.sel2_d = din("sel2", (64, 4096))
        self.ropeC = din("ropeC", (128, L))
        self.ropeS = din("ropeS", (128, L))
        self.zfT = din("zfT", (33, L))
        self.zfTr = din("zfTr", (33, L))
        self.out = nc.dram_tensor("y", [L, D], F32, kind="ExternalOutput").ap()
        self.Wb = {n: self.scr(n + "_b", s, BF16) for n, s in WNAMES}
        self.XT = self.scr("XT", (D, L), F32)
        self.uid = 0
        self.PSB = [nc.alloc_psum_tensor(f"psb{i}", [128, 512], F32).ap() for i in range(8)]
        self.off = 0
        self.persist = 0
        self.final_ops = []

    def scr(self, name, shape, dt):
        kind = "ExternalOutput" if name in self.debug else "Internal"
        t = self.nc.dram_tensor(name, list(shape), dt, kind=kind).ap()
        self.dram[name] = t
        return t

    def alloc(self, n, dt=F32):
        nb = n * (2 if dt == BF16 else 4)
        nb = (nb + 31) // 32 * 32
        assert self.off + nb <= 52480 * 4, f"SBUF overflow {self.off + nb}"
        self.uid += 1
        h = self.nc.alloc_sbuf_tensor_at(f"sb{self.uid}", [128, n], dt, offset=SB_BASE + self.off)
        self.off += nb
        return h.ap()

    def tile(self, n, dt=F32):
        return T(self.alloc(n, dt), self.R.buf())

    def ring(self, k, n, dt=F32):
        return Ring([self.tile(n, dt) for _ in range(k)])

    def bank(self, i):
        return self.PSB[i]

    def phase(self):
        self.R.barrier()
        self.off = self.persist
        self.banks = [T(self.bank(i), self.R.buf()) for i in range(8)]
        self.bank_i = 0
        for t in self.ptiles:
            t.b = self.R.buf()

    def soft_phase(self, keep, off):
        self.R.barrier()
        self.off = off
        self.banks = [T(self.bank(i), self.R.buf()) for i in range(8)]
        self.bank_i = 0
        for t in self.ptiles + list(keep):
            t.b = self.R.buf()

    def nbank(self, k=1):
        if self.bank_i + k > 8:
            self.bank_i = 0
        r = self.banks[self.bank_i:self.bank_i + k]
        self.bank_i = (self.bank_i + k) % 8
        return r

    def pe(self, fn, r, w):
        return self.R.op("pe", fn, [t.b for t in r], [t.b for t in w])

    def act(self, fn, r, w):
        return self.R.op("act", fn, [t.b for t in r], [t.b for t in w])

    def dve(self, fn, r, w):
        return self.R.op("dve", fn, [t.b for t in r], [t.b for t in w])

    def pool(self, fn, r, w):
        return self.R.op("pool", fn, [t.b for t in r], [t.b for t in w])

    def dma(self, out, in_, r=(), w=(), q="sp"):
        return self.R.dma(q, lambda e: e.dma_start(out=out, in_=in_), [t.b for t in r], [t.b for t in w])

    def sm(self, name, j=0, n=1):
        c = SM[name] + j
        return self.small.ap[:, c:c + n]

    def cst(self, name, n=128):
        return self.consts.ap[:, CS[name]:CS[name] + n]

    def setup(self):
        R = self.R
        self.ptiles = []
        self.consts = self.tile(CS["_n"])
        self.small = self.tile(SM["_n"])
        self.cb = self.tile(7 * 128, BF16)
        self.epsc = self.tile(8)
        self.rinv = self.tile(D)
        self.zc = self.tile(8)
        self.ptiles = [self.consts, self.small, self.cb, self.epsc, self.rinv, self.zc]
        self.persist = self.off
        self.banks = [T(self.bank(i), R.buf()) for i in range(8)]
        self.bank_i = 0
        self.dma(self.consts.ap, self.consts_d, w=[self.consts])
        self.dma(self.small.ap, self.small_d, w=[self.small])
        self.dve(lambda e: e.tensor_copy(self.cb.ap, self.consts.ap[:, 0:7 * 128]), [self.consts], [self.cb])
        self.dve(lambda e: e.memset(self.epsc.ap, EPS), [], [self.epsc])
        self.dve(lambda e: e.memset(self.zc.ap, 0.0), [], [self.zc])
        for n, s in WNAMES:
            w2 = self.W[n].rearrange("a k m -> (a k) m")
            wb2 = self.Wb[n].rearrange("a k m -> (a k) m")
            rows = s[0] * s[1]
            step = max(128, (1 << 21) // s[2] // 128 * 128)
            for r0 in range(0, rows, step):
                r1 = min(rows, r0 + step)
                self.dma(wb2[r0:r1, :], w2[r0:r1, :], q="pool")
        xin = self.ring(2, D)
        xo = self.ring(2, 16 * 512)
        identf = self.cst("ident")
        for t4 in range(L // 512):
            ot = xo.next()
            o3 = ot.ap.rearrange("p (k t) -> p k t", t=512)
            for tt in range(4):
                t0 = t4 * 512 + tt * 128
                xi = xin.next()
                self.dma(xi.ap, self.x[t0:t0 + 128, :], w=[xi])
                for half in range(2):
                    bk = self.nbank(2)
                    for j in range(8):
                        kc = half * 8 + j
                        b = bk[j // 4]
                        self.pe(lambda e, b=b, j=j, kc=kc, xi=xi: e.transpose(b.ap[:, (j % 4) * 128:(j % 4 + 1) * 128], xi.ap[:, kc * 128:(kc + 1) * 128], identf), [xi, self.consts], [b])
                    for jj in range(2):
                        b = bk[jj]
                        kc0 = half * 8 + jj * 4
                        f = self.act if jj == 0 else self.dve
                        if jj == 0:
                            self.act(lambda e, b=b, kc0=kc0, tt=tt, o3=o3: e.activation(o3[:, kc0:kc0 + 4, tt * 128:(tt + 1) * 128], b.ap.rearrange("p (k t) -> p k t", t=128), AF.Copy), [b], [ot])
                        else:
                            self.dve(lambda e, b=b, kc0=kc0, tt=tt, o3=o3: e.tensor_copy(o3[:, kc0:kc0 + 4, tt * 128:(tt + 1) * 128], b.ap.rearrange("p (k t) -> p k t", t=128)), [b], [ot])
            self.dma(self.XT.rearrange("(k p) t -> p k t", p=128)[:, :, t4 * 512:(t4 + 1) * 512], o3, r=[ot])

    def norm_block(self, hT, t0, TB, halo, gname, xst, sqr, rst):
        W = TB + 2 * halo
        h3 = hT.ap.rearrange("p (k t) -> p k t", t=W)
        onesb = self.cb.ap[:, 128:256]
        XT3 = self.XT.rearrange("(k p) t -> p k t", p=128)
        lo, hi = t0 - halo, t0 + TB + halo
        if lo < 0:
            self.dve(lambda e: e.memset(h3[:, :, 0:halo], 0.0), [], [hT])
        if hi > L:
            self.dve(lambda e: e.memset(h3[:, :, W - halo:W], 0.0), [], [hT])
        lo, hi = max(lo, 0), min(hi, L)
        pieces = split(hi - lo, 128)
        if len(pieces) > 1 and pieces[-1][1] < 8:
            (s1, n1), (s2, n2) = pieces[-2], pieces[-1]
            pieces = pieces[:-2] + [(s1, n1 + n2)]
        for (s, n) in pieces:
            a = lo + s
            c0 = a - (t0 - halo)
            xs = xst.next()
            x3 = xs.ap.rearrange("p (k t) -> p k t", t=136)
            self.dma(x3[:, :, 0:n], XT3[:, :, a:a + n], w=[xs])
            bk = self.nbank(1)[0]
            for kc in range(16):
                sq = sqr.next()
                self.act(lambda e, sq=sq, kc=kc, x3=x3, n=n: e.activation(sq.ap[:, 0:n], x3[:, kc, 0:n], AF.Square), [xs], [sq])
                self.pe(lambda e, sq=sq, kc=kc, bk=bk, n=n: e.matmul(bk.ap[:, 0:n], onesb, sq.ap[:, 0:n], start=(kc == 0), stop=(kc == 15)), [sq, self.cb], [bk])
            rs = rst.next()
            self.act(lambda e, rs=rs, bk=bk, n=n: e.activation(rs.ap[:, 0:n], bk.ap[:, 0:n], AF.Sqrt, bias=self.epsc.ap[:, 0:1], scale=1.0 / D), [bk, self.epsc], [rs])
            self.dve(lambda e, rs=rs, n=n: e.reciprocal(rs.ap[:, 0:n], rs.ap[:, 0:n]), [rs], [rs])
            for kc in range(16):
                self.dve(lambda e, kc=kc, x3=x3, rs=rs, n=n, c0=c0: e.scalar_tensor_tensor(h3[:, kc, c0:c0 + n], x3[:, kc, 0:n], self.sm(gname, kc), rs.ap[:, 0:n], ALU.mult, ALU.mult), [xs, rs, self.small], [hT])
        return h3

    def load_w(self, wt, Wb2, c0, ncols, KC):
        w3 = wt.ap[:, 0:KC * ncols].rearrange("p (k m) -> p k m", m=ncols)
        self.dma(w3, Wb2[:, c0:c0 + ncols].rearrange("(k p) m -> p k m", p=128), w=[wt])
        return w3

    def gemm_in(self, gname, halo, groups, TB=2048):
        W = TB + 2 * halo
        hT = self.tile(16 * W, BF16)
        xst = self.ring(2, 16 * 136)
        sqr = self.ring(3, 136, BF16)
        rst = self.ring(2, 136)
        wts = [[self.tile(16 * 512, BF16) for _ in range(2)] for _ in range(2)]
        wi = 0
        for t0 in range(0, L, TB):
            h3 = self.norm_block(hT, t0, TB, halo, gname, xst, sqr, rst)
            for g in groups:
                wset = wts[wi]
                wi ^= 1
                w3 = [self.load_w(wset[i], Wb2, c0, ncols, 16) for i, (Wb2, c0, ncols) in enumerate(g["w"])]
                for u in g["units"]:
                    if u["kind"] == "fm":
                        hh = u.get("halo", 0)
                        for (s, n) in split(TB, 512 - 2 * hh):
                            ncol = n + 2 * hh
                            c0 = s + (halo - hh)
                            bks = self.nbank(len(u["chunks"]))
                            for bk, (ti, coff) in zip(bks, u["chunks"]):
                                for kc in range(16):
                                    self.pe(lambda e, bk=bk, ti=ti, coff=coff, kc=kc, c0=c0, ncol=ncol, w3=w3: e.matmul(bk.ap[:, 0:ncol], w3[ti][:, kc, coff:coff + 128], h3[:, kc, c0:c0 + ncol], start=(kc == 0), stop=(kc == 15)), [wset[ti], hT], [bk])
                            u["epi"](t0 + s, n, bks)
                    else:
                        ti, coff, ncols = u["chunks"][0]
                        for tt in range(TB // 128):
                            bk = self.nbank(1)[0]
                            for kc in range(16):
                                self.pe(lambda e, bk=bk, ti=ti, coff=coff, ncols=ncols, kc=kc, tt=tt, w3=w3: e.matmul(bk.ap[:, 0:ncols], h3[:, kc, halo + tt * 128:halo + (tt + 1) * 128], w3[ti][:, kc, coff:coff + ncols], start=(kc == 0), stop=(kc == 15)), [wset[ti], hT], [bk])
                            u["epi"](t0 + tt * 128, bk)

    def gemm_out(self, AT, KC, Wb2, TB):
        aT = self.tile(KC * TB, BF16)
        a3 = aT.ap.rearrange("p (k t) -> p k t", t=TB)
        AT3 = AT.rearrange("(k p) t -> p k t", p=128)
        wts = self.ring(2, KC * 256, BF16)
        xr = self.ring(3, 512)
        for t0 in range(0, L, TB):
            for (k0, kn) in split(KC, 8):
                self.dma(a3[:, k0:k0 + kn, :], AT3[:, k0:k0 + kn, t0:t0 + TB], w=[aT])
            for m0 in range(0, D, 256):
                wt = wts.next()
                w3 = self.load_w(wt, Wb2, m0, 256, KC)
                for c in range(2):
                    row0 = m0 + c * 128
                    for (s, n) in split(TB, 512):
                        bk = self.nbank(1)[0]
                        xt = xr.next()
                        self.dma(xt.ap, self.XT[row0:row0 + 128, t0 + s:t0 + s + 512], w=[xt])
                        for kc in range(KC):
                            self.pe(lambda e, bk=bk, kc=kc, c=c, s=s, w3=w3: e.matmul(bk.ap, w3[:, kc, c * 128:(c + 1) * 128], a3[:, kc, s:s + 512], start=(kc == 0), stop=(kc == KC - 1)), [wt, aT], [bk])
                        self.dve(lambda e, bk=bk, xt=xt: e.tensor_tensor(xt.ap, bk.ap, xt.ap, ALU.add), [bk, xt], [xt])
                        self.dma(self.XT[row0:row0 + 128, t0 + s:t0 + s + 512], xt.ap, r=[xt])

    def conv_epi(self, bk, n, taps, cname, cj, nt, out_t):
        base = cj * (taps + 1)
        self.act(lambda e: e.activation(out_t.ap[:, 0:n], bk.ap[:, 0:n], AF.Identity, bias=self.sm(cname, base + taps), scale=self.sm(cname, base)), [bk, self.small], [out_t])
        for k in range(1, taps):
            self.dve(lambda e, k=k: e.scalar_tensor_tensor(out_t.ap[:, 0:n], bk.ap[:, k:k + n], self.sm(cname, base + k), out_t.ap[:, 0:n], ALU.mult, ALU.add), [bk, self.small, out_t], [out_t])

    def ffn(self, i):
        self.phase()
        AT = self.dram.get("AT") or self.scr("AT", (DFF, L), BF16)
        Wb2 = self.Wb["ffn_w_in"][i]
        tg = self.ring(2, 512)
        tu = self.ring(2, 512)
        ao = self.ring(3, 512, BF16)
        groups = []
        for gq in range(11):
            units = []
            for c in range(4):
                j = gq * 4 + c

                def epi(t, n, bks, j=j):
                    a, b = tg.next(), tu.next()
                    self.conv_epi(bks[0], n, 3, f"ffnc{i}", j, 0, a)
                    self.conv_epi(bks[1], n, 3, f"ffnc{i}", 44 + j, 0, b)
                    self.act(lambda e: e.activation(a.ap[:, 0:n], a.ap[:, 0:n], AF.Silu), [a], [a])
                    o = ao.next()
                    self.dve(lambda e: e.tensor_tensor(o.ap[:, 0:n], a.ap[:, 0:n], b.ap[:, 0:n], ALU.mult), [a, b], [o])
                    self.dma(AT[j * 128:(j + 1) * 128, t:t + n], o.ap[:, 0:n], r=[o])
                units.append(dict(kind="fm", halo=1, chunks=[(0, c * 128), (1, c * 128)], epi=epi))
            groups.append(dict(w=[(Wb2, gq * 512, 512), (Wb2, DFF + gq * 512, 512)], units=units))
        self.gemm_in(f"g_ffn{i}", 1, groups)
        self.phase()
        self.gemm_out(AT, 44, self.Wb["ffn_w_out"][i], 1024)

    def xattn(self, i):
        self.phase()
        QX = self.dram.get("QX") or self.scr("QX", (D, L), BF16)
        OX = self.dram.get("OX") or self.scr("OX", (D, L), BF16)
        identb = self.cb.ap[:, 0:128]
        kT = self.tile(16 * NMEM, BF16)
        kT3 = kT.ap.rearrange("p (k m) -> p k m", m=NMEM)
        vv = self.tile(2 * D, BF16)
        v3 = vv.ap.rearrange("p (c d) -> p c d", d=D)
        kv_persist = self.off
        mT = self.tile(16 * NMEM, BF16)
        mT3 = mT.ap.rearrange("p (k m) -> p k m", m=NMEM)
        gm = self.tile(D)
        self.dma(gm.ap, self.rows_d[:, RW[f"gmem{i}"]:RW[f"gmem{i}"] + D], w=[gm])
        mraw = self.ring(2, D)
        mn = self.ring(2, D, BF16)
        junk = self.tile(D)
        ss = self.ring(2, 8)
        for c in range(2):
            mr = mraw.next()
            self.dma(mr.ap, self.mem[c * 128:(c + 1) * 128, :], w=[mr])
            s1 = ss.next()
            self.dve(lambda e, s1=s1: e.memset(s1.ap, 0.0), [], [s1])
            self.act(lambda e, mr=mr, s1=s1: e.activation(junk.ap, mr.ap, AF.Square, accum_out=s1.ap[:, 0:1]), [mr, s1], [junk, s1])
            self.act(lambda e, s1=s1: e.activation(s1.ap[:, 1:2], s1.ap[:, 0:1], AF.Sqrt, bias=self.epsc.ap[:, 0:1], scale=1.0 / D), [s1, self.epsc], [s1])
            self.dve(lambda e, s1=s1: e.reciprocal(s1.ap[:, 2:3], s1.ap[:, 1:2]), [s1], [s1])
            m2 = mn.next()
            self.dve(lambda e, mr=mr, s1=s1, m2=m2: e.scalar_tensor_tensor(m2.ap, mr.ap, s1.ap[:, 2:3], gm.ap, ALU.mult, ALU.mult), [mr, s1, gm], [m2])
            for q4 in range(4):
                bk = self.nbank(1)[0]
                bb = bk.ap.bitcast(BF16)
                for j in range(4):
                    kc = q4 * 4 + j
                    self.pe(lambda e, bb=bb, j=j, kc=kc, m2=m2: e.transpose(bb[:, j * 128:(j + 1) * 128], m2.ap[:, kc * 128:(kc + 1) * 128], identb), [m2, self.cb], [bk])
                self.act(lambda e, bb=bb, q4=q4, c=c: e.activation(mT3[:, q4 * 4:q4 * 4 + 4, c * 128:(c + 1) * 128], bb[:, 0:512].rearrange("p (k t) -> p k t", t=128), AF.Copy), [bk], [mT])
        wts = self.ring(2, 16 * 512, BF16)
        for m0 in range(0, D, 512):
            wt = wts.next()
            w3 = self.load_w(wt, self.Wb["xa_wk"][i], m0, 512, 16)
            for c in range(4):
                bk = self.nbank(1)[0]
                for kc in range(16):
                    self.pe(lambda e, bk=bk, kc=kc, c=c, w3=w3: e.matmul(bk.ap[:, 0:NMEM], w3[:, kc, c * 128:(c + 1) * 128], mT3[:, kc, :], start=(kc == 0), stop=(kc == 15)), [wt, mT], [bk])
                self.act(lambda e, bk=bk, c=c, m0=m0: e.activation(kT3[:, m0 // 128 + c, :], bk.ap[:, 0:NMEM], AF.Copy), [bk], [kT])
            wt = wts.next()
            w3 = self.load_w(wt, self.Wb["xa_wv"][i], m0, 512, 16)
            for c in range(2):
                bk = self.nbank(1)[0]
                for kc in range(16):
                    self.pe(lambda e, bk=bk, kc=kc, c=c, w3=w3: e.matmul(bk.ap, mT3[:, kc, c * 128:(c + 1) * 128], w3[:, kc, :], start=(kc == 0), stop=(kc == 15)), [wt, mT], [bk])
                self.act(lambda e, bk=bk, c=c, m0=m0: e.activation(v3[:, c, m0:m0 + 512], bk.ap, AF.Copy), [bk], [vv])
        self.soft_phase([kT, vv], kv_persist)
        qo = self.ring(3, 512, BF16)
        groups = []
        for gq in range(4):
            units = []
            for c in range(4):
                j = gq * 4 + c

                def epi(t, n, bks, j=j):
                    o = qo.next()
                    self.act(lambda e: e.activation(o.ap[:, 0:n], bks[0].ap[:, 0:n], AF.Copy), [bks[0]], [o])
                    self.dma(QX[j * 128:(j + 1) * 128, t:t + n], o.ap[:, 0:n], r=[o])
                units.append(dict(kind="fm", chunks=[(0, c * 128)], epi=epi))
            groups.append(dict(w=[(self.Wb["xa_wq"][i], gq * 512, 512)], units=units))
        self.gemm_in(f"g_xa{i}", 0, groups)
        self.soft_phase([kT, vv], kv_persist)
        self.xa_attend(QX, OX, kT3, v3, kT, vv)
        self.phase()
        self.gemm_out(OX, 16, self.Wb["xa_wo"][i], 2048)

    def xa_attend(self, QX, OX, kT3, v3, kT, vv):
        onesb = self.cb.ap[:, 128:256]
        qin = self.ring(2, 16 * 512, BF16)
        pT = self.ring(2, 2 * 512, BF16)
        rc = self.ring(2, 512)
        oo = self.ring(2, 4 * 512, BF16)
        QX3 = QX.rearrange("(k p) t -> p k t", p=128)
        OX3 = OX.rearrange("(k p) t -> p k t", p=128)
        sc = 512.0 ** -0.5
        for tt in range(L // 512):
            qt = qin.next()
            q3 = qt.ap.rearrange("p (k t) -> p k t", t=512)
            self.dma(q3, QX3[:, :, tt * 512:(tt + 1) * 512], w=[qt])
            for h in range(4):
                bs = self.nbank(2)
                p = pT.next()
                p3 = p.ap.rearrange("p (c t) -> p c t", t=512)
                for mc in range(2):
                    for j in range(4):
                        kc = h * 4 + j
                        self.pe(lambda e, mc=mc, j=j, kc=kc, bs=bs, q3=q3: e.matmul(bs[mc].ap, kT3[:, kc, mc * 128:(mc + 1) * 128], q3[:, kc, :], start=(j == 0), stop=(j == 3)), [kT, qt], [bs[mc]])
                    self.act(lambda e, mc=mc, bs=bs, p3=p3: e.activation(p3[:, mc, :], bs[mc].ap, AF.Exp, scale=sc), [bs[mc]], [p])
                bd = self.nbank(1)[0]
                for mc in range(2):
                    self.pe(lambda e, mc=mc, bd=bd, p3=p3: e.matmul(bd.ap, onesb, p3[:, mc, :], start=(mc == 0), stop=(mc == 1)), [p, self.cb], [bd])
                r = rc.next()
                self.dve(lambda e, r=r, bd=bd: e.reciprocal(r.ap, bd.ap), [bd], [r])
                o = oo.next()
                o3 = o.ap.rearrange("p (k t) -> p k t", t=512)
                for j in range(4):
                    bo = self.nbank(1)[0]
                    for mc in range(2):
                        self.pe(lambda e, mc=mc, j=j, bo=bo, p3=p3, h=h: e.matmul(bo.ap, v3[:, mc, (h * 4 + j) * 128:(h * 4 + j + 1) * 128], p3[:, mc, :], start=(mc == 0), stop=(mc == 1)), [vv, p], [bo])
                    self.dve(lambda e, j=j, bo=bo, r=r, o3=o3: e.tensor_tensor(o3[:, j, :], bo.ap, r.ap, ALU.mult), [bo, r], [o])
                self.dma(OX3[:, h * 4:h * 4 + 4, tt * 512:(tt + 1) * 512], o3, r=[o])

    def final(self):
        self.phase()
        hT = self.tile(16 * 512)
        xst = self.ring(2, 16 * 256)
        sqr = self.ring(3, 256, BF16)
        rst = self.ring(2, 256)
        orow = self.ring(2, D)
        identf = self.cst("ident")
        XT3 = self.XT.rearrange("(k p) t -> p k t", p=128)
        onesb = self.cb.ap[:, 128:256]
        for t0 in range(0, L, 256):
            xs = xst.next()
            x3 = xs.ap.rearrange("p (k t) -> p k t", t=256)
            self.dma(x3, XT3[:, :, t0:t0 + 256], w=[xs])
            bk = self.nbank(1)[0]
            for kc in range(16):
                sq = sqr.next()
                self.act(lambda e, sq=sq, kc=kc, x3=x3: e.activation(sq.ap, x3[:, kc, :], AF.Square), [xs], [sq])
                self.pe(lambda e, sq=sq, kc=kc, bk=bk: e.matmul(bk.ap[:, 0:256], onesb, sq.ap, start=(kc == 0), stop=(kc == 15)), [sq, self.cb], [bk])
            rs = rst.next()
            self.act(lambda e, rs=rs, bk=bk: e.activation(rs.ap, bk.ap[:, 0:256], AF.Sqrt, bias=self.epsc.ap[:, 0:1], scale=1.0 / D), [bk, self.epsc], [rs])
            self.dve(lambda e, rs=rs: e.reciprocal(rs.ap, rs.ap), [rs], [rs])
            for kc in range(16):
                self.dve(lambda e, kc=kc, x3=x3, rs=rs: e.scalar_tensor_tensor(x3[:, kc, :], x3[:, kc, :], self.sm("g_fin", kc), rs.ap, ALU.mult, ALU.mult), [xs, rs, self.small], [xs])
            for tt in range(2):
                orw = orow.next()
                for q4 in range(4):
                    bk = self.nbank(1)[0]
                    for j in range(4):
                        kc = q4 * 4 + j
                        self.pe(lambda e, bk=bk, j=j, kc=kc, tt=tt, x3=x3: e.transpose(bk.ap[:, j * 128:(j + 1) * 128], x3[:, kc, tt * 128:(tt + 1) * 128], identf), [xs, self.consts], [bk])
                    if q4 % 2 == 0:
                        self.act(lambda e, bk=bk, q4=q4, orw=orw: e.activation(orw.ap[:, q4 * 512:(q4 + 1) * 512], bk.ap, AF.Copy), [bk], [orw])
                    else:
                        self.dve(lambda e, bk=bk, q4=q4, orw=orw: e.tensor_copy(orw.ap[:, q4 * 512:(q4 + 1) * 512], bk.ap), [bk], [orw])
                o = self.dma(self.out[t0 + tt * 128:t0 + (tt + 1) * 128, :], orw.ap, r=[orw])
                self.final_ops.append(o)

    def finish(self):
        self.R.emit(self.nc, self.final_ops)
        return self.nc


def build_program(stages=None, debug=()):
    B = Builder(debug=debug)
    B.setup()
    if stages is None:
        stages = []
        for i in range(NL):
            stages += [("mix", i), ("xa", i), ("ffn", i)]
    for kind, i in stages:
        if kind == "mix":
            if i % 2 == 0:
                B.mixer_even(i // 2, i)
            else:
                B.mixer_odd(i // 2, i)
        elif kind == "xa":
            B.xattn(i)
        elif kind == "ffn":
            B.ffn(i)
    B.final()
    return B.finish()


def make_in_maps(inp):
    small, rows = pack_small(inp)
    consts, sel2, ropeC, ropeS, zfT, zfTr = make_consts()
    shared = {"small": small, "rows": rows, "consts": consts, "sel2": sel2, "ropeC": ropeC, "ropeS": ropeS,
              "zfT": zfT, "zfTr": zfTr}
    for n, _ in WNAMES:
        if n in inp:
            shared[n] = np.ascontiguousarray(np.asarray(inp[n], np.float32))
    wi = np.asarray(inp["mix_w_in"], np.float32)
    qk = wi[:, :, 5184:7744].reshape(2, D, 20, 128)
    shared["mix_qkp"] = np.ascontiguousarray(qk[:, :, :, PERM].reshape(2, D, 2560))
    shared["mix_qks"] = np.ascontiguousarray(qk[:, :, :, SWAP].reshape(2, D, 2560))
    seqs = [(inp["x_prompt"][b], inp["mem_prompt"][b]) for b in range(4)] + [(inp["x_sample"][0], inp["mem_sample"][0])]
    maps = []
    for c in range(8):
        x, m = seqs[min(c, 4)]
        d = dict(shared)
        d["x"] = np.ascontiguousarray(np.asarray(x, np.float32))
        d["mem"] = np.ascontiguousarray(np.asarray(m, np.float32))
        maps.append(d)
    return maps


_NC_CACHE = {}


def kernel(**inputs):
    inp = {k: np.asarray(v) for k, v in inputs.items()}
    if "nc" not in _NC_CACHE:
        _NC_CACHE["nc"] = build_program()
    maps = make_in_maps(inp)
    res = run_bass_kernel_spmd(_NC_CACHE["nc"], maps, core_ids=list(range(8)))
    ys = [np.asarray(res.results[c]["y"], np.float32) for c in range(5)]
    y_prompt = np.stack(ys[0:4], 0)
    y_sample = ys[4][None]
    return (y_prompt, y_sample)


def _mixer_even(self, e, i):
    self.phase()
    nm = lambda n, s, dt: self.dram.get(n) or self.scr(n, s, dt)
    ZS = nm("ZS", (L, D), BF16)
    XBCT = nm("XBCT", (3072, L), BF16)
    DT = nm("DT", (L, 64), F32)
    QT = nm("QT", (D, L), BF16)
    KT = nm("KT", (512, L), BF16)
    V = nm("V", (L, 512), BF16)
    YT = nm("YT", (4096, L), BF16)
    YF = nm("YF", (L, D), F32)
    W = self.Wb["mix_w_in"][e]
    QKP = self.Wb["mix_qkp"][e]
    QKS = self.Wb["mix_qks"][e]
    onesb = self.cb.ap[:, 128:256]
    dtb = self.tile(64)
    self.dma(dtb.ap, self.rows_d[:, RW[f"dtb{e}"]:RW[f"dtb{e}"] + 64], w=[dtb])
    onec = self.tile(8)
    self.dve(lambda en: en.memset(onec.ap, 1.0), [], [onec])
    ob = self.ring(3, 512, BF16)
    tf = self.ring(2, 512)
    tf2 = self.ring(2, 512)
    rC = self.ring(2, 512)
    rS = self.ring(2, 512)
    sqb = self.ring(2, 512, BF16)
    rsr = self.ring(2, 512)
    dto = self.ring(2, 64)
    groups = []
    for g in range(4):
        def epi_z(t, bk, g=g):
            o = ob.next()
            self.act(lambda en: en.activation(o.ap, bk.ap, AF.Silu), [bk], [o])
            self.dma(ZS[t:t + 128, g * 512:(g + 1) * 512], o.ap, r=[o])
        groups.append(dict(w=[(W, g * 512, 512)], units=[dict(kind="tm", chunks=[(0, 0, 512)], epi=epi_z)]))
    for g in range(6):
        units = []
        for c in range(4):
            j = g * 4 + c

            def epi_c(t, n, bks, j=j):
                a = tf.next()
                self.conv_epi(bks[0], n, 5, f"ssdc{e}", j, 0, a)
                o = ob.next()
                self.act(lambda en: en.activation(o.ap[:, 0:n], a.ap[:, 0:n], AF.Silu), [a], [o])
                self.dma(XBCT[j * 128:(j + 1) * 128, t:t + n], o.ap[:, 0:n], r=[o])
            units.append(dict(kind="fm", halo=2, chunks=[(0, c * 128)], epi=epi_c))
        groups.append(dict(w=[(W, 2048 + g * 512, 512)], units=units))

    def epi_dt(t, bk):
        o = dto.next()
        self.dve(lambda en: en.tensor_tensor(o.ap, bk.ap[:, 0:64], dtb.ap, ALU.add), [bk, dtb], [o])
        self.act(lambda en: en.activation(o.ap, o.ap, AF.Exp), [o], [o])
        self.act(lambda en: en.activation(o.ap, o.ap, AF.Ln, bias=onec.ap[:, 0:1], scale=1.0), [o, onec], [o])
        self.dma(DT[t:t + 128, :], o.ap, r=[o])
    groups.append(dict(w=[(W, 5120, 64)], units=[dict(kind="tm", chunks=[(0, 0, 64)], epi=epi_dt)]))

    def mk_rope(j, dst, gcol):
        def epi(t, n, bks):
            A, Bk = bks
            sq = sqb.next()
            self.act(lambda en: en.activation(sq.ap[:, 0:n], A.ap[:, 0:n], AF.Square), [A], [sq])
            bs = self.nbank(1)[0]
            self.pe(lambda en: en.matmul(bs.ap[:, 0:n], onesb, sq.ap[:, 0:n], start=True, stop=True), [sq, self.cb], [bs])
            rs = rsr.next()
            self.act(lambda en: en.activation(rs.ap[:, 0:n], bs.ap[:, 0:n], AF.Sqrt, bias=self.epsc.ap[:, 0:1], scale=1.0 / 128), [bs, self.epsc], [rs])
            self.dve(lambda en: en.reciprocal(rs.ap[:, 0:n], rs.ap[:, 0:n]), [rs], [rs])
            c, s = rC.next(), rS.next()
            self.dma(c.ap[:, 0:n], self.ropeC[:, t:t + n], w=[c])
            self.dma(s.ap[:, 0:n], self.ropeS[:, t:t + n], w=[s])
            t1, t2 = tf.next(), tf2.next()
            self.dve(lambda en: en.scalar_tensor_tensor(t1.ap[:, 0:n], A.ap[:, 0:n], self.sm(f"qk{e}", gcol), c.ap[:, 0:n], ALU.mult, ALU.mult), [A, c, self.small], [t1])
            self.dve(lambda en: en.scalar_tensor_tensor(t2.ap[:, 0:n], Bk.ap[:, 0:n], self.sm(f"qk{e}", gcol + 1), s.ap[:, 0:n], ALU.mult, ALU.mult), [Bk, s, self.small], [t2])
            self.dve(lambda en: en.tensor_tensor(t1.ap[:, 0:n], t1.ap[:, 0:n], t2.ap[:, 0:n], ALU.add), [t1, t2], [t1])
            o = ob.next()
            self.dve(lambda en: en.tensor_tensor(o.ap[:, 0:n], t1.ap[:, 0:n], rs.ap[:, 0:n], ALU.mult), [t1, rs], [o])
            self.dma(dst[j * 128:(j + 1) * 128, t:t + n], o.ap[:, 0:n], r=[o])
        return epi
    for g in range(5):
        units = []
        for c in range(4):
            if g < 4:
                ep = mk_rope(g * 4 + c, QT, 0)
            else:
                ep = mk_rope(c, KT, 2)
            units.append(dict(kind="fm", chunks=[(0, c * 128), (1, c * 128)], epi=ep))
        groups.append(dict(w=[(QKP, g * 512, 512), (QKS, g * 512, 512)], units=units))

    def epi_v(t, bk):
        o = ob.next()
        self.act(lambda en: en.activation(o.ap, bk.ap, AF.Copy), [bk], [o])
        self.dma(V[t:t + 128, :], o.ap, r=[o])
    groups.append(dict(w=[(W, 7744, 512)], units=[dict(kind="tm", chunks=[(0, 0, 512)], epi=epi_v)]))
    self.gemm_in(f"g_mix{i}", 2, groups)
    self.attention(QT, KT, V, YT)
    self.ssd(e, XBCT, ZS, DT, YF, YT)
    self.phase()
    self.gemm_out(YT, 32, self.Wb["mix_w_out"][e], 1024)


def _attention(self, QT, KT, V, YT):
    self.phase()
    onesb = self.cb.ap[:, 128:256]
    kT = self.tile(4 * L, BF16)
    k3 = kT.ap.rearrange("p (h t) -> p h t", t=L)
    vv = self.tile(64 * 512, BF16)
    v3 = vv.ap.rearrange("p (c d) -> p c d", d=512)
    for h in range(4):
        self.dma(k3[:, h, :], KT[h * 128:(h + 1) * 128, :], w=[kT])
    V3 = V.rearrange("(c p) d -> p c d", p=128)
    for c0 in range(0, 64, 16):
        self.dma(v3[:, c0:c0 + 16, :], V3[:, c0:c0 + 16, :], w=[vv])
    qr = self.ring(3, 512, BF16)
    pr = self.ring(4, 512, BF16)
    rr = self.ring(2, 512)
    orr = self.ring(2, 512, BF16)
    Ob, Db, Sb = self.banks[0:2], self.banks[2:4], self.banks[4:8]
    sc = 128.0 ** -0.5
    it = 0
    for tt in range(L // 512):
        for h in range(16):
            kvh = h // 4
            q = qr.next()
            self.dma(q.ap, QT[h * 128:(h + 1) * 128, tt * 512:(tt + 1) * 512], w=[q])
            O, Dn = Ob[it % 2], Db[it % 2]
            it += 1
            pend = None
            for kc in range(65):
                if kc < 64:
                    S = Sb[kc % 4]
                    self.pe(lambda en, S=S, kc=kc, kvh=kvh, q=q: en.matmul(S.ap, k3[:, kvh, kc * 128:(kc + 1) * 128], q.ap, start=True, stop=True), [kT, q], [S])
                    P = pr.next()
                    self.act(lambda en, S=S, P=P: en.activation(P.ap, S.ap, AF.Exp, scale=sc), [S], [P])
                if pend is not None:
                    pk, pP = pend
                    self.pe(lambda en, pk=pk, pP=pP, O=O, kvh=kvh: en.matmul(O.ap, v3[:, pk, kvh * 128:(kvh + 1) * 128], pP.ap, start=(pk == 0), stop=(pk == 63)), [vv, pP], [O])
                    self.pe(lambda en, pk=pk, pP=pP, Dn=Dn: en.matmul(Dn.ap, onesb, pP.ap, start=(pk == 0), stop=(pk == 63)), [self.cb, pP], [Dn])
                pend = (kc, P) if kc < 64 else None
            r = rr.next()
            self.dve(lambda en, r=r, Dn=Dn: en.reciprocal(r.ap, Dn.ap), [Dn], [r])
            o = orr.next()
            self.dve(lambda en, o=o, O=O, r=r: en.tensor_tensor(o.ap, O.ap, r.ap, ALU.mult), [O, r], [o])
            self.dma(YT[2048 + h * 128:2048 + (h + 1) * 128, tt * 512:(tt + 1) * 512], o.ap, r=[o])


Builder.mixer_even = _mixer_even
Builder.attention = _attention


def _ssd(self, e, XBCT, ZS, DT, YF, YT):
    self.phase()
    cb = self.cb.ap
    identb, onesb = cb[:, 0:128], cb[:, 128:256]
    trib = [cb[:, 256:384], cb[:, 384:512]]
    maskf = [self.cst("triF"), self.cst("triB")]
    bn = self.banks
    arow = self.tile(64)
    self.dma(arow.ap, self.rows_d[:, RW[f"alog{e}"]:RW[f"alog{e}"] + 64], w=[arow])
    self.act(lambda en: en.activation(arow.ap, arow.ap, AF.Exp), [arow], [arow])
    self.dve(lambda en: en.tensor_scalar(arow.ap, arow.ap, -1.0, None, ALU.mult), [arow], [arow])
    dsk = self.tile(32)
    self.dma(dsk.ap, self.rows_d[:, RW[f"dsk{e}"]:RW[f"dsk{e}"] + 32], w=[dsk])
    ssdn = self.tile(D)
    self.dma(ssdn.ap, self.rows_d[:, RW[f"ssdn{e}"]:RW[f"ssdn{e}"] + D], w=[ssdn])
    sel2 = self.tile(4096, BF16)
    self.dma(sel2.ap[0:64, :], self.sel2_d, w=[sel2], q="pool")
    ST = self.tile(D)
    STb = self.tile(D, BF16)
    ST3 = ST.ap.rearrange("p (h q) -> p h q", q=64)
    xbr = self.ring(2, 24 * 512, BF16)
    dtr = self.ring(2, 4 * 64)
    R2 = lambda n, dt=F32: self.ring(2, n, dt)
    lar, lahlr, cstr, csr, totr, decr, wr, ecsr = R2(32), R2(64, BF16), R2(128), R2(32), R2(32), R2(32), R2(32), R2(32)
    cshlr, csTr = R2(64, BF16), R2(128, BF16)
    xdtr, xwr, btr, cbmr = R2(D, BF16), R2(D, BF16), R2(512, BF16), R2(512)
    difr, wtr, ychr = R2(1024), R2(1024, BF16), R2(D)
    yskr, yfr, zsr, ynr, ynTr, s1r = self.ring(1, D), self.ring(1, D), self.ring(1, D, BF16), self.ring(1, D, BF16), self.ring(1, 16 * 512, BF16), R2(8)
    XB3 = XBCT.rearrange("(k p) t -> p k t", p=128)
    YT3 = YT.rearrange("(k p) t -> p k t", p=128)
    bc = lambda ap, shape: ap.to_broadcast(shape)
    for dirn in range(2):
        self.dve(lambda en: en.memset(ST.ap, 0.0), [], [ST])
        self.dve(lambda en: en.memset(STb.ap, 0.0), [], [STb])
        order = list(range(64)) if dirn == 0 else list(range(63, -1, -1))
        xb = dts = ynT = None
        for ci, c in enumerate(order):
            t0 = c * 128
            o = (c % 4) * 128
            if ci % 4 == 0:
                s0 = (c // 4) * 512
                xb = xbr.next()
                xb3 = xb.ap.rearrange("p (k t) -> p k t", t=512)
                self.dma(xb3[:, 0:12, :], XB3[:, 0:12, s0:s0 + 512], w=[xb])
                self.dma(xb3[:, 12:24, :], XB3[:, 12:24, s0:s0 + 512], w=[xb])
                dts = dtr.next()
                dts3 = dts.ap.rearrange("p (c h) -> p c h", h=64)
                self.dma(dts3, DT[s0:s0 + 512, :].rearrange("(c p) h -> p c h", p=128), w=[dts])
                if dirn == 1:
                    ynT = ynTr.next()
                    ynT3 = ynT.ap.rearrange("p (k t) -> p k t", t=512)
            dtc = dts3[:, c % 4, dirn * 32:(dirn + 1) * 32]
            la, lahl, cst, cs, tot, dec, w, ecs = lar.next(), lahlr.next(), cstr.next(), csr.next(), totr.next(), decr.next(), wr.next(), ecsr.next()
            cshl, csT = cshlr.next(), csTr.next()
            self.dve(lambda en, la=la, dtc=dtc: en.tensor_tensor(la.ap, dtc, arow.ap[:, dirn * 32:(dirn + 1) * 32], ALU.mult), [dts, arow], [la])
            self.dve(lambda en, la=la, lahl=lahl: en.tensor_copy(lahl.ap[:, 0:32], la.ap), [la], [lahl])
            self.dve(lambda en, la=la, lahl=lahl: en.tensor_tensor(lahl.ap[:, 32:64], la.ap, lahl.ap[:, 0:32], ALU.subtract), [la, lahl], [lahl])
            self.pe(lambda en, lahl=lahl: en.matmul(bn[0].ap[:, 0:64], trib[dirn], lahl.ap, start=True, stop=True), [lahl, self.cb], [bn[0]])
            self.pe(lambda en, lahl=lahl: en.matmul(bn[0].ap[:, 64:128], onesb, lahl.ap, start=True, stop=True), [lahl, self.cb], [bn[0]])
            self.act(lambda en, cst=cst: en.activation(cst.ap, bn[0].ap[:, 0:128], AF.Copy), [bn[0]], [cst])
            self.dve(lambda en, cst=cst, cs=cs: en.tensor_tensor(cs.ap, cst.ap[:, 0:32], cst.ap[:, 32:64], ALU.add), [cst], [cs])
            self.dve(lambda en, cst=cst, tot=tot: en.tensor_tensor(tot.ap, cst.ap[:, 64:96], cst.ap[:, 96:128], ALU.add), [cst], [tot])
            self.act(lambda en, dec=dec, tot=tot: en.activation(dec.ap, tot.ap, AF.Exp), [tot], [dec])
            self.dve(lambda en, w=w, tot=tot, cs=cs: en.tensor_tensor(w.ap, tot.ap, cs.ap, ALU.subtract), [tot, cs], [w])
            self.act(lambda en, w=w: en.activation(w.ap, w.ap, AF.Exp), [w], [w])
            self.act(lambda en, ecs=ecs, cs=cs: en.activation(ecs.ap, cs.ap, AF.Exp), [cs], [ecs])
            self.dve(lambda en, cshl=cshl, cs=cs: en.tensor_copy(cshl.ap[:, 0:32], cs.ap), [cs], [cshl])
            self.dve(lambda en, cshl=cshl, cs=cs: en.tensor_tensor(cshl.ap[:, 32:64], cs.ap, cshl.ap[:, 0:32], ALU.subtract), [cs, cshl], [cshl])
            b1b = bn[1].ap.bitcast(BF16)
            self.pe(lambda en, cshl=cshl: en.transpose(b1b[0:64, 0:128], cshl.ap, identb), [cshl, self.cb], [bn[1]])
            self.act(lambda en, csT=csT: en.activation(csT.ap[0:64, :], b1b[0:64, 0:128], AF.Copy), [bn[1]], [csT])
            xdt, xw, bt, cbm = xdtr.next(), xwr.next(), btr.next(), cbmr.next()
            xdt3 = xdt.ap.rearrange("p (h q) -> p h q", q=64)
            for hb in range(2):
                bb = bn[2 + hb].ap.bitcast(BF16)
                for j in range(8):
                    self.pe(lambda en, bb=bb, j=j, hb=hb, xb3=xb3: en.transpose(bb[:, j * 128:(j + 1) * 128], xb3[:, hb * 8 + j, o:o + 128], identb), [xb, self.cb], [bn[2 + hb]])
                bv = bb.rearrange("p (h q) -> p h q", q=64)
                if dirn == 1:
                    ysk = yskr.next() if hb == 0 else ysk
                    ysk3 = ysk.ap.rearrange("p (h q) -> p h q", q=64)
                    self.dve(lambda en, bv=bv, hb=hb, ysk3=ysk3: en.tensor_tensor(ysk3[:, hb * 16:(hb + 1) * 16, :], bv, bc(dsk.ap[:, hb * 16:(hb + 1) * 16].unsqueeze(2), [128, 16, 64]), ALU.mult), [bn[2 + hb], dsk], [ysk])
                self.dve(lambda en, bv=bv, hb=hb, xdt3=xdt3, dtc=dtc: en.tensor_tensor(xdt3[:, hb * 16:(hb + 1) * 16, :], bv, bc(dtc[:, hb * 16:(hb + 1) * 16].unsqueeze(2), [128, 16, 64]), ALU.mult), [bn[2 + hb], dts], [xdt])
            xw3 = xw.ap.rearrange("p (h q) -> p h q", q=64)
            self.dve(lambda en, xw3=xw3, xdt3=xdt3, w=w: en.tensor_tensor(xw3, xdt3, bc(w.ap.unsqueeze(2), [128, 32, 64]), ALU.mult), [xdt, w], [xw])
            for g in range(4):
                self.pe(lambda en, g=g, xb3=xb3: en.transpose(b1b[:, 512 + g * 128:512 + (g + 1) * 128], xb3[:, 16 + g, o:o + 128], identb), [xb, self.cb], [bn[1]])
            self.act(lambda en, bt=bt: en.activation(bt.ap, b1b[:, 512:1024], AF.Copy), [bn[1]], [bt])
            for g in range(4):
                self.pe(lambda en, g=g, xb3=xb3: en.matmul(bn[4].ap[:, g * 128:(g + 1) * 128], xb3[:, 16 + g, o:o + 128], xb3[:, 20 + g, o:o + 128], start=True, stop=True), [xb], [bn[4]])
            cbm3 = cbm.ap.rearrange("p (g l) -> p g l", l=128)
            self.dve(lambda en, cbm3=cbm3: en.tensor_tensor(cbm3, bn[4].ap.rearrange("p (g l) -> p g l", l=128), bc(maskf[dirn].unsqueeze(1), [128, 4, 128]), ALU.mult), [bn[4], self.consts], [cbm])
            ych = ychr.next()
            ych3 = ych.ap.rearrange("p (h q) -> p h q", q=64)
            for g in range(4):
                for hh in range(8):
                    bk = bn[5 + hh // 4]
                    self.pe(lambda en, bk=bk, hh=hh, g=g, csT=csT: en.matmul(bk.ap[:, (hh % 4) * 128:(hh % 4 + 1) * 128], sel2.ap[0:64, (8 * g + hh) * 128:(8 * g + hh + 1) * 128], csT.ap[0:64, :], start=True, stop=True), [sel2, csT], [bk])
                dif, wt = difr.next(), wtr.next()
                dif3 = dif.ap.rearrange("p (h l) -> p h l", l=128)
                for hb in range(2):
                    self.dve(lambda en, hb=hb, g=g, dif3=dif3, cs=cs: en.tensor_tensor(dif3[:, hb * 4:(hb + 1) * 4, :], bn[5 + hb].ap.rearrange("p (h l) -> p h l", l=128), bc(cs.ap[:, 8 * g + hb * 4:8 * g + hb * 4 + 4].unsqueeze(2), [128, 4, 128]), ALU.subtract), [bn[5 + hb], cs], [dif])
                self.act(lambda en, dif=dif: en.activation(dif.ap, dif.ap, AF.Exp), [dif], [dif])
                wt3 = wt.ap.rearrange("p (h l) -> p h l", l=128)
                self.dve(lambda en, wt3=wt3, dif3=dif3, cbm3=cbm3, g=g: en.scalar_tensor_tensor(wt3, dif3, 1.0, bc(cbm3[:, g:g + 1, :], [128, 8, 128]), ALU.min, ALU.mult), [dif, cbm], [wt])
                for hh in range(8):
                    self.pe(lambda en, hh=hh, g=g, wt3=wt3, xdt=xdt: en.matmul(bn[7].ap[:, hh * 64:(hh + 1) * 64], wt3[:, hh, :], xdt.ap[:, (8 * g + hh) * 64:(8 * g + hh + 1) * 64], start=True, stop=True), [wt, xdt], [bn[7]])
                self.pe(lambda en, g=g, xb3=xb3: en.matmul(bn[2].ap, xb3[:, 20 + g, o:o + 128], STb.ap[:, g * 512:(g + 1) * 512], start=True, stop=True), [xb, STb], [bn[2]])
                self.dve(lambda en, g=g, ych3=ych3, ecs=ecs: en.tensor_tensor(ych3[:, 8 * g:8 * g + 8, :], bn[2].ap.rearrange("p (h q) -> p h q", q=64), bc(ecs.ap[:, 8 * g:8 * g + 8].unsqueeze(2), [128, 8, 64]), ALU.mult), [bn[2], ecs], [ych])
                self.dve(lambda en, g=g, ych=ych: en.tensor_tensor(ych.ap[:, g * 512:(g + 1) * 512], ych.ap[:, g * 512:(g + 1) * 512], bn[7].ap, ALU.add), [bn[7], ych], [ych])
                self.pe(lambda en, g=g, bt=bt, xw=xw: en.matmul(bn[3].ap, bt.ap[:, g * 128:(g + 1) * 128], xw.ap[:, g * 512:(g + 1) * 512], start=True, stop=True), [bt, xw], [bn[3]])
                self.dve(lambda en, g=g, dec=dec: en.tensor_tensor(ST3[:, 8 * g:8 * g + 8, :], ST3[:, 8 * g:8 * g + 8, :], bc(dec.ap[:, 8 * g:8 * g + 8].unsqueeze(2), [128, 8, 64]), ALU.mult), [ST, dec], [ST])
                self.dve(lambda en, g=g: en.tensor_tensor(ST.ap[:, g * 512:(g + 1) * 512], ST.ap[:, g * 512:(g + 1) * 512], bn[3].ap, ALU.add), [ST, bn[3]], [ST])
                self.act(lambda en, g=g: en.activation(STb.ap[:, g * 512:(g + 1) * 512], ST.ap[:, g * 512:(g + 1) * 512], AF.Copy), [ST], [STb])
            if dirn == 0:
                self.dma(YF[t0:t0 + 128, :], ych.ap, r=[ych])
            else:
                yf, zs, yn, s1 = yfr.next(), zsr.next(), ynr.next(), s1r.next()
                self.dma(yf.ap, YF[t0:t0 + 128, :], w=[yf])
                self.dma(zs.ap, ZS[t0:t0 + 128, :], w=[zs])
                self.dve(lambda en, ych=ych, yf=yf: en.tensor_tensor(ych.ap, ych.ap, yf.ap, ALU.add), [ych, yf], [ych])
                self.dve(lambda en, ych=ych, ysk=ysk: en.tensor_tensor(ych.ap, ych.ap, ysk.ap, ALU.add), [ych, ysk], [ych])
                self.dve(lambda en, ych=ych, zs=zs: en.tensor_tensor(ych.ap, ych.ap, zs.ap, ALU.mult), [ych, zs], [ych])
                self.dve(lambda en, s1=s1: en.memset(s1.ap, 0.0), [], [s1])
                self.act(lambda en, ych=ych, yf=yf, s1=s1: en.activation(yf.ap, ych.ap, AF.Square, accum_out=s1.ap[:, 0:1]), [ych, s1], [yf, s1])
                self.act(lambda en, s1=s1: en.activation(s1.ap[:, 1:2], s1.ap[:, 0:1], AF.Sqrt, bias=self.epsc.ap[:, 0:1], scale=1.0 / D), [s1, self.epsc], [s1])
                self.dve(lambda en, s1=s1: en.reciprocal(s1.ap[:, 2:3], s1.ap[:, 1:2]), [s1], [s1])
                self.dve(lambda en, yn=yn, ych=ych, s1=s1: en.scalar_tensor_tensor(yn.ap, ych.ap, s1.ap[:, 2:3], ssdn.ap, ALU.mult, ALU.mult), [ych, s1, ssdn], [yn])
                for hb in range(2):
                    bb = bn[5 + hb].ap.bitcast(BF16)
                    for j in range(8):
                        self.pe(lambda en, bb=bb, j=j, hb=hb, yn=yn: en.transpose(bb[:, j * 128:(j + 1) * 128], yn.ap[:, (hb * 8 + j) * 128:(hb * 8 + j + 1) * 128], identb), [yn, self.cb], [bn[5 + hb]])
                    f = self.act if hb == 0 else self.dve
                    if hb == 0:
                        self.act(lambda en, bb=bb, ynT3=ynT3: en.activation(ynT3[:, 0:8, o:o + 128], bb.rearrange("p (k t) -> p k t", t=128), AF.Copy), [bn[5]], [ynT])
                    else:
                        self.dve(lambda en, bb=bb, ynT3=ynT3: en.tensor_copy(ynT3[:, 8:16, o:o + 128], bb.rearrange("p (k t) -> p k t", t=128)), [bn[6]], [ynT])
                if c % 4 == 0:
                    s0 = (c // 4) * 512
                    self.dma(YT3[:, 0:16, s0:s0 + 512], ynT3, r=[ynT])


Builder.ssd = _ssd


def _fft_stage1(self, src, T1, S1):
    self.phase()
    cb = self.cb.ap
    cosb, nsinb = cb[:, 512:640], cb[:, 768:896]
    xin = self.ring(2, D, BF16)
    outb = self.ring(2, 2 * D, BF16)
    tmp = self.ring(4, 512)
    src3 = src.rearrange("(a b) c -> b a c", b=128)
    C16, S16, NS16 = self.cst("C16"), self.cst("S16"), self.cst("NS16")
    for t2 in range(128):
        xi = xin.next()
        self.dma(xi.ap[0:T1, :], src3[t2, 0:T1, :], w=[xi])
        ob = outb.next()
        ob3 = ob.ap.rearrange("p (r c) -> p r c", c=D)
        for ct in range(4):
            cs_ = slice(ct * 512, (ct + 1) * 512)
            bre, bim = self.nbank(2)
            self.pe(lambda en, bre=bre, xi=xi, cs_=cs_: en.matmul(bre.ap, cosb[0:T1, :], xi.ap[0:T1, cs_], start=True, stop=True), [xi, self.cb], [bre])
            self.pe(lambda en, bim=bim, xi=xi, cs_=cs_: en.matmul(bim.ap, nsinb[0:T1, :], xi.ap[0:T1, cs_], start=True, stop=True), [xi, self.cb], [bim])
            a, b = tmp.next(), tmp.next()
            self.act(lambda en, a=a, bre=bre, t2=t2: en.activation(a.ap, bre.ap, AF.Identity, bias=self.zc.ap[:, 0:1], scale=C16[:, t2:t2 + 1]), [bre, self.consts], [a])
            self.dve(lambda en, a=a, bim=bim, t2=t2, ob3=ob3, cs_=cs_: en.scalar_tensor_tensor(ob3[:, 0, cs_], bim.ap, S16[:, t2:t2 + 1], a.ap, ALU.mult, ALU.add), [bim, a, self.consts], [ob])
            self.act(lambda en, b=b, bre=bre, t2=t2: en.activation(b.ap, bre.ap, AF.Identity, bias=self.zc.ap[:, 0:1], scale=NS16[:, t2:t2 + 1]), [bre, self.consts], [b])
            self.dve(lambda en, b=b, bim=bim, t2=t2, ob3=ob3, cs_=cs_: en.scalar_tensor_tensor(ob3[:, 1, cs_], bim.ap, C16[:, t2:t2 + 1], b.ap, ALU.mult, ALU.add), [bim, b, self.consts], [ob])
        self.dma(S1[:, t2, :, :], ob3, r=[ob])


def _stage2_mm(self, b, b3, cs_):
    cb = self.cb.ap
    cosb, sinb, nsinb = cb[:, 512:640], cb[:, 640:768], cb[:, 768:896]
    xre, xim = self.nbank(2)
    self.pe(lambda en: en.matmul(xre.ap, cosb, b3[:, 0, cs_], start=True, stop=False), [b, self.cb], [xre])
    self.pe(lambda en: en.matmul(xre.ap, sinb, b3[:, 1, cs_], start=False, stop=True), [b, self.cb], [xre])
    self.pe(lambda en: en.matmul(xim.ap, cosb, b3[:, 1, cs_], start=True, stop=False), [b, self.cb], [xim])
    self.pe(lambda en: en.matmul(xim.ap, nsinb, b3[:, 0, cs_], start=False, stop=True), [b, self.cb], [xim])
    return xre, xim


def _mixer_odd(self, o, i):
    self.phase()
    nm = lambda n, s, dt: self.dram.get(n) or self.scr(n, s, dt)
    ZT, X0T, Z = nm("ZT", (D, L), BF16), nm("X0T", (D, L), BF16), nm("Z", (L, D), BF16)
    KERN = nm("KERN", (NFFT, D), BF16)
    S1, S2, KF = nm("S1", (128, 128, 2, D), BF16), nm("S2", (128, 128, 2, D), BF16), nm("KF", (128, 128, 2, D), BF16)
    Y, GT = nm("Y", (L, D), BF16), nm("GT", (D, L), BF16)
    W = self.Wb["hy_w_in"][o]
    cb = self.cb.ap
    identb, onesb, cosb, sinb, nsinb = cb[:, 0:128], cb[:, 128:256], cb[:, 512:640], cb[:, 640:768], cb[:, 768:896]
    C16, S16, NS16 = self.cst("C16"), self.cst("S16"), self.cst("NS16")
    tf, tf2, ob = self.ring(2, 512), self.ring(2, 512), self.ring(3, 512, BF16)
    groups = []
    for g in range(4):
        units = []
        for c in range(4):
            j = g * 4 + c

            def epi0(t, n, bks, j=j):
                a = tf.next()
                self.conv_epi(bks[0], n, 3, f"hyc{o}", j, 0, a)
                ot = ob.next()
                self.act(lambda en: en.activation(ot.ap[:, 0:n], a.ap[:, 0:n], AF.Copy), [a], [ot])
                self.dma(X0T[j * 128:(j + 1) * 128, t:t + n], ot.ap[:, 0:n], r=[ot])
            units.append(dict(kind="fm", halo=1, chunks=[(0, c * 128)], epi=epi0))
        groups.append(dict(w=[(W, g * 512, 512)], units=units))
    for g in range(4):
        units = []
        for c in range(4):
            j = g * 4 + c

            def epi1(t, n, bks, j=j):
                a, b = tf.next(), tf2.next()
                self.conv_epi(bks[0], n, 3, f"hyc{o}", 16 + j, 0, a)
                self.conv_epi(bks[1], n, 3, f"hyc{o}", 32 + j, 0, b)
                ot = ob.next()
                self.dve(lambda en: en.tensor_tensor(ot.ap[:, 0:n], a.ap[:, 0:n], b.ap[:, 0:n], ALU.mult), [a, b], [ot])
                self.dma(ZT[j * 128:(j + 1) * 128, t:t + n], ot.ap[:, 0:n], r=[ot])
            units.append(dict(kind="fm", halo=1, chunks=[(0, c * 128), (1, c * 128)], epi=epi1))
        groups.append(dict(w=[(W, 2048 + g * 512, 512), (W, 4096 + g * 512, 512)], units=units))
    self.gemm_in(f"g_mix{i}", 1, groups)
    self.phase()
    zin, zo = self.ring(2, 16 * 512, BF16), self.ring(2, D, BF16)
    ZT3 = ZT.rearrange("(k p) t -> p k t", p=128)
    for t4 in range(16):
        zi = zin.next()
        zi3 = zi.ap.rearrange("p (k t) -> p k t", t=512)
        self.dma(zi3, ZT3[:, :, t4 * 512:(t4 + 1) * 512], w=[zi])
        for tt in range(4):
            zt = zo.next()
            for hb in range(2):
                bk = self.nbank(1)[0]
                bb = bk.ap.bitcast(BF16)
                for j in range(8):
                    self.pe(lambda en, bb=bb, j=j, hb=hb, tt=tt, zi3=zi3: en.transpose(bb[:, j * 128:(j + 1) * 128], zi3[:, hb * 8 + j, tt * 128:(tt + 1) * 128], identb), [zi, self.cb], [bk])
                if hb == 0:
                    self.act(lambda en, bb=bb, zt=zt: en.activation(zt.ap[:, 0:1024], bb, AF.Copy), [bk], [zt])
                else:
                    self.dve(lambda en, bb=bb, zt=zt: en.tensor_copy(zt.ap[:, 1024:2048], bb), [bk], [zt])
            t0 = t4 * 512 + tt * 128
            self.dma(Z[t0:t0 + 128, :], zt.ap, r=[zt])
    self.phase()
    bn = self.banks
    w1, w2, w3, wo = self.tile(64, BF16), self.tile(64, BF16), self.tile(64, BF16), self.tile(2 * D, BF16)
    self.dma(w1.ap[0:33, :], self.Wb["hy_f_w1"][o], w=[w1])
    self.dma(w2.ap[0:64, :], self.Wb["hy_f_w2"][o], w=[w2])
    self.dma(w3.ap[0:64, :], self.Wb["hy_f_w3"][o], w=[w3])
    self.dma(wo.ap[0:64, :], self.Wb["hy_f_w_out"][o], w=[wo])
    delta = self.tile(D)
    self.dma(delta.ap, self.rows_d[:, RW["delta"]:RW["delta"] + D], w=[delta])
    fb = self.tile(4)
    for k in range(3):
        self.dve(lambda en, k=k: en.tensor_tensor(fb.ap[0:64, k:k + 1], self.sm(f"hyf{o}", k)[0:64, :], self.sm(f"hyf{o}", 3)[0:64, :], ALU.mult), [self.small], [fb])
    ntc = self.tile(128)
    self.dve(lambda en: en.tensor_scalar(ntc.ap, self.consts.ap[:, CS["tcol"]:CS["tcol"] + 128], -1.0, None, ALU.mult), [self.consts], [ntc])
    zfr, zbr = self.ring(2, 512), self.ring(2, 512, BF16)
    ur, rir, rfr, gr = self.ring(2, 512), self.ring(2, 512, mybir.dt.int32), self.ring(2, 512), self.ring(2, 512)
    hr = self.ring(4, 512, BF16)
    wnr, kr, abr = self.ring(2, 512), self.ring(2, D, BF16), self.ring(3, 512, BF16)
    I2P = 1.0 / (2.0 * math.pi)

    def sinlayer(wt, K, src, k):
        self.pe(lambda en: en.matmul(bn[0].ap[0:64, :], wt.ap[0:K, 0:64], src.ap[0:K, :], start=True, stop=True), [wt, src], [bn[0]])
        u, ri, rf, g_ = ur.next(), rir.next(), rfr.next(), gr.next()
        U, RI, RF, G = u.ap[0:64, :], ri.ap[0:64, :], rf.ap[0:64, :], g_.ap[0:64, :]
        self.act(lambda en: en.activation(U, bn[0].ap[0:64, :], AF.Identity, bias=fb.ap[0:64, k:k + 1], scale=self.sm(f"hyf{o}", 3)[0:64, :]), [bn[0], fb, self.small], [u])
        self.dve(lambda en: en.tensor_scalar(U, U, I2P, None, ALU.mult), [u], [u])
        self.dve(lambda en: en.tensor_copy(RI, U), [u], [ri])
        self.dve(lambda en: en.tensor_copy(RF, RI), [ri], [rf])
        self.dve(lambda en: en.tensor_tensor(U, U, RF, ALU.subtract), [u, rf], [u])
        self.dve(lambda en: en.tensor_scalar(G, U, 0.5, None, ALU.is_gt), [u], [g_])
        self.dve(lambda en: en.tensor_tensor(U, U, G, ALU.subtract), [u, g_], [u])
        self.dve(lambda en: en.tensor_scalar(G, U, -0.5, None, ALU.is_lt), [u], [g_])
        self.dve(lambda en: en.tensor_tensor(U, U, G, ALU.add), [u, g_], [u])
        h = hr.next()
        self.act(lambda en: en.activation(h.ap[0:64, :], U, AF.Sin, scale=6.28318), [u], [h])
        return h
    nacc = 0
    for half in range(2):
        zsrc = self.zfT if half == 0 else self.zfTr
        for tq in range(16):
            zf, zb = zfr.next(), zbr.next()
            self.dma(zf.ap[0:33, :], zsrc[:, tq * 512:(tq + 1) * 512], w=[zf])
            self.dve(lambda en, zf=zf, zb=zb: en.tensor_copy(zb.ap[0:33, :], zf.ap[0:33, :]), [zf], [zb])
            h = sinlayer(w1, 33, zb, 0)
            h = sinlayer(w2, 64, h, 1)
            h = sinlayer(w3, 64, h, 2)
            for cc in range(4):
                lc = tq * 4 + cc
                krow = kr.next()
                for ct in range(4):
                    bk = bn[1 + (nacc % 3)]
                    self.pe(lambda en, bk=bk, h=h, cc=cc, ct=ct: en.matmul(bk.ap, h.ap[0:64, cc * 128:(cc + 1) * 128], wo.ap[0:64, half * D + ct * 512:half * D + (ct + 1) * 512], start=True, stop=True), [h, wo], [bk])
                    wn = wnr.next()
                    self.act(lambda en, wn=wn, ct=ct, lc=lc: en.activation(wn.ap, delta.ap[:, ct * 512:(ct + 1) * 512], AF.Exp, scale=ntc.ap[:, half * 64 + lc:half * 64 + lc + 1]), [delta, ntc], [wn])
                    self.dve(lambda en, wn=wn, bk=bk, krow=krow, ct=ct: en.tensor_tensor(krow.ap[:, ct * 512:(ct + 1) * 512], bk.ap, wn.ap, ALU.mult), [bk, wn], [krow])
                    ab = abr.next()
                    self.act(lambda en, ab=ab, krow=krow, ct=ct: en.activation(ab.ap, krow.ap[:, ct * 512:(ct + 1) * 512], AF.Abs), [krow], [ab])
                    first, last = (half == 0 and lc == 0), (half == 1 and lc == 63)
                    self.pe(lambda en, ab=ab, ct=ct, first=first, last=last: en.matmul(bn[4 + ct].ap, onesb, ab.ap, start=first, stop=last), [ab, self.cb], [bn[4 + ct]])
                    nacc += 1
                if half == 0:
                    self.dma(KERN[lc * 128:(lc + 1) * 128, :], krow.ap, r=[krow])
                else:
                    nr = 128 if lc < 63 else 127
                    r0 = L + 1 + lc * 128
                    self.dma(KERN[r0:r0 + nr, :], krow.ap[0:nr, :], r=[krow])
    zr = self.tile(D, BF16)
    self.dve(lambda en: en.memset(zr.ap[0:1, :], 0.0), [], [zr])
    self.dma(KERN[L:L + 1, :], zr.ap[0:1, :], r=[zr])
    for ct in range(4):
        self.dve(lambda en, ct=ct: en.reciprocal(self.rinv.ap[:, ct * 512:(ct + 1) * 512], bn[4 + ct].ap), [bn[4 + ct]], [self.rinv])
    self.fft_stage1(KERN, 128, S1)
    self.phase()
    binr, kout = self.ring(2, 2 * D, BF16), self.ring(2, 2 * D, BF16)
    for k1 in range(128):
        b = binr.next()
        b3 = b.ap.rearrange("p (r c) -> p r c", c=D)
        self.dma(b3, S1[k1], w=[b])
        ko = kout.next()
        ko3 = ko.ap.rearrange("p (r c) -> p r c", c=D)
        for ct in range(4):
            cs_ = slice(ct * 512, (ct + 1) * 512)
            xre, xim = self.stage2_mm(b, b3, cs_)
            self.dve(lambda en, xre=xre, ko3=ko3, cs_=cs_: en.tensor_tensor(ko3[:, 0, cs_], xre.ap, self.rinv.ap[:, cs_], ALU.mult), [xre, self.rinv], [ko])
            self.dve(lambda en, xim=xim, ko3=ko3, cs_=cs_: en.tensor_tensor(ko3[:, 1, cs_], xim.ap, self.rinv.ap[:, cs_], ALU.mult), [xim, self.rinv], [ko])
        self.dma(KF[k1], ko3, r=[ko])
    self.fft_stage1(Z, 64, S1)
    self.phase()
    binr, kin, gout = self.ring(2, 2 * D, BF16), self.ring(2, 2 * D, BF16), self.ring(2, 2 * D, BF16)
    tmp = self.ring(6, 512)
    ybr = self.ring(4, 512, BF16)
    for k1 in range(128):
        b, kf, go = binr.next(), kin.next(), gout.next()
        b3 = b.ap.rearrange("p (r c) -> p r c", c=D)
        kf3 = kf.ap.rearrange("p (r c) -> p r c", c=D)
        go3 = go.ap.rearrange("p (r c) -> p r c", c=D)
        self.dma(b3, S1[k1], w=[b])
        self.dma(kf3, KF[k1], w=[kf])
        for ct in range(4):
            cs_ = slice(ct * 512, (ct + 1) * 512)
            xre, xim = self.stage2_mm(b, b3, cs_)
            t1, t2, t3, t4 = tmp.next(), tmp.next(), tmp.next(), tmp.next()
            yre, yim = ybr.next(), ybr.next()
            self.dve(lambda en, t1=t1, xre=xre, kf3=kf3, cs_=cs_: en.tensor_tensor(t1.ap, xre.ap, kf3[:, 0, cs_], ALU.mult), [xre, kf], [t1])
            self.dve(lambda en, t2=t2, xim=xim, kf3=kf3, cs_=cs_: en.tensor_tensor(t2.ap, xim.ap, kf3[:, 1, cs_], ALU.mult), [xim, kf], [t2])
            self.pool(lambda en, t1=t1, t2=t2, yre=yre: en.tensor_tensor(yre.ap, t1.ap, t2.ap, ALU.subtract), [t1, t2], [yre])
            self.dve(lambda en, t3=t3, xre=xre, kf3=kf3, cs_=cs_: en.tensor_tensor(t3.ap, xre.ap, kf3[:, 1, cs_], ALU.mult), [xre, kf], [t3])
            self.dve(lambda en, t4=t4, xim=xim, kf3=kf3, cs_=cs_: en.tensor_tensor(t4.ap, xim.ap, kf3[:, 0, cs_], ALU.mult), [xim, kf], [t4])
            self.pool(lambda en, t3=t3, t4=t4, yim=yim: en.tensor_tensor(yim.ap, t3.ap, t4.ap, ALU.add), [t3, t4], [yim])
            gre, gim = self.nbank(2)
            self.pe(lambda en, gre=gre, yre=yre: en.matmul(gre.ap, cosb, yre.ap, start=True, stop=False), [yre, self.cb], [gre])
            self.pe(lambda en, gre=gre, yim=yim: en.matmul(gre.ap, nsinb, yim.ap, start=False, stop=True), [yim, self.cb], [gre])
            self.pe(lambda en, gim=gim, yre=yre: en.matmul(gim.ap, sinb, yre.ap, start=True, stop=False), [yre, self.cb], [gim])
            self.pe(lambda en, gim=gim, yim=yim: en.matmul(gim.ap, cosb, yim.ap, start=False, stop=True), [yim, self.cb], [gim])
            t5, t6 = tmp.next(), tmp.next()
            self.act(lambda en, t5=t5, gre=gre, k1=k1: en.activation(t5.ap, gre.ap, AF.Identity, bias=self.zc.ap[:, 0:1], scale=C16[:, k1:k1 + 1]), [gre, self.consts], [t5])
            self.dve(lambda en, t5=t5, gim=gim, k1=k1, go3=go3, cs_=cs_: en.scalar_tensor_tensor(go3[:, 0, cs_], gim.ap, NS16[:, k1:k1 + 1], t5.ap, ALU.mult, ALU.add), [gim, t5, self.consts], [go])
            self.act(lambda en, t6=t6, gre=gre, k1=k1: en.activation(t6.ap, gre.ap, AF.Identity, bias=self.zc.ap[:, 0:1], scale=S16[:, k1:k1 + 1]), [gre, self.consts], [t6])
            self.dve(lambda en, t6=t6, gim=gim, k1=k1, go3=go3, cs_=cs_: en.scalar_tensor_tensor(go3[:, 1, cs_], gim.ap, C16[:, k1:k1 + 1], t6.ap, ALU.mult, ALU.add), [gim, t6, self.consts], [go])
        self.dma(S2[:, k1, :, :], go3, r=[go])
    self.phase()
    gin, yo = self.ring(2, 2 * D, BF16), self.ring(2, D, BF16)
    Y3 = Y.rearrange("(a b) c -> b a c", b=128)
    for t2 in range(128):
        g = gin.next()
        g3 = g.ap.rearrange("p (r c) -> p r c", c=D)
        self.dma(g3, S2[t2], w=[g])
        y = yo.next()
        for ct in range(4):
            cs_ = slice(ct * 512, (ct + 1) * 512)
            bk = self.nbank(1)[0]
            self.pe(lambda en, bk=bk, g3=g3, cs_=cs_: en.matmul(bk.ap[0:64, :], cosb[:, 0:64], g3[:, 0, cs_], start=True, stop=False), [g, self.cb], [bk])
            self.pe(lambda en, bk=bk, g3=g3, cs_=cs_: en.matmul(bk.ap[0:64, :], nsinb[:, 0:64], g3[:, 1, cs_], start=False, stop=True), [g, self.cb], [bk])
            self.act(lambda en, bk=bk, y=y, cs_=cs_: en.activation(y.ap[0:64, cs_], bk.ap[0:64, :], AF.Identity, bias=self.zc.ap[0:64, 0:1], scale=1.0 / NFFT), [bk], [y])
        self.dma(Y3[t2, 0:64, :], y.ap[0:64, :], r=[y])
    self.phase()
    yin = self.ring(2, D, BF16)
    ztr, x0r, gr2 = self.ring(1, 16 * 512, BF16), self.ring(1, 16 * 512, BF16), self.ring(2, 16 * 512, BF16)
    tm2 = self.ring(4, 128)
    X03 = X0T.rearrange("(k p) t -> p k t", p=128)
    GT3 = GT.rearrange("(k p) t -> p k t", p=128)
    for t4 in range(16):
        s0 = t4 * 512
        zt, x0, gg = ztr.next(), x0r.next(), gr2.next()
        zt3, x03, gg3 = [a.ap.rearrange("p (k t) -> p k t", t=512) for a in (zt, x0, gg)]
        self.dma(zt3, ZT3[:, :, s0:s0 + 512], w=[zt])
        self.dma(x03, X03[:, :, s0:s0 + 512], w=[x0])
        for tt in range(4):
            yi = yin.next()
            self.dma(yi.ap, Y[s0 + tt * 128:s0 + (tt + 1) * 128, :], w=[yi])
            for hb in range(2):
                bk = self.nbank(1)[0]
                bb = bk.ap.bitcast(BF16)
                for j in range(8):
                    self.pe(lambda en, bb=bb, j=j, hb=hb, yi=yi: en.transpose(bb[:, j * 128:(j + 1) * 128], yi.ap[:, (hb * 8 + j) * 128:(hb * 8 + j + 1) * 128], identb), [yi, self.cb], [bk])
                for j in range(8):
                    k = hb * 8 + j
                    tmq = tm2.next()
                    ts = slice(tt * 128, (tt + 1) * 128)
                    self.dve(lambda en, tmq=tmq, bb=bb, j=j, k=k, ts=ts, zt3=zt3: en.scalar_tensor_tensor(tmq.ap, zt3[:, k, ts], self.sm(f"hysk{o}", k), bb[:, j * 128:(j + 1) * 128], ALU.mult, ALU.add), [zt, bk, self.small], [tmq])
                    self.dve(lambda en, tmq=tmq, k=k, ts=ts, gg3=gg3, x03=x03: en.tensor_tensor(gg3[:, k, ts], tmq.ap, x03[:, k, ts], ALU.mult), [tmq, x0], [gg])
        self.dma(GT3[:, :, s0:s0 + 512], gg3, r=[gg])
    self.phase()
    self.gemm_out(GT, 16, self.Wb["hy_w_out"][o], 2048)


Builder.fft_stage1 = _fft_stage1
Builder.stage2_mm = _stage2_mm
Builder.mixer_odd = _mixer_odd
```

```python
import math
import types
import contextlib
import numpy as np
import concourse.bass as bass
import concourse.mybir as mybir
from concourse.bass_utils import run_bass_kernel_spmd

F32 = mybir.dt.float32
BF16 = mybir.dt.bfloat16
ALU = mybir.AluOpType
AF = mybir.ActivationFunctionType

L = 8192
D = 2048
NL = 4
DFF = 5632
F2 = 2 * DFF
NMEM = 256
EPS = 1e-6
NFFT = 2 * L
SB_BASE = 16512


class Buf:
    __slots__ = ("last_w", "readers")

    def __init__(self):
        self.last_w = None
        self.readers = []


class Op:
    __slots__ = ("eng", "fn", "deps", "signal", "sem", "count", "dma", "idx")

    def __init__(self, eng, fn, dma):
        self.eng = eng
        self.fn = fn
        self.deps = ()
        self.signal = False
        self.sem = None
        self.count = 0
        self.dma = dma


def freeze(fn):
    if fn is None or fn.__closure__ is None:
        return fn
    cells = []
    for c in fn.__closure__:
        try:
            cells.append(types.CellType(c.cell_contents))
        except ValueError:
            cells.append(types.CellType())
    return types.FunctionType(fn.__code__, fn.__globals__, fn.__name__, fn.__defaults__, tuple(cells))


ENGS = ("pe", "act", "dve", "pool", "sp")
NDMASEM = {"sp": 32, "act": 24, "pool": 12}


class Rec:
    def __init__(self):
        self.ops = []
        self.bufs = []
        self.last_eng = {e: None for e in ENGS}
        self.dma_rr = {q: 0 for q in NDMASEM}
        self.dma_last = {q: [None] * n for q, n in NDMASEM.items()}

    def buf(self):
        b = Buf()
        self.bufs.append(b)
        return b

    def _add(self, o, reads, writes, extra=()):
        deps = set(extra)
        for b in reads:
            if b.last_w is not None:
                deps.add(b.last_w)
        for b in writes:
            if b.last_w is not None:
                deps.add(b.last_w)
            deps.update(b.readers)
        for b in reads:
            b.readers.append(o)
        for b in writes:
            b.last_w = o
            b.readers = []
        deps.discard(o)
        if o.eng == "pe" and not o.dma:
            deps = {d for d in deps if d.dma or d.eng != "pe"}
        o.deps = deps
        o.idx = len(self.ops)
        self.ops.append(o)
        return o

    def op(self, eng, fn, reads=(), writes=()):
        o = Op(eng, freeze(fn), False)
        self._add(o, reads, writes)
        self.last_eng[eng] = o
        return o

    def dma(self, q, fn, reads=(), writes=()):
        o = Op(q, fn, True)
        slot = self.dma_rr[q]
        self.dma_rr[q] = (slot + 1) % NDMASEM[q]
        prev = self.dma_last[q][slot]
        o.sem = (q, slot)
        self._add(o, reads, writes, extra=(prev,) if prev is not None else ())
        self.dma_last[q][slot] = o
        return o

    def barrier(self):
        pend = [o for o in self.last_eng.values() if o is not None]
        for q in NDMASEM:
            pend += [o for o in self.dma_last[q] if o is not None]
        for e in ENGS:
            o = Op(e, None, False)
            o.deps = set(pend)
            o.idx = len(self.ops)
            self.ops.append(o)
        for b in self.bufs:
            b.last_w = None
            b.readers = []
        self.bufs = []

    def emit(self, nc, final_wait_ops=()):
        for o in self.ops:
            for d in o.deps:
                if not d.dma:
                    d.signal = True
        fin = Op("sp", None, False)
        fin.deps = set(final_wait_ops)
        fin.idx = len(self.ops)
        for d in fin.deps:
            if not d.dma:
                d.signal = True
        self.ops.append(fin)
        cnt = {e: 0 for e in ENGS}
        dcnt = {q: [0] * n for q, n in NDMASEM.items()}
        for o in self.ops:
            if o.dma:
                q, s = o.sem
                dcnt[q][s] += 16
                o.count = dcnt[q][s]
            elif o.signal:
                cnt[o.eng] += 1
                o.count = cnt[o.eng]
        with contextlib.ExitStack() as st:
            csem = {e: st.enter_context(nc.semaphore(f"c_{e}")) for e in ("pe", "act", "dve", "pool")}
            dsem = {q: [st.enter_context(nc.semaphore(f"d_{q}{i}")) for i in range(n)] for q, n in NDMASEM.items()}
            segs, cur = [], []
            for o in self.ops:
                if o.fn is None and o.eng == "pe" and cur:
                    segs.append(cur)
                    cur = []
                cur.append(o)
            segs.append(cur)
            seen = {e: {} for e in ENGS}

            def run(e, eng, ops):
                sn = seen[e]
                for o in ops:
                    if o.eng != e:
                        continue
                    for d in sorted(o.deps, key=lambda d: d.idx):
                        if d.dma:
                            sem = dsem[d.sem[0]][d.sem[1]]
                            key = ("d",) + d.sem
                        else:
                            sem = csem[d.eng]
                            key = d.eng
                        if sn.get(key, 0) >= d.count:
                            continue
                        sn[key] = d.count
                        eng.wait_ge(sem, d.count)
                    if o.fn is None:
                        continue
                    ins = o.fn(eng)
                    if o.dma:
                        ins.then_inc(dsem[o.sem[0]][o.sem[1]], 16)
                    elif o.signal:
                        ins.then_inc(csem[e], 1)

            for ops in segs:
                with nc.Block() as block:
                    @block.tensor
                    def _(eng, ops=ops):
                        run("pe", eng, ops)

                    @block.scalar
                    def _(eng, ops=ops):
                        run("act", eng, ops)

                    @block.vector
                    def _(eng, ops=ops):
                        run("dve", eng, ops)

                    @block.gpsimd
                    def _(eng, ops=ops):
                        run("pool", eng, ops)

                    @block.sync
                    def _(eng, ops=ops):
                        run("sp", eng, ops)


def _chunked(v):
    v = np.asarray(v, np.float32)
    return np.ascontiguousarray(v.reshape(-1, 128).T)


class Packer:
    def __init__(self):
        self.cols = {}
        self.parts = []
        self.n = 0

    def add(self, name, arr):
        arr = np.asarray(arr, np.float32)
        if arr.shape[0] < 128:
            arr = np.concatenate([arr, np.zeros((128 - arr.shape[0],) + arr.shape[1:], np.float32)], 0)
        arr = arr.reshape(128, -1)
        self.cols[name] = self.n
        self.parts.append(arr)
        self.n += arr.shape[1]

    def done(self):
        return np.ascontiguousarray(np.concatenate(self.parts, 1))


PERM = np.concatenate([np.arange(0, 128, 2), np.arange(1, 128, 2)])
SWAP = np.concatenate([PERM[64:], PERM[:64]])


def layout_offsets():
    sm, rw, cs = {}, {}, {}
    n = 0
    for i in range(NL):
        for nm in ("g_mix", "g_xa", "g_ffn"):
            sm[f"{nm}{i}"] = n
            n += 16
    sm["g_fin"] = n
    n += 16
    for i in range(NL):
        sm[f"ffnc{i}"] = n
        n += 88 * 4
    for e in range(2):
        sm[f"ssdc{e}"] = n
        n += 24 * 6
    for o in range(2):
        sm[f"hyc{o}"] = n
        n += 48 * 4
    for e in range(2):
        sm[f"qk{e}"] = n
        n += 4
    for o in range(2):
        sm[f"hysk{o}"] = n
        n += 16
    for o in range(2):
        sm[f"hyf{o}"] = n
        n += 4
    sm["_n"] = n
    n = 0
    for e in range(2):
        for nm, w in (("ssdn", 2048), ("dtb", 64), ("alog", 64), ("dsk", 32)):
            rw[f"{nm}{e}"] = n
            n += w
    for i in range(NL):
        rw[f"gmem{i}"] = n
        n += 2048
    rw["delta"] = n
    n += 2048
    rw["_n"] = n
    n = 0
    for nm in ("ident", "ones", "triF", "triB", "COS", "SIN", "NSIN", "C16", "S16", "NS16"):
        cs[nm] = n
        n += 128
    cs["tcol"] = n
    n += 64
    cs["tcolr"] = n
    n += 64
    cs["_n"] = n
    return sm, rw, cs


SM, RW, CS = layout_offsets()


def pack_small(inp):
    P = Packer()
    for i in range(NL):
        P.add(f"g_mix{i}", _chunked(inp["norm_mix"][i]))
        P.add(f"g_xa{i}", _chunked(inp["norm_xa"][i]))
        P.add(f"g_ffn{i}", _chunked(inp["norm_ffn"][i]))
    P.add("g_fin", _chunked(inp["final_norm"]))
    for i in range(NL):
        st = np.concatenate([inp["ffn_conv_w"][i], inp["ffn_conv_b"][i][None]], 0)
        P.add(f"ffnc{i}", st.reshape(4, 88, 128).transpose(2, 1, 0))
    for e in range(2):
        st = np.concatenate([inp["ssd_conv_w"][e], inp["ssd_conv_b"][e][None]], 0)
        P.add(f"ssdc{e}", st.reshape(6, 24, 128).transpose(2, 1, 0))
    for o in range(2):
        st = np.concatenate([inp["hy_conv_w"][o], inp["hy_conv_b"][o][None]], 0)
        P.add(f"hyc{o}", st.reshape(4, 48, 128).transpose(2, 1, 0))
    for e in range(2):
        qn, kn = inp["attn_q_norm"][e], inp["attn_k_norm"][e]
        P.add(f"qk{e}", np.stack([qn[PERM], qn[SWAP], kn[PERM], kn[SWAP]], 1))
    for o in range(2):
        P.add(f"hysk{o}", _chunked(inp["hy_skip"][o]))
    for o in range(2):
        P.add(f"hyf{o}", np.stack([inp["hy_f_b1"][o], inp["hy_f_b2"][o], inp["hy_f_b3"][o], inp["hy_f_freq"][o]], 1))
    small = P.done()
    assert P.cols == {k: v for k, v in SM.items() if k != "_n"} and P.n == SM["_n"]
    R = Packer()
    rep = lambda v: np.broadcast_to(np.asarray(v, np.float32).reshape(1, -1), (128, np.asarray(v).size))
    for e in range(2):
        R.add(f"ssdn{e}", rep(inp["ssd_norm"][e]))
        R.add(f"dtb{e}", rep(inp["ssd_dt_bias"][e]))
        R.add(f"alog{e}", rep(inp["ssd_a_log"][e]))
        R.add(f"dsk{e}", rep(inp["ssd_d"][e]))
    for i in range(NL):
        R.add(f"gmem{i}", rep(inp["norm_mem"][i]))
    deltas = np.abs(np.linspace(math.log(1e-2) / 1.5, math.log(1e-2) / 0.3, D, dtype=np.float32))
    R.add("delta", rep(deltas))
    rows = R.done()
    assert R.n == RW["_n"]
    return small, rows


def make_consts():
    C = Packer()
    i = np.arange(128)
    C.add("ident", np.eye(128))
    C.add("ones", np.ones((128, 128)))
    C.add("triF", (i[:, None] <= i[None, :]).astype(np.float32))
    C.add("triB", (i[:, None] >= i[None, :]).astype(np.float32))
    a = 2.0 * np.pi * np.outer(i, i).astype(np.float64) / 128.0
    C.add("COS", np.cos(a))
    C.add("SIN", np.sin(a))
    C.add("NSIN", -np.sin(a))
    a = 2.0 * np.pi * np.outer(i, i).astype(np.float64) / float(NFFT)
    C.add("C16", np.cos(a))
    C.add("S16", np.sin(a))
    C.add("NS16", -np.sin(a))
    t = np.linspace(0.0, 1.0, L, dtype=np.float32)
    C.add("tcol", t.reshape(64, 128).T)
    tr = t[::-1].copy()
    tr[L - 1] = 1.0e4
    C.add("tcolr", tr.reshape(64, 128).T)
    consts = C.done()
    sel2 = np.zeros((64, 32, 128), np.float32)
    for h in range(32):
        sel2[h, h, :] = 1.0
        sel2[32 + h, h, :] = 1.0
    rows = L // 64
    row = np.repeat(np.arange(rows), 64).astype(np.float32)
    col = np.tile(np.arange(64), rows).astype(np.float32)
    inv = (10000.0 ** (-np.arange(0, 64, 2, dtype=np.float32) / 64.0)).astype(np.float32)
    ang = np.concatenate([row[:, None] * inv, col[:, None] * inv], -1)
    c, s = np.cos(ang).T, np.sin(ang).T
    ropeC = np.concatenate([c, c], 0).astype(np.float32)
    ropeS = np.concatenate([-s, s], 0).astype(np.float32)
    w = (2.0 * np.pi * np.arange(L, dtype=np.float32) / L).astype(np.float32)
    f = np.linspace(1e-4, 15.0, 16, dtype=np.float32)
    fw = w[:, None] * f[None, :]
    z = np.concatenate([t[:, None], np.cos(fw), -np.sin(fw)], -1).astype(np.float32)
    zfT = np.ascontiguousarray(z.T)
    zfTr = np.ascontiguousarray(z[::-1].T)
    return consts, sel2.reshape(64, 4096), ropeC, ropeS, zfT, zfTr


WNAMES = [
    ("xa_wq", (NL, D, D)), ("xa_wk", (NL, D, D)), ("xa_wv", (NL, D, D)), ("xa_wo", (NL, D, D)),
    ("ffn_w_in", (NL, D, F2)), ("ffn_w_out", (NL, DFF, D)),
    ("mix_w_in", (2, D, 8256)), ("mix_w_out", (2, 4096, D)),
    ("mix_qkp", (2, D, 2560)), ("mix_qks", (2, D, 2560)),
    ("hy_w_in", (2, D, 3 * D)), ("hy_w_out", (2, D, D)),
    ("hy_f_w1", (2, 33, 64)), ("hy_f_w2", (2, 64, 64)), ("hy_f_w3", (2, 64, 64)), ("hy_f_w_out", (2, 64, 2 * D)),
]


class T:
    __slots__ = ("ap", "b")

    def __init__(self, ap, b):
        self.ap = ap
        self.b = b


class Ring:
    def __init__(self, items):
        self.items = items
        self.i = 0

    def next(self):
        t = self.items[self.i]
        self.i = (self.i + 1) % len(self.items)
        return t


def split(total, w):
    out, s = [], 0
    while s < total:
        n = min(w, total - s)
        out.append((s, n))
        s += n
    return out


class Builder:
    def __init__(self, n_layers=NL, debug=(), stop_after=None):
        self.nc = nc = bass.Bass("TRN2", target_bir_lowering=False)
        self.R = Rec()
        self.debug = set(debug)
        self.stop_after = stop_after
        self.n_layers = n_layers
        self.dram = {}
        din = lambda name, shape, dt=F32: nc.dram_tensor(name, list(shape), dt, kind="ExternalInput").ap()
        self.x = din("x", (L, D))
        self.mem = din("mem", (NMEM, D))
        self.W = {n: din(n, s) for n, s in WNAMES}
        self.small_d = din("small", (128, SM["_n"]))
        self.rows_d = din("rows", (128, RW["_n"]))
        self.consts_d = din("consts", (128, CS["_n"]))
        self.sel2_d = din("sel2", (64, 4096))
        self.ropeC = din("ropeC", (128, L))
        self.ropeS = din("ropeS", (128, L))
        self.zfT = din("zfT", (33, L))
        self.zfTr = din("zfTr", (33, L))
        self.out = nc.dram_tensor("y", [L, D], F32, kind="ExternalOutput").ap()
        self.Wb = {n: self.scr(n + "_b", s, BF16) for n, s in WNAMES}
        self.XT = self.scr("XT", (D, L), F32)
        self.uid = 0
        self.PSB = [nc.alloc_psum_tensor(f"psb{i}", [128, 512], F32).ap() for i in range(8)]
        self.off = 0
        self.persist = 0
        self.final_ops = []

    def scr(self, name, shape, dt):
        kind = "ExternalOutput" if name in self.debug else "Internal"
        t = self.nc.dram_tensor(name, list(shape), dt, kind=kind).ap()
        self.dram[name] = t
        return t

    def alloc(self, n, dt=F32):
        nb = n * (2 if dt == BF16 else 4)
        nb = (nb + 31) // 32 * 32
        assert self.off + nb <= 52480 * 4, f"SBUF overflow {self.off + nb}"
        self.uid += 1
        h = self.nc.alloc_sbuf_tensor_at(f"sb{self.uid}", [128, n], dt, offset=SB_BASE + self.off)
        self.off += nb
        return h.ap()

    def tile(self, n, dt=F32):
        return T(self.alloc(n, dt), self.R.buf())

    def ring(self, k, n, dt=F32):
        return Ring([self.tile(n, dt) for _ in range(k)])

    def bank(self, i):
        return self.PSB[i]

    def phase(self):
        self.R.barrier()
        self.off = self.persist
        self.banks = [T(self.bank(i), self.R.buf()) for i in range(8)]
        self.bank_i = 0
        for t in self.ptiles:
            t.b = self.R.buf()

    def soft_phase(self, keep, off):
        self.R.barrier()
        self.off = off
        self.banks = [T(self.bank(i), self.R.buf()) for i in range(8)]
        self.bank_i = 0
        for t in self.ptiles + list(keep):
            t.b = self.R.buf()

    def nbank(self, k=1):
        if self.bank_i + k > 8:
            self.bank_i = 0
        r = self.banks[self.bank_i:self.bank_i + k]
        self.bank_i = (self.bank_i + k) % 8
        return r

    def pe(self, fn, r, w):
        return self.R.op("pe", fn, [t.b for t in r], [t.b for t in w])

    def act(self, fn, r, w):
        return self.R.op("act", fn, [t.b for t in r], [t.b for t in w])

    def dve(self, fn, r, w):
        return self.R.op("dve", fn, [t.b for t in r], [t.b for t in w])

    def pool(self, fn, r, w):
        return self.R.op("pool", fn, [t.b for t in r], [t.b for t in w])

    def dma(self, out, in_, r=(), w=(), q="sp"):
        if q == "sp" and len(w) == 0 and len(r) > 0:
            q = "act"
        return self.R.dma(q, lambda e: e.dma_start(out=out, in_=in_), [t.b for t in r], [t.b for t in w])

    def sm(self, name, j=0, n=1):
        c = SM[name] + j
        return self.small.ap[:, c:c + n]

    def cst(self, name, n=128):
        return self.consts.ap[:, CS[name]:CS[name] + n]

    def setup(self):
        R = self.R
        self.ptiles = []
        self.consts = self.tile(CS["_n"])
        self.small = self.tile(SM["_n"])
        self.cb = self.tile(7 * 128, BF16)
        self.epsc = self.tile(8)
        self.rinv = self.tile(D)
        self.zc = self.tile(8)
        self.ptiles = [self.consts, self.small, self.cb, self.epsc, self.rinv, self.zc]
        self.persist = self.off
        self.banks = [T(self.bank(i), R.buf()) for i in range(8)]
        self.bank_i = 0
        self.dma(self.consts.ap, self.consts_d, w=[self.consts])
        self.dma(self.small.ap, self.small_d, w=[self.small])
        self.dve(lambda e: e.tensor_copy(self.cb.ap, self.consts.ap[:, 0:7 * 128]), [self.consts], [self.cb])
        self.dve(lambda e: e.memset(self.epsc.ap, EPS), [], [self.epsc])
        self.dve(lambda e: e.memset(self.zc.ap, 0.0), [], [self.zc])
        for n, s in WNAMES:
            w2 = self.W[n].rearrange("a k m -> (a k) m")
            wb2 = self.Wb[n].rearrange("a k m -> (a k) m")
            rows = s[0] * s[1]
            step = max(128, (1 << 21) // s[2] // 128 * 128)
            for r0 in range(0, rows, step):
                r1 = min(rows, r0 + step)
                self.dma(wb2[r0:r1, :], w2[r0:r1, :], q="pool")
        xin = self.ring(2, D)
        xo = self.ring(2, 16 * 512)
        identf = self.cst("ident")
        for t4 in range(L // 512):
            ot = xo.next()
            o3 = ot.ap.rearrange("p (k t) -> p k t", t=512)
            for tt in range(4):
                t0 = t4 * 512 + tt * 128
                xi = xin.next()
                self.dma(xi.ap, self.x[t0:t0 + 128, :], w=[xi])
                for half in range(2):
                    bk = self.nbank(2)
                    for j in range(8):
                        kc = half * 8 + j
                        b = bk[j // 4]
                        self.pe(lambda e, b=b, j=j, kc=kc, xi=xi: e.transpose(b.ap[:, (j % 4) * 128:(j % 4 + 1) * 128], xi.ap[:, kc * 128:(kc + 1) * 128], identf), [xi, self.consts], [b])
                    for jj in range(2):
                        b = bk[jj]
                        kc0 = half * 8 + jj * 4
                        f = self.act if jj == 0 else self.dve
                        if jj == 0:
                            self.act(lambda e, b=b, kc0=kc0, tt=tt, o3=o3: e.activation(o3[:, kc0:kc0 + 4, tt * 128:(tt + 1) * 128], b.ap.rearrange("p (k t) -> p k t", t=128), AF.Copy), [b], [ot])
                        else:
                            self.dve(lambda e, b=b, kc0=kc0, tt=tt, o3=o3: e.tensor_copy(o3[:, kc0:kc0 + 4, tt * 128:(tt + 1) * 128], b.ap.rearrange("p (k t) -> p k t", t=128)), [b], [ot])
            self.dma(self.XT.rearrange("(k p) t -> p k t", p=128)[:, :, t4 * 512:(t4 + 1) * 512], o3, r=[ot])

    def norm_block(self, hT, t0, TB, halo, gname, xst, sqr, rst):
        W = TB + 2 * halo
        h3 = hT.ap.rearrange("p (k t) -> p k t", t=W)
        onesb = self.cb.ap[:, 128:256]
        XT3 = self.XT.rearrange("(k p) t -> p k t", p=128)
        lo, hi = t0 - halo, t0 + TB + halo
        if lo < 0:
            self.dve(lambda e: e.memset(h3[:, :, 0:halo], 0.0), [], [hT])
        if hi > L:
            self.dve(lambda e: e.memset(h3[:, :, W - halo:W], 0.0), [], [hT])
        lo, hi = max(lo, 0), min(hi, L)
        pieces = split(hi - lo, 128)
        if len(pieces) > 1 and pieces[-1][1] < 8:
            (s1, n1), (s2, n2) = pieces[-2], pieces[-1]
            pieces = pieces[:-2] + [(s1, n1 + n2)]
        for (s, n) in pieces:
            a = lo + s
            c0 = a - (t0 - halo)
            xs = xst.next()
            x3 = xs.ap.rearrange("p (k t) -> p k t", t=136)
            self.dma(x3[:, :, 0:n], XT3[:, :, a:a + n], w=[xs])
            bk = self.nbank(1)[0]
            for kc in range(16):
                sq = sqr.next()
                self.act(lambda e, sq=sq, kc=kc, x3=x3, n=n: e.activation(sq.ap[:, 0:n], x3[:, kc, 0:n], AF.Square), [xs], [sq])
                self.pe(lambda e, sq=sq, kc=kc, bk=bk, n=n: e.matmul(bk.ap[:, 0:n], onesb, sq.ap[:, 0:n], start=(kc == 0), stop=(kc == 15)), [sq, self.cb], [bk])
            rs = rst.next()
            self.act(lambda e, rs=rs, bk=bk, n=n: e.activation(rs.ap[:, 0:n], bk.ap[:, 0:n], AF.Sqrt, bias=self.epsc.ap[:, 0:1], scale=1.0 / D), [bk, self.epsc], [rs])
            self.dve(lambda e, rs=rs, n=n: e.reciprocal(rs.ap[:, 0:n], rs.ap[:, 0:n]), [rs], [rs])
            for kc in range(16):
                self.dve(lambda e, kc=kc, x3=x3, rs=rs, n=n, c0=c0: e.scalar_tensor_tensor(h3[:, kc, c0:c0 + n], x3[:, kc, 0:n], self.sm(gname, kc), rs.ap[:, 0:n], ALU.mult, ALU.mult), [xs, rs, self.small], [hT])
        return h3

    def load_w(self, wt, Wb2, c0, ncols, KC):
        w3 = wt.ap[:, 0:KC * ncols].rearrange("p (k m) -> p k m", m=ncols)
        self.dma(w3, Wb2[:, c0:c0 + ncols].rearrange("(k p) m -> p k m", p=128), w=[wt])
        return w3

    def gemm_in(self, gname, halo, groups, TB=2048):
        W = TB + 2 * halo
        hT = self.tile(16 * W, BF16)
        xst = self.ring(2, 16 * 136)
        sqr = self.ring(3, 136, BF16)
        rst = self.ring(2, 136)
        wts = [[self.tile(16 * 512, BF16) for _ in range(2)] for _ in range(2)]
        wi = 0
        for t0 in range(0, L, TB):
            h3 = self.norm_block(hT, t0, TB, halo, gname, xst, sqr, rst)
            for g in groups:
                wset = wts[wi]
                wi ^= 1
                w3 = [self.load_w(wset[i], Wb2, c0, ncols, 16) for i, (Wb2, c0, ncols) in enumerate(g["w"])]
                for u in g["units"]:
                    if u["kind"] == "fm":
                        hh = u.get("halo", 0)
                        for (s, n) in split(TB, 512 - 2 * hh):
                            ncol = n + 2 * hh
                            c0 = s + (halo - hh)
                            bks = self.nbank(len(u["chunks"]))
                            for bk, (ti, coff) in zip(bks, u["chunks"]):
                                for kc in range(16):
                                    self.pe(lambda e, bk=bk, ti=ti, coff=coff, kc=kc, c0=c0, ncol=ncol, w3=w3: e.matmul(bk.ap[:, 0:ncol], w3[ti][:, kc, coff:coff + 128], h3[:, kc, c0:c0 + ncol], start=(kc == 0), stop=(kc == 15)), [wset[ti], hT], [bk])
                            u["epi"](t0 + s, n, bks)
                    else:
                        ti, coff, ncols = u["chunks"][0]
                        for tt in range(TB // 128):
                            bk = self.nbank(1)[0]
                            for kc in range(16):
                                self.pe(lambda e, bk=bk, ti=ti, coff=coff, ncols=ncols, kc=kc, tt=tt, w3=w3: e.matmul(bk.ap[:, 0:ncols], h3[:, kc, halo + tt * 128:halo + (tt + 1) * 128], w3[ti][:, kc, coff:coff + ncols], start=(kc == 0), stop=(kc == 15)), [wset[ti], hT], [bk])
                            u["epi"](t0 + tt * 128, bk)

    def gemm_out(self, AT, KC, Wb2, TB):
        aT = self.tile(KC * TB, BF16)
        a3 = aT.ap.rearrange("p (k t) -> p k t", t=TB)
        AT3 = AT.rearrange("(k p) t -> p k t", p=128)
        wts = self.ring(2, KC * 256, BF16)
        xr = self.ring(3, 512)
        for t0 in range(0, L, TB):
            for (k0, kn) in split(KC, 8):
                self.dma(a3[:, k0:k0 + kn, :], AT3[:, k0:k0 + kn, t0:t0 + TB], w=[aT])
            for m0 in range(0, D, 256):
                wt = wts.next()
                w3 = self.load_w(wt, Wb2, m0, 256, KC)
                for c in range(2):
                    row0 = m0 + c * 128
                    for (s, n) in split(TB, 512):
                        bk = self.nbank(1)[0]
                        xt = xr.next()
                        self.dma(xt.ap, self.XT[row0:row0 + 128, t0 + s:t0 + s + 512], w=[xt])
                        for kc in range(KC):
                            self.pe(lambda e, bk=bk, kc=kc, c=c, s=s, w3=w3: e.matmul(bk.ap, w3[:, kc, c * 128:(c + 1) * 128], a3[:, kc, s:s + 512], start=(kc == 0), stop=(kc == KC - 1)), [wt, aT], [bk])
                        self.dve(lambda e, bk=bk, xt=xt: e.tensor_tensor(xt.ap, bk.ap, xt.ap, ALU.add), [bk, xt], [xt])
                        self.dma(self.XT[row0:row0 + 128, t0 + s:t0 + s + 512], xt.ap, r=[xt])

    def conv_epi(self, bk, n, taps, cname, cj, nt, out_t):
        base = cj * (taps + 1)
        self.act(lambda e: e.activation(out_t.ap[:, 0:n], bk.ap[:, 0:n], AF.Identity, bias=self.sm(cname, base + taps), scale=self.sm(cname, base)), [bk, self.small], [out_t])
        for k in range(1, taps):
            self.dve(lambda e, k=k: e.scalar_tensor_tensor(out_t.ap[:, 0:n], bk.ap[:, k:k + n], self.sm(cname, base + k), out_t.ap[:, 0:n], ALU.mult, ALU.add), [bk, self.small, out_t], [out_t])

    def ffn(self, i):
        self.phase()
        AT = self.dram.get("AT") or self.scr("AT", (DFF, L), BF16)
        Wb2 = self.Wb["ffn_w_in"][i]
        tg = self.ring(2, 512)
        tu = self.ring(2, 512)
        ao = self.ring(3, 512, BF16)
        groups = []
        for gq in range(11):
            units = []
            for c in range(4):
                j = gq * 4 + c

                def epi(t, n, bks, j=j):
                    a, b = tg.next(), tu.next()
                    self.conv_epi(bks[0], n, 3, f"ffnc{i}", j, 0, a)
                    self.conv_epi(bks[1], n, 3, f"ffnc{i}", 44 + j, 0, b)
                    self.act(lambda e: e.activation(a.ap[:, 0:n], a.ap[:, 0:n], AF.Silu), [a], [a])
                    o = ao.next()
                    self.dve(lambda e: e.tensor_tensor(o.ap[:, 0:n], a.ap[:, 0:n], b.ap[:, 0:n], ALU.mult), [a, b], [o])
                    self.dma(AT[j * 128:(j + 1) * 128, t:t + n], o.ap[:, 0:n], r=[o])
                units.append(dict(kind="fm", halo=1, chunks=[(0, c * 128), (1, c * 128)], epi=epi))
            groups.append(dict(w=[(Wb2, gq * 512, 512), (Wb2, DFF + gq * 512, 512)], units=units))
        self.gemm_in(f"g_ffn{i}", 1, groups)
        self.phase()
        self.gemm_out(AT, 44, self.Wb["ffn_w_out"][i], 1024)

    def xattn(self, i):
        self.phase()
        QX = self.dram.get("QX") or self.scr("QX", (D, L), BF16)
        OX = self.dram.get("OX") or self.scr("OX", (D, L), BF16)
        identb = self.cb.ap[:, 0:128]
        kT = self.tile(16 * NMEM, BF16)
        kT3 = kT.ap.rearrange("p (k m) -> p k m", m=NMEM)
        vv = self.tile(2 * D, BF16)
        v3 = vv.ap.rearrange("p (c d) -> p c d", d=D)
        kv_persist = self.off
        mT = self.tile(16 * NMEM, BF16)
        mT3 = mT.ap.rearrange("p (k m) -> p k m", m=NMEM)
        gm = self.tile(D)
        self.dma(gm.ap, self.rows_d[:, RW[f"gmem{i}"]:RW[f"gmem{i}"] + D], w=[gm])
        mraw = self.ring(2, D)
        mn = self.ring(2, D, BF16)
        junk = self.tile(D)
        ss = self.ring(2, 8)
        for c in range(2):
            mr = mraw.next()
            self.dma(mr.ap, self.mem[c * 128:(c + 1) * 128, :], w=[mr])
            s1 = ss.next()
            self.dve(lambda e, s1=s1: e.memset(s1.ap, 0.0), [], [s1])
            self.act(lambda e, mr=mr, s1=s1: e.activation(junk.ap, mr.ap, AF.Square, accum_out=s1.ap[:, 0:1]), [mr, s1], [junk, s1])
            self.act(lambda e, s1=s1: e.activation(s1.ap[:, 1:2], s1.ap[:, 0:1], AF.Sqrt, bias=self.epsc.ap[:, 0:1], scale=1.0 / D), [s1, self.epsc], [s1])
            self.dve(lambda e, s1=s1: e.reciprocal(s1.ap[:, 2:3], s1.ap[:, 1:2]), [s1], [s1])
            m2 = mn.next()
            self.dve(lambda e, mr=mr, s1=s1, m2=m2: e.scalar_tensor_tensor(m2.ap, mr.ap, s1.ap[:, 2:3], gm.ap, ALU.mult, ALU.mult), [mr, s1, gm], [m2])
            for q4 in range(4):
                bk = self.nbank(1)[0]
                bb = bk.ap.bitcast(BF16)
                for j in range(4):
                    kc = q4 * 4 + j
                    self.pe(lambda e, bb=bb, j=j, kc=kc, m2=m2: e.transpose(bb[:, j * 128:(j + 1) * 128], m2.ap[:, kc * 128:(kc + 1) * 128], identb), [m2, self.cb], [bk])
                self.act(lambda e, bb=bb, q4=q4, c=c: e.activation(mT3[:, q4 * 4:q4 * 4 + 4, c * 128:(c + 1) * 128], bb[:, 0:512].rearrange("p (k t) -> p k t", t=128), AF.Copy), [bk], [mT])
        wts = self.ring(2, 16 * 512, BF16)
        for m0 in range(0, D, 512):
            wt = wts.next()
            w3 = self.load_w(wt, self.Wb["xa_wk"][i], m0, 512, 16)
            for c in range(4):
                bk = self.nbank(1)[0]
                for kc in range(16):
                    self.pe(lambda e, bk=bk, kc=kc, c=c, w3=w3: e.matmul(bk.ap[:, 0:NMEM], w3[:, kc, c * 128:(c + 1) * 128], mT3[:, kc, :], start=(kc == 0), stop=(kc == 15)), [wt, mT], [bk])
                self.act(lambda e, bk=bk, c=c, m0=m0: e.activation(kT3[:, m0 // 128 + c, :], bk.ap[:, 0:NMEM], AF.Copy), [bk], [kT])
            wt = wts.next()
            w3 = self.load_w(wt, self.Wb["xa_wv"][i], m0, 512, 16)
            for c in range(2):
                bk = self.nbank(1)[0]
                for kc in range(16):
                    self.pe(lambda e, bk=bk, kc=kc, c=c, w3=w3: e.matmul(bk.ap, mT3[:, kc, c * 128:(c + 1) * 128], w3[:, kc, :], start=(kc == 0), stop=(kc == 15)), [wt, mT], [bk])
                self.act(lambda e, bk=bk, c=c, m0=m0: e.activation(v3[:, c, m0:m0 + 512], bk.ap, AF.Copy), [bk], [vv])
        self.soft_phase([kT, vv], kv_persist)
        qo = self.ring(3, 512, BF16)
        groups = []
        for gq in range(4):
            units = []
            for c in range(4):
                j = gq * 4 + c

                def epi(t, n, bks, j=j):
                    o = qo.next()
                    self.act(lambda e: e.activation(o.ap[:, 0:n], bks[0].ap[:, 0:n], AF.Copy), [bks[0]], [o])
                    self.dma(QX[j * 128:(j + 1) * 128, t:t + n], o.ap[:, 0:n], r=[o])
                units.append(dict(kind="fm", chunks=[(0, c * 128)], epi=epi))
            groups.append(dict(w=[(self.Wb["xa_wq"][i], gq * 512, 512)], units=units))
        self.gemm_in(f"g_xa{i}", 0, groups)
        self.soft_phase([kT, vv], kv_persist)
        self.xa_attend(QX, OX, kT3, v3, kT, vv)
        self.phase()
        self.gemm_out(OX, 16, self.Wb["xa_wo"][i], 2048)

    def xa_attend(self, QX, OX, kT3, v3, kT, vv):
        onesb = self.cb.ap[:, 128:256]
        qin = self.ring(2, 16 * 512, BF16)
        pT = self.ring(2, 2 * 512, BF16)
        rc = self.ring(2, 512)
        oo = self.ring(2, 4 * 512, BF16)
        QX3 = QX.rearrange("(k p) t -> p k t", p=128)
        OX3 = OX.rearrange("(k p) t -> p k t", p=128)
        sc = 512.0 ** -0.5
        for tt in range(L // 512):
            qt = qin.next()
            q3 = qt.ap.rearrange("p (k t) -> p k t", t=512)
            self.dma(q3, QX3[:, :, tt * 512:(tt + 1) * 512], w=[qt])
            for h in range(4):
                bs = self.nbank(2)
                p = pT.next()
                p3 = p.ap.rearrange("p (c t) -> p c t", t=512)
                for mc in range(2):
                    for j in range(4):
                        kc = h * 4 + j
                        self.pe(lambda e, mc=mc, j=j, kc=kc, bs=bs, q3=q3: e.matmul(bs[mc].ap, kT3[:, kc, mc * 128:(mc + 1) * 128], q3[:, kc, :], start=(j == 0), stop=(j == 3)), [kT, qt], [bs[mc]])
                    self.act(lambda e, mc=mc, bs=bs, p3=p3: e.activation(p3[:, mc, :], bs[mc].ap, AF.Exp, scale=sc), [bs[mc]], [p])
                bd = self.nbank(1)[0]
                for mc in range(2):
                    self.pe(lambda e, mc=mc, bd=bd, p3=p3: e.matmul(bd.ap, onesb, p3[:, mc, :], start=(mc == 0), stop=(mc == 1)), [p, self.cb], [bd])
                r = rc.next()
                self.dve(lambda e, r=r, bd=bd: e.reciprocal(r.ap, bd.ap), [bd], [r])
                o = oo.next()
                o3 = o.ap.rearrange("p (k t) -> p k t", t=512)
                for j in range(4):
                    bo = self.nbank(1)[0]
                    for mc in range(2):
                        self.pe(lambda e, mc=mc, j=j, bo=bo, p3=p3, h=h: e.matmul(bo.ap, v3[:, mc, (h * 4 + j) * 128:(h * 4 + j + 1) * 128], p3[:, mc, :], start=(mc == 0), stop=(mc == 1)), [vv, p], [bo])
                    self.dve(lambda e, j=j, bo=bo, r=r, o3=o3: e.tensor_tensor(o3[:, j, :], bo.ap, r.ap, ALU.mult), [bo, r], [o])
                self.dma(OX3[:, h * 4:h * 4 + 4, tt * 512:(tt + 1) * 512], o3, r=[o])

    def final(self):
        self.phase()
        hT = self.tile(16 * 512)
        xst = self.ring(2, 16 * 256)
        sqr = self.ring(3, 256, BF16)
        rst = self.ring(2, 256)
        orow = self.ring(2, D)
        identf = self.cst("ident")
        XT3 = self.XT.rearrange("(k p) t -> p k t", p=128)
        onesb = self.cb.ap[:, 128:256]
        for t0 in range(0, L, 256):
            xs = xst.next()
            x3 = xs.ap.rearrange("p (k t) -> p k t", t=256)
            self.dma(x3, XT3[:, :, t0:t0 + 256], w=[xs])
            bk = self.nbank(1)[0]
            for kc in range(16):
                sq = sqr.next()
                self.act(lambda e, sq=sq, kc=kc, x3=x3: e.activation(sq.ap, x3[:, kc, :], AF.Square), [xs], [sq])
                self.pe(lambda e, sq=sq, kc=kc, bk=bk: e.matmul(bk.ap[:, 0:256], onesb, sq.ap, start=(kc == 0), stop=(kc == 15)), [sq, self.cb], [bk])
            rs = rst.next()
            self.act(lambda e, rs=rs, bk=bk: e.activation(rs.ap, bk.ap[:, 0:256], AF.Sqrt, bias=self.epsc.ap[:, 0:1], scale=1.0 / D), [bk, self.epsc], [rs])
            self.dve(lambda e, rs=rs: e.reciprocal(rs.ap, rs.ap), [rs], [rs])
            for kc in range(16):
                self.dve(lambda e, kc=kc, x3=x3, rs=rs: e.scalar_tensor_tensor(x3[:, kc, :], x3[:, kc, :], self.sm("g_fin", kc), rs.ap, ALU.mult, ALU.mult), [xs, rs, self.small], [xs])
            for tt in range(2):
                orw = orow.next()
                for q4 in range(4):
                    bk = self.nbank(1)[0]
                    for j in range(4):
                        kc = q4 * 4 + j
                        self.pe(lambda e, bk=bk, j=j, kc=kc, tt=tt, x3=x3: e.transpose(bk.ap[:, j * 128:(j + 1) * 128], x3[:, kc, tt * 128:(tt + 1) * 128], identf), [xs, self.consts], [bk])
                    if q4 % 2 == 0:
                        self.act(lambda e, bk=bk, q4=q4, orw=orw: e.activation(orw.ap[:, q4 * 512:(q4 + 1) * 512], bk.ap, AF.Copy), [bk], [orw])
                    else:
                        self.dve(lambda e, bk=bk, q4=q4, orw=orw: e.tensor_copy(orw.ap[:, q4 * 512:(q4 + 1) * 512], bk.ap), [bk], [orw])
                o = self.dma(self.out[t0 + tt * 128:t0 + (tt + 1) * 128, :], orw.ap, r=[orw])
                self.final_ops.append(o)

    def finish(self):
        self.R.emit(self.nc, self.final_ops)
        return self.nc


def build_program(stages=None, debug=()):
    B = Builder(debug=debug)
    B.setup()
    if stages is None:
        stages = []
        for i in range(NL):
            stages += [("mix", i), ("xa", i), ("ffn", i)]
    for kind, i in stages:
        if kind == "mix":
            if i % 2 == 0:
                B.mixer_even(i // 2, i)
            else:
                B.mixer_odd(i // 2, i)
        elif kind == "xa":
            B.xattn(i)
        elif kind == "ffn":
            B.ffn(i)
    B.final()
    return B.finish()


def make_in_maps(inp):
    small, rows = pack_small(inp)
    consts, sel2, ropeC, ropeS, zfT, zfTr = make_consts()
    shared = {"small": small, "rows": rows, "consts": consts, "sel2": sel2, "ropeC": ropeC, "ropeS": ropeS,
              "zfT": zfT, "zfTr": zfTr}
    for n, _ in WNAMES:
        if n in inp:
            shared[n] = np.ascontiguousarray(np.asarray(inp[n], np.float32))
    wi = np.asarray(inp["mix_w_in"], np.float32)
    qk = wi[:, :, 5184:7744].reshape(2, D, 20, 128)
    shared["mix_qkp"] = np.ascontiguousarray(qk[:, :, :, PERM].reshape(2, D, 2560))
    shared["mix_qks"] = np.ascontiguousarray(qk[:, :, :, SWAP].reshape(2, D, 2560))
    seqs = [(inp["x_prompt"][b], inp["mem_prompt"][b]) for b in range(4)] + [(inp["x_sample"][0], inp["mem_sample"][0])]
    maps = []
    for c in range(8):
        x, m = seqs[min(c, 4)]
        d = dict(shared)
        d["x"] = np.ascontiguousarray(np.asarray(x, np.float32))
        d["mem"] = np.ascontiguousarray(np.asarray(m, np.float32))
        maps.append(d)
    return maps


_NC_CACHE = {}


def kernel(**inputs):
    inp = {k: np.asarray(v) for k, v in inputs.items()}
    if "nc" not in _NC_CACHE:
        _NC_CACHE["nc"] = build_program()
    maps = make_in_maps(inp)
    res = run_bass_kernel_spmd(_NC_CACHE["nc"], maps, core_ids=list(range(8)))
    ys = [np.asarray(res.results[c]["y"], np.float32) for c in range(5)]
    y_prompt = np.stack(ys[0:4], 0)
    y_sample = ys[4][None]
    return (y_prompt, y_sample)


def _mixer_even(self, e, i):
    self.phase()
    nm = lambda n, s, dt: self.dram.get(n) or self.scr(n, s, dt)
    ZS = nm("ZS", (L, D), BF16)
    XBCT = nm("XBCT", (3072, L), BF16)
    DT = nm("DT", (L, 64), F32)
    QT = nm("QT", (D, L), BF16)
    KT = nm("KT", (512, L), BF16)
    V = nm("V", (L, 512), BF16)
    YT = nm("YT", (4096, L), BF16)
    YF = nm("YF", (L, D), F32)
    W = self.Wb["mix_w_in"][e]
    QKP = self.Wb["mix_qkp"][e]
    QKS = self.Wb["mix_qks"][e]
    onesb = self.cb.ap[:, 128:256]
    dtb = self.tile(64)
    self.dma(dtb.ap, self.rows_d[:, RW[f"dtb{e}"]:RW[f"dtb{e}"] + 64], w=[dtb])
    onec = self.tile(8)
    self.dve(lambda en: en.memset(onec.ap, 1.0), [], [onec])
    ob = self.ring(3, 512, BF16)
    tf = self.ring(2, 512)
    tf2 = self.ring(2, 512)
    rC = self.ring(2, 512)
    rS = self.ring(2, 512)
    sqb = self.ring(2, 512, BF16)
    rsr = self.ring(2, 512)
    dto = self.ring(2, 64)
    groups = []
    for g in range(4):
        def epi_z(t, bk, g=g):
            o = ob.next()
            self.act(lambda en: en.activation(o.ap, bk.ap, AF.Silu), [bk], [o])
            self.dma(ZS[t:t + 128, g * 512:(g + 1) * 512], o.ap, r=[o])
        groups.append(dict(w=[(W, g * 512, 512)], units=[dict(kind="tm", chunks=[(0, 0, 512)], epi=epi_z)]))
    for g in range(6):
        units = []
        for c in range(4):
            j = g * 4 + c

            def epi_c(t, n, bks, j=j):
                a = tf.next()
                self.conv_epi(bks[0], n, 5, f"ssdc{e}", j, 0, a)
                o = ob.next()
                self.act(lambda en: en.activation(o.ap[:, 0:n], a.ap[:, 0:n], AF.Silu), [a], [o])
                self.dma(XBCT[j * 128:(j + 1) * 128, t:t + n], o.ap[:, 0:n], r=[o])
            units.append(dict(kind="fm", halo=2, chunks=[(0, c * 128)], epi=epi_c))
        groups.append(dict(w=[(W, 2048 + g * 512, 512)], units=units))

    def epi_dt(t, bk):
        o = dto.next()
        self.dve(lambda en: en.tensor_tensor(o.ap, bk.ap[:, 0:64], dtb.ap, ALU.add), [bk, dtb], [o])
        self.act(lambda en: en.activation(o.ap, o.ap, AF.Exp), [o], [o])
        self.act(lambda en: en.activation(o.ap, o.ap, AF.Ln, bias=onec.ap[:, 0:1], scale=1.0), [o, onec], [o])
        self.dma(DT[t:t + 128, :], o.ap, r=[o])
    groups.append(dict(w=[(W, 5120, 64)], units=[dict(kind="tm", chunks=[(0, 0, 64)], epi=epi_dt)]))

    def mk_rope(j, dst, gcol):
        def epi(t, n, bks):
            A, Bk = bks
            sq = sqb.next()
            self.act(lambda en: en.activation(sq.ap[:, 0:n], A.ap[:, 0:n], AF.Square), [A], [sq])
            bs = self.nbank(1)[0]
            self.pe(lambda en: en.matmul(bs.ap[:, 0:n], onesb, sq.ap[:, 0:n], start=True, stop=True), [sq, self.cb], [bs])
            rs = rsr.next()
            self.act(lambda en: en.activation(rs.ap[:, 0:n], bs.ap[:, 0:n], AF.Sqrt, bias=self.epsc.ap[:, 0:1], scale=1.0 / 128), [bs, self.epsc], [rs])
            self.dve(lambda en: en.reciprocal(rs.ap[:, 0:n], rs.ap[:, 0:n]), [rs], [rs])
            c, s = rC.next(), rS.next()
            self.dma(c.ap[:, 0:n], self.ropeC[:, t:t + n], w=[c])
            self.dma(s.ap[:, 0:n], self.ropeS[:, t:t + n], w=[s])
            t1, t2 = tf.next(), tf2.next()
            self.dve(lambda en: en.scalar_tensor_tensor(t1.ap[:, 0:n], A.ap[:, 0:n], self.sm(f"qk{e}", gcol), c.ap[:, 0:n], ALU.mult, ALU.mult), [A, c, self.small], [t1])
            self.dve(lambda en: en.scalar_tensor_tensor(t2.ap[:, 0:n], Bk.ap[:, 0:n], self.sm(f"qk{e}", gcol + 1), s.ap[:, 0:n], ALU.mult, ALU.mult), [Bk, s, self.small], [t2])
            self.dve(lambda en: en.tensor_tensor(t1.ap[:, 0:n], t1.ap[:, 0:n], t2.ap[:, 0:n], ALU.add), [t1, t2], [t1])
            o = ob.next()
            self.dve(lambda en: en.tensor_tensor(o.ap[:, 0:n], t1.ap[:, 0:n], rs.ap[:, 0:n], ALU.mult), [t1, rs], [o])
            self.dma(dst[j * 128:(j + 1) * 128, t:t + n], o.ap[:, 0:n], r=[o])
        return epi
    for g in range(5):
        units = []
        for c in range(4):
            if g < 4:
                ep = mk_rope(g * 4 + c, QT, 0)
            else:
                ep = mk_rope(c, KT, 2)
            units.append(dict(kind="fm", chunks=[(0, c * 128), (1, c * 128)], epi=ep))
        groups.append(dict(w=[(QKP, g * 512, 512), (QKS, g * 512, 512)], units=units))

    def epi_v(t, bk):
        o = ob.next()
        self.act(lambda en: en.activation(o.ap, bk.ap, AF.Copy), [bk], [o])
        self.dma(V[t:t + 128, :], o.ap, r=[o])
    groups.append(dict(w=[(W, 7744, 512)], units=[dict(kind="tm", chunks=[(0, 0, 512)], epi=epi_v)]))
    self.gemm_in(f"g_mix{i}", 2, groups)
    self.attention(QT, KT, V, YT)
    self.ssd(e, XBCT, ZS, DT, YF, YT)
    self.phase()
    self.gemm_out(YT, 32, self.Wb["mix_w_out"][e], 1024)


def _attention(self, QT, KT, V, YT):
    self.phase()
    onesb = self.cb.ap[:, 128:256]
    kT = self.tile(4 * L, BF16)
    k3 = kT.ap.rearrange("p (h t) -> p h t", t=L)
    vv = self.tile(64 * 512, BF16)
    v3 = vv.ap.rearrange("p (c d) -> p c d", d=512)
    for h in range(4):
        self.dma(k3[:, h, :], KT[h * 128:(h + 1) * 128, :], w=[kT])
    V3 = V.rearrange("(c p) d -> p c d", p=128)
    for c0 in range(0, 64, 16):
        self.dma(v3[:, c0:c0 + 16, :], V3[:, c0:c0 + 16, :], w=[vv])
    qr = self.ring(3, 512, BF16)
    pr = self.ring(4, 512, BF16)
    rr = self.ring(2, 512)
    orr = self.ring(2, 512, BF16)
    Ob, Db, Sb = self.banks[0:2], self.banks[2:4], self.banks[4:8]
    sc = 128.0 ** -0.5
    it = 0
    for tt in range(L // 512):
        for h in range(16):
            kvh = h // 4
            q = qr.next()
            self.dma(q.ap, QT[h * 128:(h + 1) * 128, tt * 512:(tt + 1) * 512], w=[q])
            O, Dn = Ob[it % 2], Db[it % 2]
            it += 1
            pend = None
            for kc in range(65):
                if kc < 64:
                    S = Sb[kc % 4]
                    self.pe(lambda en, S=S, kc=kc, kvh=kvh, q=q: en.matmul(S.ap, k3[:, kvh, kc * 128:(kc + 1) * 128], q.ap, start=True, stop=True), [kT, q], [S])
                    P = pr.next()
                    self.act(lambda en, S=S, P=P: en.activation(P.ap, S.ap, AF.Exp, scale=sc), [S], [P])
                if pend is not None:
                    pk, pP = pend
                    self.pe(lambda en, pk=pk, pP=pP, O=O, kvh=kvh: en.matmul(O.ap, v3[:, pk, kvh * 128:(kvh + 1) * 128], pP.ap, start=(pk == 0), stop=(pk == 63)), [vv, pP], [O])
                    self.pe(lambda en, pk=pk, pP=pP, Dn=Dn: en.matmul(Dn.ap, onesb, pP.ap, start=(pk == 0), stop=(pk == 63)), [self.cb, pP], [Dn])
                pend = (kc, P) if kc < 64 else None
            r = rr.next()
            self.dve(lambda en, r=r, Dn=Dn: en.reciprocal(r.ap, Dn.ap), [Dn], [r])
            o = orr.next()
            self.dve(lambda en, o=o, O=O, r=r: en.tensor_tensor(o.ap, O.ap, r.ap, ALU.mult), [O, r], [o])
            self.dma(YT[2048 + h * 128:2048 + (h + 1) * 128, tt * 512:(tt + 1) * 512], o.ap, r=[o])


Builder.mixer_even = _mixer_even
Builder.attention = _attention


def _ssd(self, e, XBCT, ZS, DT, YF, YT):
    self.phase()
    cb = self.cb.ap
    identb, onesb = cb[:, 0:128], cb[:, 128:256]
    trib = [cb[:, 256:384], cb[:, 384:512]]
    maskf = [self.cst("triF"), self.cst("triB")]
    bn = self.banks
    arow = self.tile(64)
    self.dma(arow.ap, self.rows_d[:, RW[f"alog{e}"]:RW[f"alog{e}"] + 64], w=[arow])
    self.act(lambda en: en.activation(arow.ap, arow.ap, AF.Exp), [arow], [arow])
    self.dve(lambda en: en.tensor_scalar(arow.ap, arow.ap, -1.0, None, ALU.mult), [arow], [arow])
    dsk = self.tile(32)
    self.dma(dsk.ap, self.rows_d[:, RW[f"dsk{e}"]:RW[f"dsk{e}"] + 32], w=[dsk])
    ssdn = self.tile(D)
    self.dma(ssdn.ap, self.rows_d[:, RW[f"ssdn{e}"]:RW[f"ssdn{e}"] + D], w=[ssdn])
    sel2 = self.tile(4096, BF16)
    self.dma(sel2.ap[0:64, :], self.sel2_d, w=[sel2], q="pool")
    ST = self.tile(D)
    STb = self.tile(D, BF16)
    ST3 = ST.ap.rearrange("p (h q) -> p h q", q=64)
    xbr = self.ring(2, 24 * 512, BF16)
    dtr = self.ring(2, 4 * 64)
    R2 = lambda n, dt=F32: self.ring(2, n, dt)
    lar, lahlr, cstr, csr, totr, decr, wr, ecsr = R2(32), R2(64, BF16), R2(128), R2(32), R2(32), R2(32), R2(32), R2(32)
    cshlr, csTr = R2(64, BF16), R2(128, BF16)
    xdtr, xwr, btr, cbmr = R2(D, BF16), R2(D, BF16), R2(512, BF16), R2(512)
    difr, wtr, ychr = R2(1024), R2(1024, BF16), R2(D)
    yskr, yfr, zsr, ynr, ynTr, s1r = self.ring(1, D), self.ring(1, D), self.ring(1, D, BF16), self.ring(1, D, BF16), self.ring(1, 16 * 512, BF16), R2(8)
    XB3 = XBCT.rearrange("(k p) t -> p k t", p=128)
    YT3 = YT.rearrange("(k p) t -> p k t", p=128)
    bc = lambda ap, shape: ap.to_broadcast(shape)
    for dirn in range(2):
        if dirn == 1:
            self.R.barrier()
        self.dve(lambda en: en.memset(ST.ap, 0.0), [], [ST])
        self.dve(lambda en: en.memset(STb.ap, 0.0), [], [STb])
        order = list(range(64)) if dirn == 0 else list(range(63, -1, -1))
        xb = dts = ynT = None
        for ci, c in enumerate(order):
            t0 = c * 128
            o = (c % 4) * 128
            if ci % 4 == 0:
                s0 = (c // 4) * 512
                xb = xbr.next()
                xb3 = xb.ap.rearrange("p (k t) -> p k t", t=512)
                self.dma(xb3[:, 0:12, :], XB3[:, 0:12, s0:s0 + 512], w=[xb])
                self.dma(xb3[:, 12:24, :], XB3[:, 12:24, s0:s0 + 512], w=[xb])
                dts = dtr.next()
                dts3 = dts.ap.rearrange("p (c h) -> p c h", h=64)
                self.dma(dts3, DT[s0:s0 + 512, :].rearrange("(c p) h -> p c h", p=128), w=[dts])
                if dirn == 1:
                    ynT = ynTr.next()
                    ynT3 = ynT.ap.rearrange("p (k t) -> p k t", t=512)
            dtc = dts3[:, c % 4, dirn * 32:(dirn + 1) * 32]
            la, lahl, cst, cs, tot, dec, w, ecs = lar.next(), lahlr.next(), cstr.next(), csr.next(), totr.next(), decr.next(), wr.next(), ecsr.next()
            cshl, csT = cshlr.next(), csTr.next()
            self.dve(lambda en, la=la, dtc=dtc: en.tensor_tensor(la.ap, dtc, arow.ap[:, dirn * 32:(dirn + 1) * 32], ALU.mult), [dts, arow], [la])
            self.dve(lambda en, la=la, lahl=lahl: en.tensor_copy(lahl.ap[:, 0:32], la.ap), [la], [lahl])
            self.dve(lambda en, la=la, lahl=lahl: en.tensor_tensor(lahl.ap[:, 32:64], la.ap, lahl.ap[:, 0:32], ALU.subtract), [la, lahl], [lahl])
            self.pe(lambda en, lahl=lahl: en.matmul(bn[0].ap[:, 0:64], trib[dirn], lahl.ap, start=True, stop=True), [lahl, self.cb], [bn[0]])
            self.pe(lambda en, lahl=lahl: en.matmul(bn[0].ap[:, 64:128], onesb, lahl.ap, start=True, stop=True), [lahl, self.cb], [bn[0]])
            self.act(lambda en, cst=cst: en.activation(cst.ap, bn[0].ap[:, 0:128], AF.Copy), [bn[0]], [cst])
            self.dve(lambda en, cst=cst, cs=cs: en.tensor_tensor(cs.ap, cst.ap[:, 0:32], cst.ap[:, 32:64], ALU.add), [cst], [cs])
            self.dve(lambda en, cst=cst, tot=tot: en.tensor_tensor(tot.ap, cst.ap[:, 64:96], cst.ap[:, 96:128], ALU.add), [cst], [tot])
            self.act(lambda en, dec=dec, tot=tot: en.activation(dec.ap, tot.ap, AF.Exp), [tot], [dec])
            self.dve(lambda en, w=w, tot=tot, cs=cs: en.tensor_tensor(w.ap, tot.ap, cs.ap, ALU.subtract), [tot, cs], [w])
            self.act(lambda en, w=w: en.activation(w.ap, w.ap, AF.Exp), [w], [w])
            self.act(lambda en, ecs=ecs, cs=cs: en.activation(ecs.ap, cs.ap, AF.Exp), [cs], [ecs])
            self.dve(lambda en, cshl=cshl, cs=cs: en.tensor_copy(cshl.ap[:, 0:32], cs.ap), [cs], [cshl])
            self.dve(lambda en, cshl=cshl, cs=cs: en.tensor_tensor(cshl.ap[:, 32:64], cs.ap, cshl.ap[:, 0:32], ALU.subtract), [cs, cshl], [cshl])
            b1b = bn[1].ap.bitcast(BF16)
            self.pe(lambda en, cshl=cshl: en.transpose(b1b[0:64, 0:128], cshl.ap, identb), [cshl, self.cb], [bn[1]])
            self.act(lambda en, csT=csT: en.activation(csT.ap[0:64, :], b1b[0:64, 0:128], AF.Copy), [bn[1]], [csT])
            xdt, xw, bt, cbm = xdtr.next(), xwr.next(), btr.next(), cbmr.next()
            xdt3 = xdt.ap.rearrange("p (h q) -> p h q", q=64)
            for hb in range(2):
                bb = bn[2 + hb].ap.bitcast(BF16)
                for j in range(8):
                    self.pe(lambda en, bb=bb, j=j, hb=hb, xb3=xb3: en.transpose(bb[:, j * 128:(j + 1) * 128], xb3[:, hb * 8 + j, o:o + 128], identb), [xb, self.cb], [bn[2 + hb]])
                bv = bb.rearrange("p (h q) -> p h q", q=64)
                if dirn == 1:
                    ysk = yskr.next() if hb == 0 else ysk
                    ysk3 = ysk.ap.rearrange("p (h q) -> p h q", q=64)
                    self.dve(lambda en, bv=bv, hb=hb, ysk3=ysk3: en.tensor_tensor(ysk3[:, hb * 16:(hb + 1) * 16, :], bv, bc(dsk.ap[:, hb * 16:(hb + 1) * 16].unsqueeze(2), [128, 16, 64]), ALU.mult), [bn[2 + hb], dsk], [ysk])
                self.dve(lambda en, bv=bv, hb=hb, xdt3=xdt3, dtc=dtc: en.tensor_tensor(xdt3[:, hb * 16:(hb + 1) * 16, :], bv, bc(dtc[:, hb * 16:(hb + 1) * 16].unsqueeze(2), [128, 16, 64]), ALU.mult), [bn[2 + hb], dts], [xdt])
            xw3 = xw.ap.rearrange("p (h q) -> p h q", q=64)
            self.dve(lambda en, xw3=xw3, xdt3=xdt3, w=w: en.tensor_tensor(xw3, xdt3, bc(w.ap.unsqueeze(2), [128, 32, 64]), ALU.mult), [xdt, w], [xw])
            for g in range(4):
                self.pe(lambda en, g=g, xb3=xb3: en.transpose(b1b[:, 512 + g * 128:512 + (g + 1) * 128], xb3[:, 16 + g, o:o + 128], identb), [xb, self.cb], [bn[1]])
            self.act(lambda en, bt=bt: en.activation(bt.ap, b1b[:, 512:1024], AF.Copy), [bn[1]], [bt])
            for g in range(4):
                self.pe(lambda en, g=g, xb3=xb3: en.matmul(bn[4].ap[:, g * 128:(g + 1) * 128], xb3[:, 16 + g, o:o + 128], xb3[:, 20 + g, o:o + 128], start=True, stop=True), [xb], [bn[4]])
            cbm3 = cbm.ap.rearrange("p (g l) -> p g l", l=128)
            self.dve(lambda en, cbm3=cbm3: en.tensor_tensor(cbm3, bn[4].ap.rearrange("p (g l) -> p g l", l=128), bc(maskf[dirn].unsqueeze(1), [128, 4, 128]), ALU.mult), [bn[4], self.consts], [cbm])
            ych = ychr.next()
            ych3 = ych.ap.rearrange("p (h q) -> p h q", q=64)
            for g in range(4):
                for hh in range(8):
                    bk = bn[5 + hh // 4]
                    self.pe(lambda en, bk=bk, hh=hh, g=g, csT=csT: en.matmul(bk.ap[:, (hh % 4) * 128:(hh % 4 + 1) * 128], sel2.ap[0:64, (8 * g + hh) * 128:(8 * g + hh + 1) * 128], csT.ap[0:64, :], start=True, stop=True), [sel2, csT], [bk])
                dif, wt = difr.next(), wtr.next()
                dif3 = dif.ap.rearrange("p (h l) -> p h l", l=128)
                for hb in range(2):
                    self.dve(lambda en, hb=hb, g=g, dif3=dif3, cs=cs: en.tensor_tensor(dif3[:, hb * 4:(hb + 1) * 4, :], bn[5 + hb].ap.rearrange("p (h l) -> p h l", l=128), bc(cs.ap[:, 8 * g + hb * 4:8 * g + hb * 4 + 4].unsqueeze(2), [128, 4, 128]), ALU.subtract), [bn[5 + hb], cs], [dif])
                self.act(lambda en, dif=dif: en.activation(dif.ap, dif.ap, AF.Exp), [dif], [dif])
                wt3 = wt.ap.rearrange("p (h l) -> p h l", l=128)
                self.dve(lambda en, wt3=wt3, dif3=dif3, cbm3=cbm3, g=g: en.scalar_tensor_tensor(wt3, dif3, 1.0, bc(cbm3[:, g:g + 1, :], [128, 8, 128]), ALU.min, ALU.mult), [dif, cbm], [wt])
                for hh in range(8):
                    self.pe(lambda en, hh=hh, g=g, wt3=wt3, xdt=xdt: en.matmul(bn[7].ap[:, hh * 64:(hh + 1) * 64], wt3[:, hh, :], xdt.ap[:, (8 * g + hh) * 64:(8 * g + hh + 1) * 64], start=True, stop=True), [wt, xdt], [bn[7]])
                self.pe(lambda en, g=g, xb3=xb3: en.matmul(bn[2].ap, xb3[:, 20 + g, o:o + 128], STb.ap[:, g * 512:(g + 1) * 512], start=True, stop=True), [xb, STb], [bn[2]])
                self.dve(lambda en, g=g, ych3=ych3, ecs=ecs: en.tensor_tensor(ych3[:, 8 * g:8 * g + 8, :], bn[2].ap.rearrange("p (h q) -> p h q", q=64), bc(ecs.ap[:, 8 * g:8 * g + 8].unsqueeze(2), [128, 8, 64]), ALU.mult), [bn[2], ecs], [ych])
                self.dve(lambda en, g=g, ych=ych: en.tensor_tensor(ych.ap[:, g * 512:(g + 1) * 512], ych.ap[:, g * 512:(g + 1) * 512], bn[7].ap, ALU.add), [bn[7], ych], [ych])
                self.pe(lambda en, g=g, bt=bt, xw=xw: en.matmul(bn[3].ap, bt.ap[:, g * 128:(g + 1) * 128], xw.ap[:, g * 512:(g + 1) * 512], start=True, stop=True), [bt, xw], [bn[3]])
                self.dve(lambda en, g=g, dec=dec: en.tensor_tensor(ST3[:, 8 * g:8 * g + 8, :], ST3[:, 8 * g:8 * g + 8, :], bc(dec.ap[:, 8 * g:8 * g + 8].unsqueeze(2), [128, 8, 64]), ALU.mult), [ST, dec], [ST])
                self.dve(lambda en, g=g: en.tensor_tensor(ST.ap[:, g * 512:(g + 1) * 512], ST.ap[:, g * 512:(g + 1) * 512], bn[3].ap, ALU.add), [ST, bn[3]], [ST])
                self.act(lambda en, g=g: en.activation(STb.ap[:, g * 512:(g + 1) * 512], ST.ap[:, g * 512:(g + 1) * 512], AF.Copy), [ST], [STb])
            if dirn == 0:
                self.dma(YF[t0:t0 + 128, :], ych.ap, r=[ych])
            else:
                yf, zs, yn, s1 = yfr.next(), zsr.next(), ynr.next(), s1r.next()
                self.dma(yf.ap, YF[t0:t0 + 128, :], w=[yf])
                self.dma(zs.ap, ZS[t0:t0 + 128, :], w=[zs])
                self.dve(lambda en, ych=ych, yf=yf: en.tensor_tensor(ych.ap, ych.ap, yf.ap, ALU.add), [ych, yf], [ych])
                self.dve(lambda en, ych=ych, ysk=ysk: en.tensor_tensor(ych.ap, ych.ap, ysk.ap, ALU.add), [ych, ysk], [ych])
                self.dve(lambda en, ych=ych, zs=zs: en.tensor_tensor(ych.ap, ych.ap, zs.ap, ALU.mult), [ych, zs], [ych])
                self.dve(lambda en, s1=s1: en.memset(s1.ap, 0.0), [], [s1])
                self.act(lambda en, ych=ych, yf=yf, s1=s1: en.activation(yf.ap, ych.ap, AF.Square, accum_out=s1.ap[:, 0:1]), [ych, s1], [yf, s1])
                self.act(lambda en, s1=s1: en.activation(s1.ap[:, 1:2], s1.ap[:, 0:1], AF.Sqrt, bias=self.epsc.ap[:, 0:1], scale=1.0 / D), [s1, self.epsc], [s1])
                self.dve(lambda en, s1=s1: en.reciprocal(s1.ap[:, 2:3], s1.ap[:, 1:2]), [s1], [s1])
                self.dve(lambda en, yn=yn, ych=ych, s1=s1: en.scalar_tensor_tensor(yn.ap, ych.ap, s1.ap[:, 2:3], ssdn.ap, ALU.mult, ALU.mult), [ych, s1, ssdn], [yn])
                for hb in range(2):
                    bb = bn[5 + hb].ap.bitcast(BF16)
                    for j in range(8):
                        self.pe(lambda en, bb=bb, j=j, hb=hb, yn=yn: en.transpose(bb[:, j * 128:(j + 1) * 128], yn.ap[:, (hb * 8 + j) * 128:(hb * 8 + j + 1) * 128], identb), [yn, self.cb], [bn[5 + hb]])
                    f = self.act if hb == 0 else self.dve
                    if hb == 0:
                        self.act(lambda en, bb=bb, ynT3=ynT3: en.activation(ynT3[:, 0:8, o:o + 128], bb.rearrange("p (k t) -> p k t", t=128), AF.Copy), [bn[5]], [ynT])
                    else:
                        self.dve(lambda en, bb=bb, ynT3=ynT3: en.tensor_copy(ynT3[:, 8:16, o:o + 128], bb.rearrange("p (k t) -> p k t", t=128)), [bn[6]], [ynT])
                if c % 4 == 0:
                    s0 = (c // 4) * 512
                    self.dma(YT3[:, 0:16, s0:s0 + 512], ynT3, r=[ynT])


Builder.ssd = _ssd


def _fft_stage1(self, src, T1, S1):
    self.phase()
    cb = self.cb.ap
    cosb, nsinb = cb[:, 512:640], cb[:, 768:896]
    xin = self.ring(2, D, BF16)
    outb = self.ring(2, 2 * D, BF16)
    tmp = self.ring(4, 512)
    src3 = src.rearrange("(a b) c -> b a c", b=128)
    C16, S16, NS16 = self.cst("C16"), self.cst("S16"), self.cst("NS16")
    for t2 in range(128):
        xi = xin.next()
        self.dma(xi.ap[0:T1, :], src3[t2, 0:T1, :], w=[xi])
        ob = outb.next()
        ob3 = ob.ap.rearrange("p (r c) -> p r c", c=D)
        for ct in range(4):
            cs_ = slice(ct * 512, (ct + 1) * 512)
            bre, bim = self.nbank(2)
            self.pe(lambda en, bre=bre, xi=xi, cs_=cs_: en.matmul(bre.ap, cosb[0:T1, :], xi.ap[0:T1, cs_], start=True, stop=True), [xi, self.cb], [bre])
            self.pe(lambda en, bim=bim, xi=xi, cs_=cs_: en.matmul(bim.ap, nsinb[0:T1, :], xi.ap[0:T1, cs_], start=True, stop=True), [xi, self.cb], [bim])
            a, b = tmp.next(), tmp.next()
            self.act(lambda en, a=a, bre=bre, t2=t2: en.activation(a.ap, bre.ap, AF.Identity, bias=self.zc.ap[:, 0:1], scale=C16[:, t2:t2 + 1]), [bre, self.consts], [a])
            self.dve(lambda en, a=a, bim=bim, t2=t2, ob3=ob3, cs_=cs_: en.scalar_tensor_tensor(ob3[:, 0, cs_], bim.ap, S16[:, t2:t2 + 1], a.ap, ALU.mult, ALU.add), [bim, a, self.consts], [ob])
            self.act(lambda en, b=b, bre=bre, t2=t2: en.activation(b.ap, bre.ap, AF.Identity, bias=self.zc.ap[:, 0:1], scale=NS16[:, t2:t2 + 1]), [bre, self.consts], [b])
            self.dve(lambda en, b=b, bim=bim, t2=t2, ob3=ob3, cs_=cs_: en.scalar_tensor_tensor(ob3[:, 1, cs_], bim.ap, C16[:, t2:t2 + 1], b.ap, ALU.mult, ALU.add), [bim, b, self.consts], [ob])
        self.dma(S1[:, t2, :, :], ob3, r=[ob])


def _stage2_mm(self, b, b3, cs_):
    cb = self.cb.ap
    cosb, sinb, nsinb = cb[:, 512:640], cb[:, 640:768], cb[:, 768:896]
    xre, xim = self.nbank(2)
    self.pe(lambda en: en.matmul(xre.ap, cosb, b3[:, 0, cs_], start=True, stop=False), [b, self.cb], [xre])
    self.pe(lambda en: en.matmul(xre.ap, sinb, b3[:, 1, cs_], start=False, stop=True), [b, self.cb], [xre])
    self.pe(lambda en: en.matmul(xim.ap, cosb, b3[:, 1, cs_], start=True, stop=False), [b, self.cb], [xim])
    self.pe(lambda en: en.matmul(xim.ap, nsinb, b3[:, 0, cs_], start=False, stop=True), [b, self.cb], [xim])
    return xre, xim


def _mixer_odd(self, o, i):
    self.phase()
    nm = lambda n, s, dt: self.dram.get(n) or self.scr(n, s, dt)
    ZT, X0T, Z = nm("ZT", (D, L), BF16), nm("X0T", (D, L), BF16), nm("Z", (L, D), BF16)
    KERN = nm("KERN", (NFFT, D), BF16)
    S1, S2, KF = nm("S1", (128, 128, 2, D), BF16), nm("S2", (128, 128, 2, D), BF16), nm("KF", (128, 128, 2, D), BF16)
    Y, GT = nm("Y", (L, D), BF16), nm("GT", (D, L), BF16)
    W = self.Wb["hy_w_in"][o]
    cb = self.cb.ap
    identb, onesb, cosb, sinb, nsinb = cb[:, 0:128], cb[:, 128:256], cb[:, 512:640], cb[:, 640:768], cb[:, 768:896]
    C16, S16, NS16 = self.cst("C16"), self.cst("S16"), self.cst("NS16")
    tf, tf2, ob = self.ring(2, 512), self.ring(2, 512), self.ring(3, 512, BF16)
    groups = []
    for g in range(4):
        units = []
        for c in range(4):
            j = g * 4 + c

            def epi0(t, n, bks, j=j):
                a = tf.next()
                self.conv_epi(bks[0], n, 3, f"hyc{o}", j, 0, a)
                ot = ob.next()
                self.act(lambda en: en.activation(ot.ap[:, 0:n], a.ap[:, 0:n], AF.Copy), [a], [ot])
                self.dma(X0T[j * 128:(j + 1) * 128, t:t + n], ot.ap[:, 0:n], r=[ot])
            units.append(dict(kind="fm", halo=1, chunks=[(0, c * 128)], epi=epi0))
        groups.append(dict(w=[(W, g * 512, 512)], units=units))
    for g in range(4):
        units = []
        for c in range(4):
            j = g * 4 + c

            def epi1(t, n, bks, j=j):
                a, b = tf.next(), tf2.next()
                self.conv_epi(bks[0], n, 3, f"hyc{o}", 16 + j, 0, a)
                self.conv_epi(bks[1], n, 3, f"hyc{o}", 32 + j, 0, b)
                ot = ob.next()
                self.dve(lambda en: en.tensor_tensor(ot.ap[:, 0:n], a.ap[:, 0:n], b.ap[:, 0:n], ALU.mult), [a, b], [ot])
                self.dma(ZT[j * 128:(j + 1) * 128, t:t + n], ot.ap[:, 0:n], r=[ot])
            units.append(dict(kind="fm", halo=1, chunks=[(0, c * 128), (1, c * 128)], epi=epi1))
        groups.append(dict(w=[(W, 2048 + g * 512, 512), (W, 4096 + g * 512, 512)], units=units))
    self.gemm_in(f"g_mix{i}", 1, groups)
    self.phase()
    zin, zo = self.ring(2, 16 * 512, BF16), self.ring(2, D, BF16)
    ZT3 = ZT.rearrange("(k p) t -> p k t", p=128)
    for t4 in range(16):
        zi = zin.next()
        zi3 = zi.ap.rearrange("p (k t) -> p k t", t=512)
        self.dma(zi3, ZT3[:, :, t4 * 512:(t4 + 1) * 512], w=[zi])
        for tt in range(4):
            zt = zo.next()
            for hb in range(2):
                bk = self.nbank(1)[0]
                bb = bk.ap.bitcast(BF16)
                for j in range(8):
                    self.pe(lambda en, bb=bb, j=j, hb=hb, tt=tt, zi3=zi3: en.transpose(bb[:, j * 128:(j + 1) * 128], zi3[:, hb * 8 + j, tt * 128:(tt + 1) * 128], identb), [zi, self.cb], [bk])
                if hb == 0:
                    self.act(lambda en, bb=bb, zt=zt: en.activation(zt.ap[:, 0:1024], bb, AF.Copy), [bk], [zt])
                else:
                    self.dve(lambda en, bb=bb, zt=zt: en.tensor_copy(zt.ap[:, 1024:2048], bb), [bk], [zt])
            t0 = t4 * 512 + tt * 128
            self.dma(Z[t0:t0 + 128, :], zt.ap, r=[zt])
    self.phase()
    bn = self.banks
    w1, w2, w3, wo = self.tile(64, BF16), self.tile(64, BF16), self.tile(64, BF16), self.tile(2 * D, BF16)
    self.dma(w1.ap[0:33, :], self.Wb["hy_f_w1"][o], w=[w1])
    self.dma(w2.ap[0:64, :], self.Wb["hy_f_w2"][o], w=[w2])
    self.dma(w3.ap[0:64, :], self.Wb["hy_f_w3"][o], w=[w3])
    self.dma(wo.ap[0:64, :], self.Wb["hy_f_w_out"][o], w=[wo])
    delta = self.tile(D)
    self.dma(delta.ap, self.rows_d[:, RW["delta"]:RW["delta"] + D], w=[delta])
    fb = self.tile(4)
    for k in range(3):
        self.dve(lambda en, k=k: en.tensor_tensor(fb.ap[0:64, k:k + 1], self.sm(f"hyf{o}", k)[0:64, :], self.sm(f"hyf{o}", 3)[0:64, :], ALU.mult), [self.small], [fb])
    ntc = self.tile(128)
    self.dve(lambda en: en.tensor_scalar(ntc.ap, self.consts.ap[:, CS["tcol"]:CS["tcol"] + 128], -1.0, None, ALU.mult), [self.consts], [ntc])
    zfr, zbr = self.ring(2, 512), self.ring(2, 512, BF16)
    ur, rir, rfr, gr = self.ring(2, 512), self.ring(2, 512, mybir.dt.int32), self.ring(2, 512), self.ring(2, 512)
    hr = self.ring(4, 512, BF16)
    wnr, kr, abr = self.ring(2, 512), self.ring(2, D, BF16), self.ring(3, 512, BF16)
    I2P = 1.0 / (2.0 * math.pi)

    def sinlayer(wt, K, src, k):
        self.pe(lambda en: en.matmul(bn[0].ap[0:64, :], wt.ap[0:K, 0:64], src.ap[0:K, :], start=True, stop=True), [wt, src], [bn[0]])
        u, ri, rf, g_ = ur.next(), rir.next(), rfr.next(), gr.next()
        U, RI, RF, G = u.ap[0:64, :], ri.ap[0:64, :], rf.ap[0:64, :], g_.ap[0:64, :]
        self.act(lambda en: en.activation(U, bn[0].ap[0:64, :], AF.Identity, bias=fb.ap[0:64, k:k + 1], scale=self.sm(f"hyf{o}", 3)[0:64, :]), [bn[0], fb, self.small], [u])
        self.dve(lambda en: en.tensor_scalar(U, U, I2P, None, ALU.mult), [u], [u])
        self.dve(lambda en: en.tensor_copy(RI, U), [u], [ri])
        self.dve(lambda en: en.tensor_copy(RF, RI), [ri], [rf])
        self.dve(lambda en: en.tensor_tensor(U, U, RF, ALU.subtract), [u, rf], [u])
        self.dve(lambda en: en.tensor_scalar(G, U, 0.5, None, ALU.is_gt), [u], [g_])
        self.dve(lambda en: en.tensor_tensor(U, U, G, ALU.subtract), [u, g_], [u])
        self.dve(lambda en: en.tensor_scalar(G, U, -0.5, None, ALU.is_lt), [u], [g_])
        self.dve(lambda en: en.tensor_tensor(U, U, G, ALU.add), [u, g_], [u])
        h = hr.next()
        self.act(lambda en: en.activation(h.ap[0:64, :], U, AF.Sin, scale=6.28318), [u], [h])
        return h
    nacc = 0
    for half in range(2):
        zsrc = self.zfT if half == 0 else self.zfTr
        for tq in range(16):
            zf, zb = zfr.next(), zbr.next()
            self.dma(zf.ap[0:33, :], zsrc[:, tq * 512:(tq + 1) * 512], w=[zf])
            self.dve(lambda en, zf=zf, zb=zb: en.tensor_copy(zb.ap[0:33, :], zf.ap[0:33, :]), [zf], [zb])
            h = sinlayer(w1, 33, zb, 0)
            h = sinlayer(w2, 64, h, 1)
            h = sinlayer(w3, 64, h, 2)
            for cc in range(4):
                lc = tq * 4 + cc
                krow = kr.next()
                for ct in range(4):
                    bk = bn[1 + (nacc % 3)]
                    self.pe(lambda en, bk=bk, h=h, cc=cc, ct=ct: en.matmul(bk.ap, h.ap[0:64, cc * 128:(cc + 1) * 128], wo.ap[0:64, half * D + ct * 512:half * D + (ct + 1) * 512], start=True, stop=True), [h, wo], [bk])
                    wn = wnr.next()
                    self.act(lambda en, wn=wn, ct=ct, lc=lc: en.activation(wn.ap, delta.ap[:, ct * 512:(ct + 1) * 512], AF.Exp, scale=ntc.ap[:, half * 64 + lc:half * 64 + lc + 1]), [delta, ntc], [wn])
                    self.dve(lambda en, wn=wn, bk=bk, krow=krow, ct=ct: en.tensor_tensor(krow.ap[:, ct * 512:(ct + 1) * 512], bk.ap, wn.ap, ALU.mult), [bk, wn], [krow])
                    ab = abr.next()
                    self.act(lambda en, ab=ab, krow=krow, ct=ct: en.activation(ab.ap, krow.ap[:, ct * 512:(ct + 1) * 512], AF.Abs), [krow], [ab])
                    first, last = (half == 0 and lc == 0), (half == 1 and lc == 63)
                    self.pe(lambda en, ab=ab, ct=ct, first=first, last=last: en.matmul(bn[4 + ct].ap, onesb, ab.ap, start=first, stop=last), [ab, self.cb], [bn[4 + ct]])
                    nacc += 1
                if half == 0:
                    self.dma(KERN[lc * 128:(lc + 1) * 128, :], krow.ap, r=[krow])
                else:
                    nr = 128 if lc < 63 else 127
                    r0 = L + 1 + lc * 128
                    self.dma(KERN[r0:r0 + nr, :], krow.ap[0:nr, :], r=[krow])
    zr = self.tile(D, BF16)
    self.dve(lambda en: en.memset(zr.ap[0:1, :], 0.0), [], [zr])
    self.dma(KERN[L:L + 1, :], zr.ap[0:1, :], r=[zr])
    for ct in range(4):
        self.dve(lambda en, ct=ct: en.reciprocal(self.rinv.ap[:, ct * 512:(ct + 1) * 512], bn[4 + ct].ap), [bn[4 + ct]], [self.rinv])
    self.fft_stage1(KERN, 128, S1)
    self.phase()
    binr, kout = self.ring(2, 2 * D, BF16), self.ring(2, 2 * D, BF16)
    for k1 in range(128):
        b = binr.next()
        b3 = b.ap.rearrange("p (r c) -> p r c", c=D)
        self.dma(b3, S1[k1], w=[b])
        ko = kout.next()
        ko3 = ko.ap.rearrange("p (r c) -> p r c", c=D)
        for ct in range(4):
            cs_ = slice(ct * 512, (ct + 1) * 512)
            xre, xim = self.stage2_mm(b, b3, cs_)
            self.dve(lambda en, xre=xre, ko3=ko3, cs_=cs_: en.tensor_tensor(ko3[:, 0, cs_], xre.ap, self.rinv.ap[:, cs_], ALU.mult), [xre, self.rinv], [ko])
            self.dve(lambda en, xim=xim, ko3=ko3, cs_=cs_: en.tensor_tensor(ko3[:, 1, cs_], xim.ap, self.rinv.ap[:, cs_], ALU.mult), [xim, self.rinv], [ko])
        self.dma(KF[k1], ko3, r=[ko])
    self.fft_stage1(Z, 64, S1)
    self.phase()
    binr, kin, gout = self.ring(2, 2 * D, BF16), self.ring(2, 2 * D, BF16), self.ring(2, 2 * D, BF16)
    tmp = self.ring(6, 512)
    ybr = self.ring(4, 512, BF16)
    for k1 in range(128):
        b, kf, go = binr.next(), kin.next(), gout.next()
        b3 = b.ap.rearrange("p (r c) -> p r c", c=D)
        kf3 = kf.ap.rearrange("p (r c) -> p r c", c=D)
        go3 = go.ap.rearrange("p (r c) -> p r c", c=D)
        self.dma(b3, S1[k1], w=[b])
        self.dma(kf3, KF[k1], w=[kf])
        for ct in range(4):
            cs_ = slice(ct * 512, (ct + 1) * 512)
            xre, xim = self.stage2_mm(b, b3, cs_)
            t1, t2, t3, t4 = tmp.next(), tmp.next(), tmp.next(), tmp.next()
            yre, yim = ybr.next(), ybr.next()
            self.dve(lambda en, t1=t1, xre=xre, kf3=kf3, cs_=cs_: en.tensor_tensor(t1.ap, xre.ap, kf3[:, 0, cs_], ALU.mult), [xre, kf], [t1])
            self.dve(lambda en, t2=t2, xim=xim, kf3=kf3, cs_=cs_: en.tensor_tensor(t2.ap, xim.ap, kf3[:, 1, cs_], ALU.mult), [xim, kf], [t2])
            self.pool(lambda en, t1=t1, t2=t2, yre=yre: en.tensor_tensor(yre.ap, t1.ap, t2.ap, ALU.subtract), [t1, t2], [yre])
            self.dve(lambda en, t3=t3, xre=xre, kf3=kf3, cs_=cs_: en.tensor_tensor(t3.ap, xre.ap, kf3[:, 1, cs_], ALU.mult), [xre, kf], [t3])
            self.dve(lambda en, t4=t4, xim=xim, kf3=kf3, cs_=cs_: en.tensor_tensor(t4.ap, xim.ap, kf3[:, 0, cs_], ALU.mult), [xim, kf], [t4])
            self.pool(lambda en, t3=t3, t4=t4, yim=yim: en.tensor_tensor(yim.ap, t3.ap, t4.ap, ALU.add), [t3, t4], [yim])
            gre, gim = self.nbank(2)
            self.pe(lambda en, gre=gre, yre=yre: en.matmul(gre.ap, cosb, yre.ap, start=True, stop=False), [yre, self.cb], [gre])
            self.pe(lambda en, gre=gre, yim=yim: en.matmul(gre.ap, nsinb, yim.ap, start=False, stop=True), [yim, self.cb], [gre])
            self.pe(lambda en, gim=gim, yre=yre: en.matmul(gim.ap, sinb, yre.ap, start=True, stop=False), [yre, self.cb], [gim])
            self.pe(lambda en, gim=gim, yim=yim: en.matmul(gim.ap, cosb, yim.ap, start=False, stop=True), [yim, self.cb], [gim])
            t5, t6 = tmp.next(), tmp.next()
            self.act(lambda en, t5=t5, gre=gre, k1=k1: en.activation(t5.ap, gre.ap, AF.Identity, bias=self.zc.ap[:, 0:1], scale=C16[:, k1:k1 + 1]), [gre, self.consts], [t5])
            self.dve(lambda en, t5=t5, gim=gim, k1=k1, go3=go3, cs_=cs_: en.scalar_tensor_tensor(go3[:, 0, cs_], gim.ap, NS16[:, k1:k1 + 1], t5.ap, ALU.mult, ALU.add), [gim, t5, self.consts], [go])
            self.act(lambda en, t6=t6, gre=gre, k1=k1: en.activation(t6.ap, gre.ap, AF.Identity, bias=self.zc.ap[:, 0:1], scale=S16[:, k1:k1 + 1]), [gre, self.consts], [t6])
            self.dve(lambda en, t6=t6, gim=gim, k1=k1, go3=go3, cs_=cs_: en.scalar_tensor_tensor(go3[:, 1, cs_], gim.ap, C16[:, k1:k1 + 1], t6.ap, ALU.mult, ALU.add), [gim, t6, self.consts], [go])
        self.dma(S2[:, k1, :, :], go3, r=[go])
    self.phase()
    gin, yo = self.ring(2, 2 * D, BF16), self.ring(2, D, BF16)
    Y3 = Y.rearrange("(a b) c -> b a c", b=128)
    for t2 in range(128):
        g = gin.next()
        g3 = g.ap.rearrange("p (r c) -> p r c", c=D)
        self.dma(g3, S2[t2], w=[g])
        y = yo.next()
        for ct in range(4):
            cs_ = slice(ct * 512, (ct + 1) * 512)
            bk = self.nbank(1)[0]
            self.pe(lambda en, bk=bk, g3=g3, cs_=cs_: en.matmul(bk.ap[0:64, :], cosb[:, 0:64], g3[:, 0, cs_], start=True, stop=False), [g, self.cb], [bk])
            self.pe(lambda en, bk=bk, g3=g3, cs_=cs_: en.matmul(bk.ap[0:64, :], nsinb[:, 0:64], g3[:, 1, cs_], start=False, stop=True), [g, self.cb], [bk])
            self.act(lambda en, bk=bk, y=y, cs_=cs_: en.activation(y.ap[0:64, cs_], bk.ap[0:64, :], AF.Identity, bias=self.zc.ap[0:64, 0:1], scale=1.0 / NFFT), [bk], [y])
        self.dma(Y3[t2, 0:64, :], y.ap[0:64, :], r=[y])
    self.phase()
    yin = self.ring(2, D, BF16)
    ztr, x0r, gr2 = self.ring(1, 16 * 512, BF16), self.ring(1, 16 * 512, BF16), self.ring(2, 16 * 512, BF16)
    tm2 = self.ring(4, 128)
    X03 = X0T.rearrange("(k p) t -> p k t", p=128)
    GT3 = GT.rearrange("(k p) t -> p k t", p=128)
    for t4 in range(16):
        s0 = t4 * 512
        zt, x0, gg = ztr.next(), x0r.next(), gr2.next()
        zt3, x03, gg3 = [a.ap.rearrange("p (k t) -> p k t", t=512) for a in (zt, x0, gg)]
        self.dma(zt3, ZT3[:, :, s0:s0 + 512], w=[zt])
        self.dma(x03, X03[:, :, s0:s0 + 512], w=[x0])
        for tt in range(4):
            yi = yin.next()
            self.dma(yi.ap, Y[s0 + tt * 128:s0 + (tt + 1) * 128, :], w=[yi])
            for hb in range(2):
                bk = self.nbank(1)[0]
                bb = bk.ap.bitcast(BF16)
                for j in range(8):
                    self.pe(lambda en, bb=bb, j=j, hb=hb, yi=yi: en.transpose(bb[:, j * 128:(j + 1) * 128], yi.ap[:, (hb * 8 + j) * 128:(hb * 8 + j + 1) * 128], identb), [yi, self.cb], [bk])
                for j in range(8):
                    k = hb * 8 + j
                    tmq = tm2.next()
                    ts = slice(tt * 128, (tt + 1) * 128)
                    self.dve(lambda en, tmq=tmq, bb=bb, j=j, k=k, ts=ts, zt3=zt3: en.scalar_tensor_tensor(tmq.ap, zt3[:, k, ts], self.sm(f"hysk{o}", k), bb[:, j * 128:(j + 1) * 128], ALU.mult, ALU.add), [zt, bk, self.small], [tmq])
                    self.dve(lambda en, tmq=tmq, k=k, ts=ts, gg3=gg3, x03=x03: en.tensor_tensor(gg3[:, k, ts], tmq.ap, x03[:, k, ts], ALU.mult), [tmq, x0], [gg])
        self.dma(GT3[:, :, s0:s0 + 512], gg3, r=[gg])
    self.phase()
    self.gemm_out(GT, 16, self.Wb["hy_w_out"][o], 2048)


Builder.fft_stage1 = _fft_stage1
Builder.stage2_mm = _stage2_mm
Builder.mixer_odd = _mixer_odd
```
